# Optimizing a Trainium2 kernel written in Bass

```python
import jax, jax.numpy as jnp
from jax import lax
import numpy as np

D_MODEL = 2048
BATCH = 4
SEQ = 8192
DEPTH = 2

HEAD_DIM = 64
Q_BLOCK = 128
A_GROUPS = ((128, 1), (512, 4), (2048, 16))
A_HEADS_PER_GROUP = 4
A_HEADS = A_HEADS_PER_GROUP * len(A_GROUPS)
B_HEADS = 8
B_KV_HEADS = 2
B_WINDOW = 128
C_HEADS = 12
C_KV_HEADS = 2
CMP_BLOCK = 32
CMP_STRIDE = 16
CMP_HIDDEN = 256
SLC_BLOCK = 64
SLC_TOPK = 16
C_WINDOW = 512
N_BRANCHES = 3
D_FF = 5632
RMS_EPS = 1e-6
NEG_INF = -1e30

A_OUT = A_HEADS_PER_GROUP * HEAD_DIM
B_OUT = B_HEADS * HEAD_DIM
C_OUT = C_HEADS * HEAD_DIM
A_QKV_COLS = 3 * A_HEADS * HEAD_DIM
B_Q_COLS = B_HEADS * HEAD_DIM
B_KV_COLS = 2 * B_KV_HEADS * HEAD_DIM
C_Q_COLS = C_HEADS * HEAD_DIM
C_KV_COLS = 6 * C_KV_HEADS * HEAD_DIM
C_GATE_COLS = 3 * C_HEADS
GATE_COLS = N_BRANCHES * D_MODEL
IN_COLS = A_QKV_COLS + B_Q_COLS + B_KV_COLS + C_Q_COLS + C_KV_COLS + C_GATE_COLS + GATE_COLS

kernel_name = 'hybrid_dilated_sink_nsa_macaron'


def _rmsnorm(x, g):
    xf = x.astype(jnp.float32)
    y = xf * lax.rsqrt(jnp.mean(xf * xf, axis=-1, keepdims=True) + RMS_EPS)
    return (y * g.astype(jnp.float32)).astype(x.dtype)


def _alibi_slopes(n_heads):
    return jnp.asarray(2.0 ** (-8.0 * np.arange(1, n_heads + 1, dtype=np.float32) / n_heads), dtype=jnp.float32)


def _swiglu(h, w_gu, w_down):
    gate, up = jnp.split(h @ w_gu, 2, axis=-1)
    return (jax.nn.silu(gate) * up) @ w_down


def _masked_softmax(s, mask, sinks=None, with_lse=False):
    s = jnp.where(mask, s, NEG_INF)
    m = jnp.max(s, axis=-1, keepdims=True)
    if sinks is not None:
        m = jnp.maximum(m, sinks)
    e = jnp.where(mask, jnp.exp(s - m), 0.0)
    den = jnp.sum(e, axis=-1, keepdims=True)
    if sinks is not None:
        den = den + jnp.exp(sinks - m)
    p = e / jnp.where(den > 0, den, 1.0)
    if with_lse:
        return p, (m + jnp.log(den))[..., 0]
    return p


def _banded_attention(q, k, v, max_dist, slopes, dist_scale, sinks=None, with_lse=False):
    n, length, g, r, dh = q.shape
    nb = -(-max_dist // Q_BLOCK)
    nq = -(-length // Q_BLOCK)
    pad = nq * Q_BLOCK - length
    nk = (nb + 1) * Q_BLOCK
    q = jnp.pad(q, ((0, 0), (0, pad), (0, 0), (0, 0), (0, 0)))
    kv_pad = ((0, 0), (nb * Q_BLOCK, pad), (0, 0), (0, 0))
    k = jnp.pad(k, kv_pad)
    v = jnp.pad(v, kv_pad)
    rel = nb * Q_BLOCK + jnp.arange(Q_BLOCK)[:, None] - jnp.arange(nk)[None, :]
    band = (rel >= 0) & (rel <= max_dist)
    bias = -(slopes * dist_scale)[:, :, None, None] * rel.astype(jnp.float32)
    sink_logits = None if sinks is None else sinks[:, :, None, None].astype(jnp.float32)
    scale = dh ** -0.5

    def one_block(i):
        qb = lax.dynamic_slice_in_dim(q, i * Q_BLOCK, Q_BLOCK, axis=1)
        kb = lax.dynamic_slice_in_dim(k, i * Q_BLOCK, nk, axis=1)
        vb = lax.dynamic_slice_in_dim(v, i * Q_BLOCK, nk, axis=1)
        s = jnp.einsum('nqgrd,nkgd->ngrqk', qb, kb).astype(jnp.float32) * scale + bias
        key_pos = (i - nb) * Q_BLOCK + jnp.arange(nk)
        mask = band & (key_pos >= 0)[None, :]
        if with_lse:
            p, lse = _masked_softmax(s, mask, sink_logits, True)
            return jnp.einsum('ngrqk,nkgd->nqgrd', p.astype(vb.dtype), vb), lse
        p = _masked_softmax(s, mask, sink_logits)
        return jnp.einsum('ngrqk,nkgd->nqgrd', p.astype(vb.dtype), vb)

    res = lax.map(one_block, jnp.arange(nq))
    o = res[0] if with_lse else res
    o = jnp.moveaxis(o, 0, 1).reshape(n, nq * Q_BLOCK, g, r, dh)[:, :length]
    if with_lse:
        lse = res[1].transpose(1, 0, 4, 2, 3).reshape(n, nq * Q_BLOCK, g, r)[:, :length]
        return o, lse
    return o


def _dilated_mixer(q, k, v, slopes):
    b, s, _, dh = q.shape
    hpg = A_HEADS_PER_GROUP
    outs, lses = [], []
    for gi, (window, dil) in enumerate(A_GROUPS):
        hs = slice(gi * hpg, (gi + 1) * hpg)

        def to_strided(t):
            return t.reshape(b, s // dil, dil, hpg, dh).transpose(0, 2, 1, 3, 4).reshape(b * dil, s // dil, hpg, dh)

        o, lse = _banded_attention(to_strided(q[:, :, hs])[:, :, :, None], to_strided(k[:, :, hs]),
                                   to_strided(v[:, :, hs]), window // dil, slopes[hs][:, None], dil,
                                   with_lse=True)
        outs.append(o.reshape(b, dil, s // dil, hpg, dh).transpose(0, 2, 1, 3, 4).reshape(b, s, hpg, dh))
        lses.append(lse.reshape(b, dil, s // dil, hpg).transpose(0, 2, 1, 3).reshape(b, s, hpg))
    w = jax.nn.softmax(jnp.stack(lses), axis=0)
    o = jnp.einsum('gbsh,gbshd->bshd', w.astype(q.dtype), jnp.stack(outs))
    return o.reshape(b, s, hpg * dh)


def _sink_swa_mixer(q, k, v, slopes, sinks):
    b, s, _, dh = q.shape
    r = B_HEADS // B_KV_HEADS
    o = _banded_attention(q.reshape(b, s, B_KV_HEADS, r, dh), k, v, B_WINDOW - 1,
                          slopes.reshape(B_KV_HEADS, r), 1, sinks=sinks.reshape(B_KV_HEADS, r))
    return o.reshape(b, s, B_OUT)


def _compress(t, pos, w1, w2):
    b, s, g, dh = t.shape
    n_cmp = (s - CMP_BLOCK) // CMP_STRIDE + 1
    idx = CMP_STRIDE * jnp.arange(n_cmp)[:, None] + jnp.arange(CMP_BLOCK)[None, :]
    blocks = t[:, idx] + pos[None, None, :, None, :]
    blocks = blocks.transpose(0, 1, 3, 2, 4).reshape(b, n_cmp, g, CMP_BLOCK * dh)
    return jax.nn.silu(blocks @ w1) @ w2


def _nsa_mixer(q, k_cmp, v_cmp, k_slc, v_slc, k_win, v_win, gate_logits, slopes, q_gain, k_gain,
               cmp_pos, cmp_w1, cmp_w2):
    b, s, _, dh = q.shape
    g = C_KV_HEADS
    r = C_HEADS // g
    scale = dh ** -0.5
    q = _rmsnorm(q, q_gain).reshape(b, s, g, r, dh)
    slopes = slopes.reshape(g, r)
    sl = slopes[None, :, :, None, None]
    kc = _rmsnorm(_compress(k_cmp, cmp_pos[0], cmp_w1[0], cmp_w2[0]), k_gain)
    vc = _compress(v_cmp, cmp_pos[1], cmp_w1[1], cmp_w2[1])
    n_cmp = kc.shape[1]
    cmp_start = CMP_STRIDE * jnp.arange(n_cmp)
    cmp_end = cmp_start + CMP_BLOCK - 1
    n_slc = s // SLC_BLOCK
    slc_start = SLC_BLOCK * jnp.arange(n_slc)
    overlap = ((cmp_start[:, None] <= slc_start[None, :] + SLC_BLOCK - 1)
               & (cmp_end[:, None] >= slc_start[None, :])).astype(jnp.float32)
    n_top = min(SLC_TOPK, n_slc)
    k_slc = _rmsnorm(k_slc, k_gain)
    ks_blk = k_slc.reshape(b, n_slc, SLC_BLOCK, g, dh).transpose(0, 3, 1, 2, 4)
    vs_blk = v_slc.reshape(b, n_slc, SLC_BLOCK, g, dh).transpose(0, 3, 1, 2, 4)
    bi = jnp.arange(b)[:, None, None, None]
    gi = jnp.arange(g)[None, :, None, None]
    n_keys = n_top * SLC_BLOCK

    def one_block(i):
        qb = lax.dynamic_slice_in_dim(q, i * Q_BLOCK, Q_BLOCK, axis=1)
        t = i * Q_BLOCK + jnp.arange(Q_BLOCK)
        d_cmp = (t[:, None] - cmp_end[None, :]).astype(jnp.float32)
        s_cmp = jnp.einsum('bqgrd,bngd->bgrqn', qb, kc).astype(jnp.float32) * scale - sl * d_cmp
        p_cmp = _masked_softmax(s_cmp, d_cmp >= 0)
        o_cmp = jnp.einsum('bgrqn,bngd->bqgrd', p_cmp.astype(vc.dtype), vc)
        imp = jnp.einsum('bgrqn,nj->bgqj', p_cmp, overlap)
        cur = (t // SLC_BLOCK)[:, None]
        j = jnp.arange(n_slc)[None, :]
        forced = (j == 0) | (j == cur) | (j == cur - 1)
        imp = jnp.where(j <= cur, jnp.where(forced, jnp.inf, imp), -jnp.inf)
        _, sel = lax.top_k(imp, n_top)
        kg = ks_blk[bi, gi, sel].reshape(b, g, Q_BLOCK, n_keys, dh)
        vg = vs_blk[bi, gi, sel].reshape(b, g, Q_BLOCK, n_keys, dh)
        key_pos = sel[..., None] * SLC_BLOCK + jnp.arange(SLC_BLOCK)
        d_slc = (t[:, None, None] - key_pos).reshape(b, g, Q_BLOCK, n_keys).astype(jnp.float32)[:, :, None]
        s_slc = jnp.einsum('bqgrd,bgqkd->bgrqk', qb, kg).astype(jnp.float32) * scale - sl * d_slc
        p_slc = _masked_softmax(s_slc, d_slc >= 0)
        o_slc = jnp.einsum('bgrqk,bgqkd->bqgrd', p_slc.astype(vg.dtype), vg)
        return o_cmp, o_slc

    o_cmp, o_slc = lax.map(one_block, jnp.arange(s // Q_BLOCK))
    o_cmp = jnp.moveaxis(o_cmp, 0, 1).reshape(b, s, g, r, dh)
    o_slc = jnp.moveaxis(o_slc, 0, 1).reshape(b, s, g, r, dh)
    o_win = _banded_attention(q, _rmsnorm(k_win, k_gain), v_win, C_WINDOW - 1, slopes, 1)
    gate = jax.nn.sigmoid(gate_logits.astype(jnp.float32)).astype(q.dtype).reshape(b, s, g, r, 3)
    o = gate[..., 0:1] * o_cmp + gate[..., 1:2] * o_slc + gate[..., 2:3] * o_win
    return o.reshape(b, s, C_OUT)


def _token_mixing(h, w_in, qk_gain, sinks, cmp_pos, cmp_w1, cmp_w2, w_branch_a, w_branch_b, w_branch_c, w_out):
    b, s, _ = h.shape
    dh = HEAD_DIM
    proj = h @ w_in
    offs, acc = [], 0
    for width in (A_QKV_COLS, B_Q_COLS, B_KV_COLS, C_Q_COLS, C_KV_COLS, C_GATE_COLS):
        acc += width
        offs.append(acc)
    a_qkv, b_q, b_kv, c_q, c_kv, c_gate, br_gate = jnp.split(proj, offs, axis=-1)
    a_qkv = a_qkv.reshape(b, s, 3, A_HEADS, dh)
    o_a = _dilated_mixer(_rmsnorm(a_qkv[:, :, 0], qk_gain[0, 0]), _rmsnorm(a_qkv[:, :, 1], qk_gain[0, 1]),
                         a_qkv[:, :, 2], _alibi_slopes(A_HEADS))
    b_kv = b_kv.reshape(b, s, 2, B_KV_HEADS, dh)
    o_b = _sink_swa_mixer(_rmsnorm(b_q.reshape(b, s, B_HEADS, dh), qk_gain[1, 0]),
                          _rmsnorm(b_kv[:, :, 0], qk_gain[1, 1]), b_kv[:, :, 1], _alibi_slopes(B_HEADS), sinks)
    c_kv = c_kv.reshape(b, s, 6, C_KV_HEADS, dh)
    o_c = _nsa_mixer(c_q.reshape(b, s, C_HEADS, dh), c_kv[:, :, 0], c_kv[:, :, 1], c_kv[:, :, 2], c_kv[:, :, 3],
                     c_kv[:, :, 4], c_kv[:, :, 5], c_gate, _alibi_slopes(C_HEADS), qk_gain[2, 0], qk_gain[2, 1],
                     cmp_pos, cmp_w1, cmp_w2)
    gates = jax.nn.sigmoid(br_gate.astype(jnp.float32)).astype(h.dtype).reshape(b, s, N_BRANCHES, D_MODEL)
    merged = (gates[:, :, 0] * (o_a @ w_branch_a) + gates[:, :, 1] * (o_b @ w_branch_b)
              + gates[:, :, 2] * (o_c @ w_branch_c))
    return merged @ w_out


def setup_inputs(seed: int = 0) -> dict:
    key = jax.random.key(seed)
    ks = jax.random.split(key, 18)
    f32 = jnp.float32

    def dense(k, shape, fan_in):
        return jax.random.normal(k, shape, f32) * (fan_in ** -0.5)

    def gain(k, shape):
        return 1.0 + 0.05 * jax.random.normal(k, shape, f32)

    return {
        'x': jax.random.normal(ks[0], (BATCH, SEQ, D_MODEL), f32),
        'ffn1_norm': gain(ks[1], (DEPTH, D_MODEL)),
        'ffn1_w_gu': dense(ks[2], (DEPTH, D_MODEL, 2 * D_FF), D_MODEL),
        'ffn1_w_down': dense(ks[3], (DEPTH, D_FF, D_MODEL), D_FF),
        'mix_norm': gain(ks[4], (DEPTH, D_MODEL)),
        'w_in': dense(ks[5], (DEPTH, D_MODEL, IN_COLS), D_MODEL),
        'qk_gain': gain(ks[6], (DEPTH, 3, 2, HEAD_DIM)),
        'sinks': 0.5 * jax.random.normal(ks[7], (DEPTH, B_HEADS), f32),
        'cmp_pos': 0.02 * jax.random.normal(ks[8], (DEPTH, 2, CMP_BLOCK, HEAD_DIM), f32),
        'cmp_w1': dense(ks[9], (DEPTH, 2, CMP_BLOCK * HEAD_DIM, CMP_HIDDEN), CMP_BLOCK * HEAD_DIM),
        'cmp_w2': dense(ks[10], (DEPTH, 2, CMP_HIDDEN, HEAD_DIM), CMP_HIDDEN),
        'w_branch_a': dense(ks[11], (DEPTH, A_OUT, D_MODEL), A_OUT),
        'w_branch_b': dense(ks[12], (DEPTH, B_OUT, D_MODEL), B_OUT),
        'w_branch_c': dense(ks[13], (DEPTH, C_OUT, D_MODEL), C_OUT),
        'w_out': dense(ks[14], (DEPTH, D_MODEL, D_MODEL), D_MODEL),
        'ffn2_norm': gain(ks[15], (DEPTH, D_MODEL)),
        'ffn2_w_gu': dense(ks[16], (DEPTH, D_MODEL, 2 * D_FF), D_MODEL),
        'ffn2_w_down': dense(ks[17], (DEPTH, D_FF, D_MODEL), D_FF),
    }


def reference(x, ffn1_norm, ffn1_w_gu, ffn1_w_down, mix_norm, w_in, qk_gain, sinks, cmp_pos, cmp_w1, cmp_w2,
              w_branch_a, w_branch_b, w_branch_c, w_out, ffn2_norm, ffn2_w_gu, ffn2_w_down):
    for l in range(DEPTH):
        x = x + 0.5 * _swiglu(_rmsnorm(x, ffn1_norm[l]), ffn1_w_gu[l], ffn1_w_down[l])
        x = x + _token_mixing(_rmsnorm(x, mix_norm[l]), w_in[l], qk_gain[l], sinks[l], cmp_pos[l], cmp_w1[l],
                              cmp_w2[l], w_branch_a[l], w_branch_b[l], w_branch_c[l], w_out[l])
        x = x + 0.5 * _swiglu(_rmsnorm(x, ffn2_norm[l]), ffn2_w_gu[l], ffn2_w_down[l])
    return x
```

```python
import numpy as np
import ml_dtypes
import concourse.bass as bass
import concourse.mybir as mybir
from concourse.bass_utils import run_bass_kernel_spmd

F32 = mybir.dt.float32
BF16 = mybir.dt.bfloat16
AF = mybir.ActivationFunctionType
ALU = mybir.AluOpType
AX = mybir.AxisListType

D = 2048
DFF = 5632
KC = D // 128
FC = DFF // 128
RMS_EPS = 1e-6


class Prog:
    CE = ('pe', 'act', 'dve', 'pool')
    ENG = ('pe', 'act', 'dve', 'pool', 'sp')

    def __init__(self, nc, n_dma=16):
        self.nc = nc
        self.streams = {e: [] for e in self.ENG}
        self.cnt = {e: 0 for e in self.CE}
        self.esem = {e: nc.alloc_semaphore('sem_' + e) for e in self.CE}
        self.dsem = [nc.alloc_semaphore('sem_d%d' % i) for i in range(n_dma)]
        self.dcount = [0] * n_dma
        self.dnext = 0
        self.seen = {e: {} for e in self.ENG}
        self.lastw = {}
        self.readers = {}
        self.n_wait = 0

    def sem(self, key):
        return self.esem[key[1]] if key[0] == 'e' else self.dsem[key[1]]

    def _deps(self, reads, writes):
        deps = []
        for r in reads:
            t = self.lastw.get(r)
            if t:
                deps.append(t)
        for w in writes:
            t = self.lastw.get(w)
            if t:
                deps.append(t)
            deps.extend(self.readers.get(w, {}).items())
        return deps

    def _waits(self, eng, deps):
        need = {}
        for key, val in deps:
            if key == ('e', 'pe') and eng == 'pe':
                continue
            if self.seen[eng].get(key, 0) >= val:
                continue
            if need.get(key, 0) < val:
                need[key] = val
        for key, val in need.items():
            self.seen[eng][key] = val
        self.n_wait += len(need)
        return list(need.items())

    def _commit(self, tok, reads, writes):
        for w in writes:
            self.lastw[w] = tok
            self.readers[w] = {}
        for r in reads:
            d = self.readers.setdefault(r, {})
            if d.get(tok[0], 0) < tok[1]:
                d[tok[0]] = tok[1]

    def op(self, eng, fn, reads=(), writes=()):
        waits = self._waits(eng, self._deps(reads, writes))
        self.cnt[eng] += 1
        tok = (('e', eng), self.cnt[eng])
        self.streams[eng].append((waits, fn, ('e', eng)))
        self._commit(tok, reads, writes)

    def dma(self, out, in_, reads=(), writes=(), q='sp'):
        k = self.dnext
        self.dnext = (k + 1) % len(self.dsem)
        deps = self._deps(reads, writes)
        if self.dcount[k]:
            deps.append((('d', k), 16 * self.dcount[k]))
        waits = self._waits(q, deps)
        self.dcount[k] += 1
        tok = (('d', k), 16 * self.dcount[k])
        self.streams[q].append((waits, (lambda e, o=out, i=in_: e.dma_start(out=o, in_=i)), ('d', k)))
        self._commit(tok, reads, writes)

    def barrier(self):
        allt = [(('e', e), self.cnt[e]) for e in self.CE if self.cnt[e]]
        allt += [(('d', k), 16 * c) for k, c in enumerate(self.dcount) if c]
        for e in self.ENG:
            waits = self._waits(e, allt)
            if waits:
                self.streams[e].append((waits, None, None))
        self.lastw.clear()
        self.readers.clear()

    def emit(self):
        nc = self.nc

        def run(name, eng):
            for waits, fn, inc in self.streams[name]:
                for key, val in waits:
                    eng.wait_ge(self.sem(key), val)
                if fn is not None:
                    ins = fn(eng)
                    ins.then_inc(self.sem(inc), 16 if inc[0] == 'd' else 1)

        with nc.Block() as block:
            @block.tensor
            def _(e):
                run('pe', e)

            @block.scalar
            def _(e):
                run('act', e)

            @block.vector
            def _(e):
                run('dve', e)

            @block.gpsimd
            def _(e):
                run('pool', e)

            @block.sync
            def _(e):
                run('sp', e)


class Arena:
    def __init__(self, nc, kbytes=204):
        self.words = kbytes * 1024 // 4
        self.t = nc.alloc_sbuf_tensor('arena', [128, self.words], F32)
        self.off = 0

    def reset(self):
        self.off = 0

    def alloc(self, cols, dtype=F32):
        nw = cols if dtype == F32 else (cols + 1) // 2
        nw = (nw + 7) // 8 * 8
        assert self.off + nw <= self.words, 'SBUF arena overflow: %d + %d > %d' % (self.off, nw, self.words)
        ap = self.t[:, self.off:self.off + nw]
        self.off += nw
        if dtype != F32:
            ap = ap.bitcast(dtype)[:, 0:cols]
        else:
            ap = ap[:, 0:cols]
        return ap


class Ctx:
    def __init__(self, nc):
        self.nc = nc
        self.P = Prog(nc)
        self.A = Arena(nc)
        self.psum = nc.alloc_psum_tensor('psum', [128, 8 * 512], F32)
        self.uid = 0

    def bank(self, b, nb=1):
        return self.psum[:, b * 512:(b + nb) * 512]

    def bank_bf(self, b, nb=1):
        return self.psum[:, b * 512:(b + nb) * 512].bitcast(BF16)

    def key(self, name):
        self.uid += 1
        return '%s#%d' % (name, self.uid)


def phase_cast(cx, pairs, cb=4096):
    P, A = cx.P, cx.A
    A.reset()
    nslot = 3
    stg = [A.alloc(cb, F32) for _ in range(nslot)]
    obf = [A.alloc(cb, BF16) for _ in range(nslot)]
    ks = [cx.key('cs') for _ in range(nslot)]
    ko = [cx.key('co') for _ in range(nslot)]
    engs = ('act', 'dve', 'pool')
    i = 0
    for src, dst in pairs:
        R, C = src.shape
        assert R % 128 == 0
        for r0 in range(0, R, 128):
            for c0 in range(0, C, cb):
                c1 = min(C, c0 + cb)
                n = c1 - c0
                s = i % nslot
                P.dma(stg[s][:, 0:n], src[r0:r0 + 128, c0:c1], writes=[ks[s]])
                e = engs[i % 3]
                if e == 'act':
                    P.op('act', lambda g, o=obf[s][:, 0:n], a=stg[s][:, 0:n]: g.copy(o, a), reads=[ks[s]], writes=[ko[s]])
                else:
                    P.op(e, lambda g, o=obf[s][:, 0:n], a=stg[s][:, 0:n]: g.tensor_copy(o, a), reads=[ks[s]], writes=[ko[s]])
                P.dma(dst[r0:r0 + 128, c0:c1], obf[s][:, 0:n], reads=[ko[s]])
                i += 1
    P.barrier()


def emit_norm_transpose(cx, x_src, t0, nsub, gcol, kg, xnT, kxn, xs, kxs, xb, kxb, junk, kjunk, st, kst, ident, tbank0):
    P = cx.P
    TT = nsub * 128
    xnT3 = xnT.rearrange('p (k t) -> p k t', k=KC)
    for s in range(nsub):
        sl = s % len(xs)
        P.dma(xs[sl], x_src[t0 + s * 128:t0 + (s + 1) * 128, :], writes=[kxs[sl]])
        P.op('act', lambda g, o=junk, a=xs[sl], acc=st[sl][:, 0:1]: g.activation(o, a, AF.Square, accum_out=acc),
             reads=[kxs[sl]], writes=[kjunk, kst[sl] + 'a'])
        P.op('dve', lambda g, o=st[sl][:, 1:2], a=st[sl][:, 0:1]: g.tensor_scalar(o, a, 1.0 / D, RMS_EPS, ALU.mult, ALU.add),
             reads=[kst[sl] + 'a'], writes=[kst[sl] + 'b'])
        P.op('act', lambda g, o=st[sl][:, 3:4], a=st[sl][:, 1:2]: g.sqrt(o, a),
             reads=[kst[sl] + 'b'], writes=[kst[sl] + 'd'])
        P.op('dve', lambda g, o=st[sl][:, 2:3], a=st[sl][:, 3:4]: g.reciprocal(o, a),
             reads=[kst[sl] + 'd'], writes=[kst[sl] + 'c'])
        P.op('pool', lambda g, o=xb[sl], a=xs[sl], r=st[sl][:, 2:3]: g.tensor_scalar(o, a, r, None, ALU.mult),
             reads=[kxs[sl], kst[sl] + 'c'], writes=[kxb[sl]])
        tb = tbank0 + 2 * (s % 2)
        kb = ('bank', tb), ('bank', tb + 1)
        pbf = cx.bank_bf(tb, 2)
        for kc in range(KC):
            b = kb[0] if kc < 8 else kb[1]
            P.op('pe', lambda g, o=pbf[:, kc * 128:(kc + 1) * 128], a=xb[sl][:, kc * 128:(kc + 1) * 128]: g.transpose(o, a, ident),
                 reads=[kxb[sl]], writes=[b])
        P.op('dve', lambda g, o=xnT3[:, :, s * 128:(s + 1) * 128], a=pbf.rearrange('p (k t) -> p k t', k=KC),
             gc=gcol.unsqueeze(2).to_broadcast([128, KC, 128]): g.tensor_tensor(o, a, gc, ALU.mult),
             reads=[kb[0], kb[1], kg], writes=[kxn])


def load_ident(cx, ident_dram):
    pass


def phase_ffn(cx, x_in, x_out, gcol_d, wgu, wdn, NT, ident_d, TT=512):
    P, A = cx.P, cx.A
    A.reset()
    nsub = TT // 128
    ident = A.alloc(128, BF16)
    kid = cx.key('ident')
    gcol = A.alloc(KC, F32)
    kg = cx.key('gcol')
    P.dma(ident, ident_d[:, :], writes=[kid])
    P.dma(gcol, gcol_d[:, :], writes=[kg])
    xnT = A.alloc(KC * TT, BF16)
    kxn = cx.key('xnT')
    hT = A.alloc(FC * TT, BF16)
    khT = [cx.key('hT') for _ in range(FC)]
    xs = [A.alloc(D, F32) for _ in range(2)]
    kxs = [cx.key('xs') for _ in range(2)]
    xb = [A.alloc(D, BF16) for _ in range(2)]
    kxb = [cx.key('xb') for _ in range(2)]
    junk = A.alloc(D, BF16)
    kjunk = cx.key('junk')
    st = [A.alloc(8, F32) for _ in range(2)]
    kst = [cx.key('st') for _ in range(2)]
    NW = 3
    wg = [A.alloc(KC * 256, BF16) for _ in range(NW)]
    kwg = [cx.key('wg') for _ in range(NW)]
    wd = [A.alloc(D, BF16) for _ in range(NW)]
    kwd = [cx.key('wd') for _ in range(NW)]
    sg = [A.alloc(TT, F32) for _ in range(2)]
    ksg = [cx.key('sg') for _ in range(2)]
    xr = [A.alloc(D, F32) for _ in range(2)]
    kxr = [cx.key('xr') for _ in range(2)]
    xo = [A.alloc(D, F32) for _ in range(2)]
    kxo = [cx.key('xo') for _ in range(2)]

    iw = 0
    idn = 0
    for t0 in range(0, NT, TT):
        emit_norm_transpose(cx, x_in, t0, nsub, gcol, kg, xnT, kxn, xs, kxs, xb, kxb, junk, kjunk, st, kst, ident, 4)
        for fc in range(FC):
            s = iw % NW
            iw += 1
            P.dma(wg[s], wgu[fc * 128:(fc + 1) * 128, :], writes=[kwg[s]])
            bg = 2 * (fc % 2)
            bu = bg + 1
            pg, pu = cx.bank(bg)[:, 0:TT], cx.bank(bu)[:, 0:TT]
            for kc in range(KC):
                P.op('pe', lambda g, o=pg, w=wg[s][:, kc * 256:kc * 256 + 128], a=xnT[:, kc * TT:(kc + 1) * TT], k=kc:
                     g.matmul(o, w, a, start=(k == 0), stop=(k == KC - 1)),
                     reads=[kwg[s], kxn], writes=[('bank', bg)])
            for kc in range(KC):
                P.op('pe', lambda g, o=pu, w=wg[s][:, kc * 256 + 128:(kc + 1) * 256], a=xnT[:, kc * TT:(kc + 1) * TT], k=kc:
                     g.matmul(o, w, a, start=(k == 0), stop=(k == KC - 1)),
                     reads=[kwg[s], kxn], writes=[('bank', bu)])
            q = fc % 2
            P.op('act', lambda g, o=sg[q], a=pg: g.activation(o, a, AF.Silu), reads=[('bank', bg)], writes=[ksg[q]])
            P.op('dve', lambda g, o=hT[:, fc * TT:(fc + 1) * TT], a=pu, b=sg[q]: g.tensor_tensor(o, a, b, ALU.mult),
                 reads=[('bank', bu), ksg[q]], writes=[khT[fc]])
        for s0 in range(0, nsub, 2):
            for fc in range(FC):
                s = idn % NW
                idn += 1
                P.dma(wd[s], wdn[fc * 128:(fc + 1) * 128, :], writes=[kwd[s]])
                for ss in range(2):
                    for dc in range(4):
                        b = ss * 4 + dc
                        P.op('pe', lambda g, o=cx.bank(b), h=hT[:, fc * TT + (s0 + ss) * 128:fc * TT + (s0 + ss + 1) * 128],
                             w=wd[s][:, dc * 512:(dc + 1) * 512], f=fc: g.matmul(o, h, w, start=(f == 0), stop=(f == FC - 1)),
                             reads=[kwd[s], khT[fc]], writes=[('bank', b)])
            for ss in range(2):
                tok = t0 + (s0 + ss) * 128
                q = ss
                P.dma(xr[q], x_in[tok:tok + 128, :], writes=[kxr[q]])
                for dc in range(4):
                    b = ss * 4 + dc
                    P.op('dve', lambda g, o=xo[q][:, dc * 512:(dc + 1) * 512], a=cx.bank(b), r=xr[q][:, dc * 512:(dc + 1) * 512]:
                         g.scalar_tensor_tensor(o, a, 0.5, r, ALU.mult, ALU.add),
                         reads=[('bank', b), kxr[q]], writes=[kxo[q]])
                P.dma(x_out[tok:tok + 128, :], xo[q], reads=[kxo[q]])
    P.barrier()


HD = 64
P1_BLOCKS = [('n', 512, 0, 0), ('n', 512, 0, 512), ('n', 512, 0, 1024), ('n', 512, 0, 1536),
             ('n', 512, 1, 0), ('n', 512, 1, 512), ('n', 128, 1, 1024),
             ('c', 512, 1, 1152), ('c', 512, 1, 1664), ('c', 384, 1, 2176),
             ('g', 36, 2, 0)]
NB1 = len(P1_BLOCKS)


def phase_proj(cx, x_in, gcol_d, win_b, gain_d, qn_d, kv_d, cg_d, NT, ident_d, TT=512):
    P, A = cx.P, cx.A
    A.reset()
    nsub = TT // 128
    ident = A.alloc(128, BF16)
    kid = cx.key('ident')
    gcol = A.alloc(KC, F32)
    kg = cx.key('gcol')
    gain = A.alloc(3200, F32)
    kgain = cx.key('gain')
    P.dma(ident, ident_d[:, :], writes=[kid])
    P.dma(gcol, gcol_d[:, :], writes=[kg])
    P.dma(gain, gain_d[:, :], writes=[kgain])
    P.op('dve', lambda g, o=gain[:, 0:2048]: g.tensor_scalar(o, o, 0.125, None, ALU.mult), reads=[kgain], writes=[kgain])
    xnT = A.alloc(KC * TT, BF16)
    kxn = cx.key('xnT')
    xs = [A.alloc(D, F32) for _ in range(2)]
    kxs = [cx.key('xs') for _ in range(2)]
    xb = [A.alloc(D, BF16) for _ in range(2)]
    kxb = [cx.key('xb') for _ in range(2)]
    junk = A.alloc(D, BF16)
    kjunk = cx.key('junk')
    st = [A.alloc(8, F32) for _ in range(2)]
    kst = [cx.key('st') for _ in range(2)]
    NW = 2
    wt = [A.alloc(KC * 512, BF16) for _ in range(NW)]
    kwt = [cx.key('wt') for _ in range(NW)]
    NE = 2
    sq = [A.alloc(512, F32) for _ in range(NE)]
    ksq = [cx.key('sq') for _ in range(NE)]
    tmp = [A.alloc(512, F32) for _ in range(NE)]
    ktmp = [cx.key('tmp') for _ in range(NE)]
    ss = [A.alloc(32, F32) for _ in range(NE)]
    kss = [cx.key('ss') for _ in range(NE)]
    ob = [A.alloc(512, BF16) for _ in range(NE)]
    kob = [cx.key('ob') for _ in range(NE)]
    og = [A.alloc(64, F32) for _ in range(NE)]
    kog = [cx.key('og') for _ in range(NE)]
    dests = (qn_d, kv_d, cg_d)
    iw = 0
    ie = 0
    for t0 in range(0, NT, TT):
        emit_norm_transpose(cx, x_in, t0, nsub, gcol, kg, xnT, kxn, xs, kxs, xb, kxb, junk, kjunk, st, kst, ident, 4)
        for bi, (kind, w, di, dc) in enumerate(P1_BLOCKS):
            sl = iw % NW
            iw += 1
            P.dma(wt[sl], win_b[bi * 128:(bi + 1) * 128, :], writes=[kwt[sl]])
            for s in range(nsub):
                b = (bi * nsub + s) % 4
                ps = cx.bank(b)[:, 0:w]
                for kc in range(KC):
                    P.op('pe', lambda g, o=ps, a=xnT[:, kc * TT + s * 128:kc * TT + (s + 1) * 128],
                         wv=wt[sl][:, kc * 512:kc * 512 + w], k=kc: g.matmul(o, a, wv, start=(k == 0), stop=(k == KC - 1)),
                         reads=[kxn, kwt[sl]], writes=[('bank', b)])
                e = ie % NE
                ie += 1
                tok = t0 + s * 128
                dst = dests[di][tok:tok + 128, dc:dc + w]
                if kind == 'n':
                    nh = w // HD
                    gofs = dc if di == 0 else 2048 + dc
                    P.op('act', lambda g, o=sq[e][:, 0:w], a=ps: g.activation(o, a, AF.Square), reads=[('bank', b)], writes=[ksq[e]])
                    P.op('dve', lambda g, o=ss[e][:, 0:nh], a=sq[e][:, 0:w].rearrange('p (h d) -> p h d', d=HD):
                         g.tensor_reduce(o, a, AX.X, ALU.add), reads=[ksq[e]], writes=[kss[e] + 'a'])
                    P.op('dve', lambda g, o=ss[e][:, 8:8 + nh], a=ss[e][:, 0:nh]: g.tensor_scalar(o, a, 1.0 / HD, RMS_EPS, ALU.mult, ALU.add),
                         reads=[kss[e] + 'a'], writes=[kss[e] + 'b'])
                    P.op('act', lambda g, o=ss[e][:, 16:16 + nh], a=ss[e][:, 8:8 + nh]: g.sqrt(o, a), reads=[kss[e] + 'b'], writes=[kss[e] + 'c'])
                    P.op('dve', lambda g, o=ss[e][:, 24:24 + nh], a=ss[e][:, 16:16 + nh]: g.reciprocal(o, a), reads=[kss[e] + 'c'], writes=[kss[e] + 'd'])
                    P.op('dve', lambda g, o=tmp[e][:, 0:w].rearrange('p (h d) -> p h d', d=HD), a=ps.rearrange('p (h d) -> p h d', d=HD),
                         r=ss[e][:, 24:24 + nh].unsqueeze(2).to_broadcast([128, nh, HD]): g.tensor_tensor(o, a, r, ALU.mult),
                         reads=[('bank', b), kss[e] + 'd'], writes=[ktmp[e]])
                    P.op('pool', lambda g, o=ob[e][:, 0:w], a=tmp[e][:, 0:w], gn=gain[:, gofs:gofs + w]: g.tensor_tensor(o, a, gn, ALU.mult),
                         reads=[ktmp[e], kgain], writes=[kob[e]])
                    P.dma(dst, ob[e][:, 0:w], reads=[kob[e]])
                elif kind == 'c':
                    P.op('act', lambda g, o=ob[e][:, 0:w], a=ps: g.copy(o, a), reads=[('bank', b)], writes=[kob[e]])
                    P.dma(dst, ob[e][:, 0:w], reads=[kob[e]])
                else:
                    P.op('act', lambda g, o=og[e][:, 0:w], a=ps: g.activation(o, a, AF.Sigmoid), reads=[('bank', b)], writes=[kog[e]])
                    P.dma(dst, og[e][:, 0:w], reads=[kog[e]])
    P.barrier()


def _r(a, b):
    return np.arange(a, b)


_Aq, _Ak, _Av = _r(0, 768), _r(768, 1536), _r(1536, 2304)
_Bq, _Bk, _Bv = _r(2304, 2816), _r(2816, 2944), _r(2944, 3072)
_Cq = _r(3072, 3840)
_Ckc, _Cvc, _Cks, _Cvs, _Ckw, _Cvw = [_r(3840 + 128 * i, 3968 + 128 * i) for i in range(6)]
_Cg = _r(4608, 4644)
GATE0 = 4644
P1_ORDER = np.concatenate([_Aq, _Bq, _Cq, _Ak, _Bk, _Cks, _Ckw, _Av, _Bv, _Ckc, _Cvc, _Cvs, _Cvw, _Cg])
P1_OFFS = [0, 512, 1024, 1536, 2048, 2560, 3072, 3200, 3712, 4224, 4608]
KV_AK, KV_BK, KV_CKS, KV_CKW, KV_AV, KV_BV, KV_CKC, KV_CVC, KV_CVS, KV_CVW = 0, 768, 896, 1024, 1152, 1920, 2048, 2176, 2304, 2432
QN_A, QN_B, QN_C = 0, 768, 1280


def lay_col(v):
    return np.ascontiguousarray(v.reshape(-1, 128).T)


def lay_wgu(w):
    return np.ascontiguousarray(w.reshape(KC, 128, 2, FC, 128).transpose(3, 1, 0, 2, 4)).reshape(FC * 128, KC * 256)


def lay_win_p1(w_in):
    out = np.zeros((NB1, 128, KC, 512), np.float32)
    for b, (kind, w, di, dc) in enumerate(P1_BLOCKS):
        cols = P1_ORDER[P1_OFFS[b]:P1_OFFS[b] + w]
        out[b, :, :, :w] = w_in[:, cols].reshape(KC, 128, w).transpose(1, 0, 2)
    return out.reshape(NB1 * 128, KC * 512)


def lay_gain(qk_gain):
    row = np.concatenate([np.tile(qk_gain[0, 0], 12), np.tile(qk_gain[1, 0], 8), np.tile(qk_gain[2, 0], 12),
                          np.tile(qk_gain[0, 1], 12), np.tile(qk_gain[1, 1], 2), np.tile(qk_gain[2, 1], 4)]).astype(np.float32)
    return np.ascontiguousarray(np.broadcast_to(row[None, :], (128, 3200)))


NEG = -30000.0
BIGF = 1.0e30
N_QH = 32
MB_B, MB_A0, MB_A1, MB_A2, MB_CW, MB_SL = 0, 3, 6, 12, 30, 36
NMASK = 38


def _bf(x):
    return np.asarray(x, np.float32).astype(ml_dtypes.bfloat16)


def _slopes(n):
    return (2.0 ** (-8.0 * np.arange(1, n + 1, dtype=np.float32) / n)).astype(np.float32)


def head_slopes():
    return np.concatenate([_slopes(12), _slopes(8), _slopes(12)]).astype(np.float64)


def build_consts(hf, NQT):
    S = 2 * NQT * 128
    NT = NQT * 128
    sl = head_slopes()
    s_hi = _bf(sl).astype(np.float64)
    s_lo = _bf(sl - s_hi).astype(np.float64)
    slp = s_hi + s_lo
    i = np.arange(NT)
    tq = (2 * (i // 128) + hf) * 128 + (i % 128)
    qpos = np.zeros((7, N_QH, NT), np.float64)
    qpos[0] = (128 * s_hi)[:, None]
    qpos[1] = (128 * s_lo)[:, None]
    qpos[2] = s_hi[:, None]
    qpos[3] = s_lo[:, None]
    v = -slp[:, None] * tq[None, :]
    v1 = _bf(v).astype(np.float64)
    v2 = _bf(v - v1).astype(np.float64)
    v3 = _bf(v - v1 - v2).astype(np.float64)
    qpos[4], qpos[5], qpos[6] = v1, v2, v3

    def kp(pos):
        t = np.zeros((7, len(pos)), np.float64)
        t[0] = t[1] = pos // 128
        t[2] = t[3] = pos % 128
        t[4:7] = 1.0
        return _bf(t)
    kpos = kp(np.arange(S))
    ncp = (S - 32) // 16 + 1
    NCT = (ncp + 127) // 128
    kposc = kp(16 * np.arange(NCT * 128) + 31)
    ki = np.arange(128)[:, None]
    qi = np.arange(128)[None, :]

    def wmask(delta, lo, hi, dil):
        dist = 128 * delta + qi - ki
        ok = (dist >= lo) & (dist <= hi) & (dist % dil == 0)
        return np.where(ok, 0.0, NEG)
    masks = np.zeros((NMASK, 128, 128), np.float32)
    for base, nrel, lo, hi, dil in ((MB_B, 3, 0, 127, 1), (MB_A0, 3, 0, 128, 1), (MB_A1, 6, 0, 512, 4),
                                    (MB_A2, 18, 0, 2048, 16), (MB_CW, 6, 0, 511, 1), (MB_SL, 2, 0, 1 << 30, 1)):
        for r in range(nrel):
            masks[base + r] = wmask(hf - 1 + r, lo, hi, dil)
    masks = np.ascontiguousarray(masks.transpose(1, 0, 2)).reshape(128, NMASK * 128)
    cmpmask = np.zeros((NQT, 128, 2, 128), np.float32)
    force = np.zeros((NQT, 128, 2, 128), np.float32)
    for it in range(NQT):
        Tq = 2 * it + hf
        t = Tq * 128 + np.arange(128)
        for rc in range(2):
            c = Tq // 16 - rc
            n = c * 128 + np.arange(128)
            ok = (16 * n[:, None] + 31) <= t[None, :]
            cmpmask[it, :, rc, :] = np.where(ok, 0.0, NEG)
        cur = t // 64
        j = np.arange(128)[None, :]
        forced = (j == 0) | (j == cur[:, None]) | (j == cur[:, None] - 1)
        force[it, :, 0, :] = np.where(forced, BIGF, -BIGF)
        force[it, :, 1, :] = np.where(j <= cur[:, None], BIGF, -BIGF)
    n = np.arange(NCT * 128)
    j = np.arange(128)
    ov = ((16 * n[:, None] <= 64 * j[None, :] + 63) & (16 * n[:, None] + 31 >= 64 * j[None, :]) & (n[:, None] < ncp))
    ov = np.ascontiguousarray(ov.reshape(NCT, 128, 128).transpose(1, 0, 2)).reshape(128, NCT * 128)
    tb = np.broadcast_to((1e-30 * (128 - np.arange(128)))[None, :], (128, 128))
    pp = np.arange(128)[:, None, None]
    t16 = np.arange(32)[None, :, None]
    kk = np.arange(128)[None, None, :]
    e32 = ((pp % 64) == 2 * t16 + (kk >= 64)).astype(np.float32).reshape(128, 4096)
    return {
        'e32': _bf(e32),
        'qpos': _bf(qpos), 'kpos': kpos, 'kposc': kposc, 'masks': _bf(masks),
        'cmpmask': _bf(cmpmask.reshape(NQT, 128, 256)), 'force': force.reshape(NQT, 128, 256).astype(np.float32),
        'ov': _bf(ov), 'tb': np.ascontiguousarray(tb).astype(np.float32),
        'ident': np.eye(128, dtype=ml_dtypes.bfloat16),
    }


CONST_SPECS = lambda NQT: {
    'qpos': ([7, N_QH, NQT * 128], BF16), 'kpos': ([7, 2 * NQT * 128], BF16),
    'kposc': ([7, ((2 * NQT * 128 - 32) // 16 + 1 + 127) // 128 * 128], BF16),
    'masks': ([128, NMASK * 128], BF16), 'cmpmask': ([NQT, 128, 256], BF16), 'force': ([NQT, 128, 256], F32),
    'ov': ([128, ((2 * NQT * 128 - 32) // 16 + 1 + 127) // 128 * 128], BF16), 'tb': ([128, 128], F32),
    'ident': ([128, 128], BF16), 'e32': ([128, 4096], BF16),
}


def kv_rows(kvf_d, T, NT, r0=0, r1=128, c0=0, c1=2560):
    base = (T % 2) * NT + (T // 2) * 128
    return kvf_d[base + r0:base + r1, c0:c1]


def phase_cmp(cx, kvf_d, w1_b, w2_b, pos_d, kgain_d, ident_d, cmpk_d, cmpv_d, NQT):
    P, A = cx.P, cx.A
    A.reset()
    NT = NQT * 128
    NTL = 2 * NQT
    S = NTL * 128
    ncp = (S - 32) // 16 + 1
    NCT = (ncp + 127) // 128
    NCP = NCT * 128
    ident = A.alloc(128, BF16)
    kid = cx.key('ident')
    P.dma(ident, ident_d[:, :], writes=[kid])
    kgain = A.alloc(64, F32)
    kkg = cx.key('kgain')
    P.dma(kgain, kgain_d[:, :], writes=[kkg])
    TT2 = A.alloc(S, BF16)
    ktt = cx.key('tt2')
    stage = [A.alloc(128, BF16) for _ in range(4)]
    kstage = [cx.key('stage') for _ in range(4)]
    for s_ in range(4):
        P.op('pool', lambda g, o=stage[s_]: g.memset(o, 0.0), writes=[kstage[s_]])
    w1 = A.alloc(16 * 256, BF16)
    kw1 = cx.key('w1')
    w2 = A.alloc(128, BF16)
    kw2 = cx.key('w2')
    posf = A.alloc(16, F32)
    posb = A.alloc(16, BF16)
    kpos = cx.key('pos')
    bvec = A.alloc(2, F32)
    kbv = cx.key('bvec')
    HT = A.alloc(2 * NCP, BF16)
    kht = cx.key('HT')
    P.op('pool', lambda g, o=HT: g.memset(o, 0.0), writes=[kht])
    sq = A.alloc(64, F32)
    st = A.alloc(8, F32)
    tmp = A.alloc(64, F32)
    ob = [A.alloc(64, BF16) for _ in range(2)]
    kob = [cx.key('ob') for _ in range(2)]
    kfin = cx.key('fin')
    io = 0
    ist = 0
    for kind in range(2):
        col0 = KV_CKC if kind == 0 else KV_CVC
        P.dma(w1, w1_b[kind * 128:(kind + 1) * 128, :], writes=[kw1])
        P.dma(w2, w2_b[kind * 128:(kind + 1) * 128, :], writes=[kw2])
        P.dma(posf, pos_d[kind * 128:(kind + 1) * 128, :], writes=[kpos + 'f'])
        P.op('dve', lambda g, o=posb, a=posf: g.tensor_copy(o, a), reads=[kpos + 'f'], writes=[kpos])
        for hc in range(2):
            for c in range(16):
                P.op('pe', lambda g, o=cx.bank(2)[:, hc:hc + 1], w=w1[:, c * 256 + hc * 128:c * 256 + (hc + 1) * 128], a=posb[:, c:c + 1], k=c, h=hc:
                     g.matmul(o, w, a, start=(k == 0 and h == 0), stop=(k == 15), skip_group_check=True),
                     reads=[kw1, kpos], writes=[('bank', 2)])
        P.op('dve', lambda g, o=bvec, a=cx.bank(2)[:, 0:2]: g.tensor_copy(o, a), reads=[('bank', 2)], writes=[kbv])
        for gi in range(2):
            col = col0 + 64 * gi
            for T in range(NTL):
                sg_ = ist % 4
                ist += 1
                P.dma(stage[sg_][:, 0:64], kv_rows(kvf_d, T, NT, 0, 128, col, col + 64), writes=[kstage[sg_]])
                P.dma(stage[sg_][0:127, 64:128], kv_rows(kvf_d, T, NT, 1, 128, col, col + 64), writes=[kstage[sg_] + 'b'])
                rd = [kstage[sg_], kstage[sg_] + 'b']
                if T + 1 < NTL:
                    P.dma(stage[sg_][127:128, 64:128], kv_rows(kvf_d, T + 1, NT, 0, 1, col, col + 64), writes=[kstage[sg_] + 'c'])
                    rd.append(kstage[sg_] + 'c')
                bnk = T % 2
                P.op('pe', lambda g, o=cx.bank_bf(bnk)[:, 0:128], a=stage[sg_]: g.transpose(o, a, ident), reads=rd + [kid], writes=[('bank', bnk)])
                eng = 'act' if T % 2 == 0 else 'dve'
                if eng == 'act':
                    P.op('act', lambda g, o=TT2[:, T * 128:(T + 1) * 128], a=cx.bank_bf(bnk)[:, 0:128]: g.copy(o, a),
                         reads=[('bank', bnk)], writes=[ktt, kstage[sg_], kstage[sg_] + 'b', kstage[sg_] + 'c'])
                else:
                    P.op('dve', lambda g, o=TT2[:, T * 128:(T + 1) * 128], a=cx.bank_bf(bnk)[:, 0:128]: g.tensor_copy(o, a),
                         reads=[('bank', bnk)], writes=[ktt, kstage[sg_], kstage[sg_] + 'b', kstage[sg_] + 'c'])
            TT3 = TT2.rearrange('p (n s) -> p n s', s=16)
            for hc in range(2):
                for n0 in range(0, ncp, 512):
                    n1 = min(ncp, n0 + 512)
                    bnk = 3 + hc
                    for c in range(16):
                        P.op('pe', lambda g, o=cx.bank(bnk)[:, 0:n1 - n0], w=w1[:, c * 256 + hc * 128:c * 256 + (hc + 1) * 128],
                             a=TT3[:, n0 + (2 * c) // 16:n1 + (2 * c) // 16, (2 * c) % 16], k=c: g.matmul(o, w, a, start=(k == 0), stop=(k == 15)),
                             reads=[kw1, ktt], writes=[('bank', bnk)])
                    P.op('act', lambda g, o=HT[:, hc * NCP + n0:hc * NCP + n1], a=cx.bank(bnk)[:, 0:n1 - n0], b=bvec[:, hc:hc + 1]:
                         g.activation(o, a, AF.Silu, bias=b), reads=[('bank', bnk), kbv], writes=[kht])
            for j in range(NCT):
                bnk = 5 + j % 2
                for hc in range(2):
                    P.op('pe', lambda g, o=cx.bank(bnk)[:, 0:64], a=HT[:, hc * NCP + j * 128:hc * NCP + (j + 1) * 128], w=w2[:, hc * 64:(hc + 1) * 64], h=hc:
                         g.matmul(o, a, w, start=(h == 0), stop=(h == 1)), reads=[kht, kw2], writes=[('bank', bnk)])
                o_ = io % 2
                io += 1
                ps = cx.bank(bnk)[:, 0:64]
                if kind == 0:
                    P.op('act', lambda g, o=sq, a=ps, acc=st[:, 0:1]: g.activation(o, a, AF.Square, accum_out=acc), reads=[('bank', bnk)], writes=[kfin + 'a'])
                    P.op('dve', lambda g, o=st[:, 1:2], a=st[:, 0:1]: g.tensor_scalar(o, a, 1.0 / HD, RMS_EPS, ALU.mult, ALU.add), reads=[kfin + 'a'], writes=[kfin + 'b'])
                    P.op('act', lambda g, o=st[:, 2:3], a=st[:, 1:2]: g.sqrt(o, a), reads=[kfin + 'b'], writes=[kfin + 'c'])
                    P.op('dve', lambda g, o=st[:, 3:4], a=st[:, 2:3]: g.reciprocal(o, a), reads=[kfin + 'c'], writes=[kfin + 'd'])
                    P.op('dve', lambda g, o=tmp, a=ps, r=st[:, 3:4]: g.tensor_scalar(o, a, r, None, ALU.mult), reads=[('bank', bnk), kfin + 'd'], writes=[kfin + 'e'])
                    P.op('dve', lambda g, o=ob[o_], a=tmp, gn=kgain: g.tensor_tensor(o, a, gn, ALU.mult), reads=[kfin + 'e', kkg], writes=[kob[o_]])
                    P.dma(cmpk_d[j * 128:(j + 1) * 128, gi * 64:(gi + 1) * 64], ob[o_], reads=[kob[o_]])
                else:
                    P.op('act', lambda g, o=ob[o_], a=ps: g.copy(o, a), reads=[('bank', bnk)], writes=[kob[o_]])
                    P.dma(cmpv_d[j * 128:(j + 1) * 128, gi * 64:(gi + 1) * 64], ob[o_], reads=[kob[o_]])
    P.barrier()


def phase_attn(cx, qn_d, kvf_d, cmpk_d, cmpv_d, cg_d, sinks_d, C, ot_d, NQT):
    P, A = cx.P, cx.A
    A.reset()
    NT = NQT * 128
    NTL = 2 * NQT
    S = NTL * 128
    ncp = (S - 32) // 16 + 1
    NCT = (ncp + 127) // 128
    VW = 66
    GROUPS = [('A0', 4, KV_AK, KV_AV, 3, 5, 0, 1, MB_A0), ('A1', 4, KV_AK + 256, KV_AV + 256, 6, 8, 4, 1, MB_A1),
              ('A2', 4, KV_AK + 512, KV_AV + 512, 18, 20, 8, 1, MB_A2), ('B', 2, KV_BK, KV_BV, 3, 5, 12, 4, MB_B),
              ('CW', 2, KV_CKW, KV_CVW, 6, 8, 20, 6, MB_CW), ('SL', 2, KV_CKS, KV_CVS, NTL, NTL, 20, 6, MB_SL)]
    GI = {g[0]: g for g in GROUPS}

    def load_const(name, cols, dtype, src):
        t = A.alloc(cols, dtype)
        k = cx.key(name)
        P.dma(t, src, writes=[k])
        return t, k
    ident, kid = load_const('ident', 128, BF16, C['ident'][:, :])
    masks, kmk = load_const('masks', NMASK * 128, BF16, C['masks'][:, :])
    ov, kov = load_const('ov', NCT * 128, BF16, C['ov'][:, :])
    tb, ktb = load_const('tb', 128, F32, C['tb'][:, :])
    e32, ke32 = load_const('e32', 4096, BF16, C['e32'][:, :])
    esink, kes = load_const('esink', 8, F32, sinks_d[:, :])
    P.op('act', lambda g, o=esink: g.activation(o, o, AF.Exp), reads=[kes], writes=[kes])
    ident3 = lambda R: ident.unsqueeze(1).to_broadcast([128, R, 128])
    kt, vc, kK, kV = {}, {}, {}, {}
    for (name, ns, kcol, vcol, nrel, ring, qh0, R, mb) in GROUPS + [('CMP', 2, 0, 0, NCT, NCT, 20, 6, 0)]:
        kt[name] = A.alloc(ring * ns * 128, BF16)
        vc[name] = A.alloc(ring * ns * VW, BF16)
        kK[name] = [cx.key('K' + name) for _ in range(ring)]
        kV[name] = [cx.key('V' + name) for _ in range(ring)]
        P.op('pool', lambda g, o=vc[name]: g.memset(o, 1.0), writes=kV[name])

    def Kap(name, pos, s):
        ns = 2 if name == 'CMP' else GI[name][1]
        return kt[name][0:71, (pos * ns + s) * 128:(pos * ns + s + 1) * 128]

    def Vap(name, pos, s):
        ns = 2 if name == 'CMP' else GI[name][1]
        return vc[name][:, (pos * ns + s) * VW:(pos * ns + s) * VW + 65]
    QaT = [A.alloc(N_QH * 128, BF16) for _ in range(2)]
    kQ = [cx.key('QaT') for _ in range(2)]
    qst = [A.alloc(2048, BF16) for _ in range(2)]
    kqst = [cx.key('qst') for _ in range(2)]
    kvst = [A.alloc(2560, BF16) for _ in range(2)]
    kkvst = [cx.key('kvst') for _ in range(2)]
    NPT = 4
    PT = [A.alloc(512, BF16) for _ in range(NPT)]
    kPT = [cx.key('PT') for _ in range(NPT)]
    cgt = [A.alloc(36, F32) for _ in range(2)]
    kcg = [cx.key('cg') for _ in range(2)]
    frc = [A.alloc(256, F32) for _ in range(2)]
    kfrc = [cx.key('frc') for _ in range(2)]
    cmk = [A.alloc(256, BF16) for _ in range(2)]
    kcmk = [cx.key('cmk') for _ in range(2)]
    oall = [A.alloc(1536, BF16) for _ in range(2)]
    koall = [cx.key('oall') for _ in range(2)]
    oTs = A.alloc(1536, BF16)
    koTs = cx.key('oTs')
    oc = A.alloc(768, F32)
    koc = [cx.key('oc') for _ in range(4)]
    tmp3 = [A.alloc(192, F32) for _ in range(2)]
    ktmp3 = [cx.key('tmp3') for _ in range(2)]
    sm = [A.alloc(16, F32) for _ in range(4)]
    ksm = [cx.key('sm') for _ in range(4)]
    imp = [A.alloc(128, F32) for _ in range(2)]
    kimp = [cx.key('imp') for _ in range(2)]
    wk = [A.alloc(128, F32) for _ in range(3)]
    kwk = cx.key('wk')
    m8 = A.alloc(16, F32)
    selm = [A.alloc(128, BF16) for _ in range(2)]
    kselm = [cx.key('selm') for _ in range(2)]
    selmT = [A.alloc(128, BF16) for _ in range(2)]
    kselmT = [cx.key('selmT') for _ in range(2)]
    ism = [0]
    itp = [0]
    tpbank = lambda: (itp.__setitem__(0, itp[0] + 1), (itp[0] % 2))[1]

    cst = [A.alloc(128, BF16) for _ in range(2)]
    kcst = [cx.key('cst') for _ in range(2)]
    for c in range(NCT):
        s_ = c % 2
        P.dma(cst[s_], cmpk_d[c * 128:(c + 1) * 128, :], writes=[kcst[s_]])
        b = tpbank()
        for g_ in range(2):
            P.op('pe', lambda g, o=cx.bank_bf(b)[0:64, g_ * 128:(g_ + 1) * 128], a=cst[s_][:, g_ * 64:(g_ + 1) * 64]: g.transpose(o, a, ident),
                 reads=[kcst[s_], kid], writes=[('bank', b)])
        P.op('dve', lambda g, o=kt['CMP'][0:64, c * 256:(c + 1) * 256], a=cx.bank_bf(b)[0:64, 0:256]: g.tensor_copy(o, a),
             reads=[('bank', b)], writes=[kK['CMP'][c]])
        P.dma(kt['CMP'][64:71, c * 256:(c + 1) * 256].rearrange('p (s t) -> p s t', s=2),
              C['kposc'][:, c * 128:(c + 1) * 128].unsqueeze(1).to_broadcast([7, 2, 128]), writes=[kK['CMP'][c] + 'p'])
        P.dma(vc['CMP'][:, c * 2 * VW:(c + 1) * 2 * VW].rearrange('p (s w) -> p s w', w=VW)[:, :, 0:64],
              cmpv_d[c * 128:(c + 1) * 128, :].rearrange('p (s d) -> p s d', d=64), reads=[kV['CMP'][c]], writes=[kV['CMP'][c] + 'd'])

    def prep(i):
        for T in (2 * i, 2 * i + 1):
            s_ = T % 2
            P.dma(kvst[s_], kv_rows(kvf_d, T, NT), writes=[kkvst[s_]])
            for (name, ns, kcol, vcol, nrel, ring, qh0, R, mb) in GROUPS:
                pos = T % ring
                b = tpbank()
                for s2 in range(ns):
                    P.op('pe', lambda g, o=cx.bank_bf(b)[0:64, s2 * 128:(s2 + 1) * 128], a=kvst[s_][:, kcol + s2 * 64:kcol + (s2 + 1) * 64]:
                         g.transpose(o, a, ident), reads=[kkvst[s_], kid], writes=[('bank', b)])
                dst = kt[name][0:64, pos * ns * 128:(pos + 1) * ns * 128]
                if b == 0:
                    P.op('act', lambda g, o=dst, a=cx.bank_bf(b)[0:64, 0:ns * 128]: g.copy(o, a), reads=[('bank', b)], writes=[kK[name][pos]])
                else:
                    P.op('dve', lambda g, o=dst, a=cx.bank_bf(b)[0:64, 0:ns * 128]: g.tensor_copy(o, a), reads=[('bank', b)], writes=[kK[name][pos]])
                P.dma(kt[name][64:71, pos * ns * 128:(pos + 1) * ns * 128].rearrange('p (s t) -> p s t', s=ns),
                      C['kpos'][:, T * 128:(T + 1) * 128].unsqueeze(1).to_broadcast([7, ns, 128]),
                      reads=[kK[name][pos]], writes=[kK[name][pos] + 'p'])
                P.op('pool', lambda g, o=vc[name][:, pos * ns * VW:(pos + 1) * ns * VW].rearrange('p (s w) -> p s w', w=VW)[:, :, 0:64],
                     a=kvst[s_][:, vcol:vcol + ns * 64].rearrange('p (s d) -> p s d', d=64): g.tensor_copy(o, a),
                     reads=[kkvst[s_], kV[name][pos]], writes=[kV[name][pos] + 'd'])
        qb = i % 2
        P.dma(qst[qb], qn_d[i * 128:(i + 1) * 128, :], writes=[kqst[qb]])
        for h0 in range(0, N_QH, 8):
            b = tpbank()
            for h in range(8):
                P.op('pe', lambda g, o=cx.bank_bf(b)[0:64, h * 128:(h + 1) * 128], a=qst[qb][:, (h0 + h) * 64:(h0 + h + 1) * 64]:
                     g.transpose(o, a, ident), reads=[kqst[qb], kid], writes=[('bank', b)])
            if b == 0:
                P.op('act', lambda g, o=QaT[qb][0:64, h0 * 128:(h0 + 8) * 128], a=cx.bank_bf(b)[0:64, :]: g.copy(o, a), reads=[('bank', b)], writes=[kQ[qb]])
            else:
                P.op('dve', lambda g, o=QaT[qb][0:64, h0 * 128:(h0 + 8) * 128], a=cx.bank_bf(b)[0:64, :]: g.tensor_copy(o, a), reads=[('bank', b)], writes=[kQ[qb]])
        P.dma(QaT[qb][64:71, :].rearrange('p (h t) -> p h t', h=N_QH), C['qpos'][:, :, i * 128:(i + 1) * 128], reads=[kQ[qb]], writes=[kQ[qb] + 'p'])
        P.dma(cgt[qb], cg_d[i * 128:(i + 1) * 128, :], writes=[kcg[qb]])
        P.dma(frc[qb], C['force'][i], writes=[kfrc[qb]])
        P.dma(cmk[qb], C['cmpmask'][i], writes=[kcmk[qb]])

    def kdeps(name, pos):
        return [kK[name][pos], kK[name][pos] + 'p']

    def vdeps(name, pos):
        return [kV[name][pos], kV[name][pos] + 'd']

    def run_tile(i):
        qb = i % 2
        Tq_c = i // 8
        cgv = cgt[qb].rearrange('p (h b) -> p h b', b=3)
        steps = []
        rounds = []

        def add_round(kind, units, **kw):
            rd = dict(kind=kind, nsteps=0, **kw)
            rd['acc'] = 6 + len(rounds) % 2
            rounds.append(rd)
            for u in units:
                for tl in u['tiles']:
                    steps.append((rd, u, tl))
                    rd['nsteps'] += 1
        for g_ in range(2):
            for half in range(2):
                tiles = []
                for c in range(0, Tq_c + 1):
                    m = []
                    if c == Tq_c:
                        m = [('w', cmk[qb][:, 0:128], [kcmk[qb]])]
                    elif c == Tq_c - 1:
                        m = [('w', cmk[qb][:, 128:256], [kcmk[qb]])]
                    tiles.append(dict(K=Kap('CMP', c, g_), V=Vap('CMP', c, g_), kd=kdeps('CMP', c), vd=vdeps('CMP', c), masks=m, ov=c))
                add_round('CMP', [dict(q0=20 + 6 * g_ + 3 * half, R=3, slot=0, tiles=tiles)], g=g_, half=half, br=0)
        units = []
        for gn in ('A0', 'A1', 'A2'):
            (name, ns, kcol, vcol, nrel, ring, qh0, R, mb) = GI[gn]
            for h in range(4):
                tiles = []
                for r in range(nrel):
                    T = 2 * i + 1 - r
                    if T < 0:
                        continue
                    pos = T % ring
                    tiles.append(dict(K=Kap(name, pos, h), V=Vap(name, pos, h), kd=kdeps(name, pos), vd=vdeps(name, pos),
                                      masks=[('w', masks[:, (mb + r) * 128:(mb + r + 1) * 128], [kmk])]))
                units.append(dict(q0=qh0 + h, R=1, slot=h, tiles=tiles))
        add_round('A', units)
        for j in range(2):
            (name, ns, kcol, vcol, nrel, ring, qh0, R, mb) = GI['B']
            tiles = []
            for r in range(nrel):
                T = 2 * i + 1 - r
                if T < 0:
                    continue
                pos = T % ring
                tiles.append(dict(K=Kap(name, pos, j), V=Vap(name, pos, j), kd=kdeps(name, pos), vd=vdeps(name, pos),
                                  masks=[('w', masks[:, (mb + r) * 128:(mb + r + 1) * 128], [kmk])]))
            add_round('B', [dict(q0=qh0 + 4 * j, R=4, slot=0, tiles=tiles)], j=j)
        for gname, br in (('CW', 2), ('SL', 1)):
            (name, ns, kcol, vcol, nrel, ring, qh0, R, mb) = GI[gname]
            for g_ in range(2):
                for half in range(2):
                    tiles = []
                    rels = range(nrel) if gname == 'CW' else range(2 * i + 1, -1, -1)
                    for r in rels:
                        T = 2 * i + 1 - r
                        if T < 0:
                            continue
                        pos = T % ring
                        m = []
                        if gname == 'CW':
                            m = [('w', masks[:, (mb + r) * 128:(mb + r + 1) * 128], [kmk])]
                        else:
                            m = [('s', (T, selmT[g_]), [kselmT[g_]])]
                            if r < 2:
                                m.append(('w', masks[:, (mb + r) * 128:(mb + r + 1) * 128], [kmk]))
                        tiles.append(dict(K=Kap(name, pos, g_), V=Vap(name, pos, g_), kd=kdeps(name, pos), vd=vdeps(name, pos), masks=m))
                    add_round(gname, [dict(q0=qh0 + 6 * g_ + 3 * half, R=3, slot=0, tiles=tiles)], g=g_, half=half, br=br)

        def emit_qk(k):
            rd, u, tl = steps[k]
            R = u['R']
            sb = 2 + k % 3
            Sv = cx.bank(sb)[:, 0:R * 128]
            S3 = Sv.rearrange('p (r q) -> p r q', r=R)
            nm = len(tl['masks'])
            P.op('pe', lambda g, o=Sv, kk=tl['K'], q=QaT[qb][0:71, u['q0'] * 128:(u['q0'] + R) * 128], last=(nm == 0):
                 g.matmul(o, kk, q, start=True, stop=last), reads=tl['kd'] + [kQ[qb], kQ[qb] + 'p'], writes=[('bank', sb)])
            for mi, (mk, ap, deps) in enumerate(tl['masks']):
                last = (mi == nm - 1)
                if mk == 'w':
                    P.op('pe', lambda g, o=S3, a=ap.unsqueeze(1).to_broadcast([128, R, 128]), last=last:
                         g.matmul(o, ident, a, start=False, stop=last), reads=deps + [kid], writes=[('bank', sb)])
                else:
                    T_, sT = ap
                    q4, t16 = T_ // 32, T_ % 32
                    P.op('pe', lambda g, o=S3, w=e32[q4 * 64:(q4 + 1) * 64, t16 * 128:(t16 + 1) * 128],
                         a=sT[q4 * 64:(q4 + 1) * 64, :].unsqueeze(1).to_broadcast([64, R, 128]), last=last:
                         g.matmul(o, w, a, start=False, stop=last), reads=deps + [ke32], writes=[('bank', sb)])
            pt = k % NPT
            P.op('act', lambda g, o=PT[pt][:, 0:R * 128], a=Sv: g.activation(o, a, AF.Exp), reads=[('bank', sb)], writes=[kPT[pt]])

        def emit_pv(k):
            rd, u, tl = steps[k]
            R = u['R']
            pt = k % NPT
            ab = rd['acc']
            for r in range(R):
                first = not rd.get('acc_started', False)
                rd['acc_started'] = True
                sl_ = u['slot'] + r
                P.op('pe', lambda g, o=cx.bank(ab)[:, sl_ * 65:(sl_ + 1) * 65], p=PT[pt][:, r * 128:(r + 1) * 128], v=tl['V'], first=first:
                     g.matmul(o, p, v, start=first, stop=True, skip_group_check=True), reads=[kPT[pt]] + tl['vd'], writes=[('bank', ab)])
            if rd['kind'] == 'CMP':
                for r in range(R):
                    first = not rd.get('u_started', False)
                    rd['u_started'] = True
                    P.op('pe', lambda g, o=cx.bank(5)[:, r * 128:(r + 1) * 128], p=PT[pt][:, r * 128:(r + 1) * 128],
                         w=ov[:, tl['ov'] * 128:(tl['ov'] + 1) * 128], first=first:
                         g.matmul(o, p, w, start=first, stop=True, skip_group_check=True), reads=[kPT[pt], kov], writes=[('bank', 5)])
            rd['nsteps'] -= 1
            if rd['nsteps'] == 0:
                finalize(rd)

        def finalize(rd):
            ab = rd['acc']
            kind = rd['kind']
            ob_ = oall[qb]
            s_ = ism[0] % 4
            ism[0] += 1
            smt, ks_ = sm[s_], ksm[s_]
            if kind in ('A', 'B'):
                accv = cx.bank(ab)[:, 0:260].rearrange('p (h c) -> p h c', c=65)
                if kind == 'A':
                    P.op('dve', lambda g, o=smt[:, 0:4], a=accv[:, :, 64]: g.reciprocal(o, a), reads=[('bank', ab)], writes=[ks_])
                    col = 0
                else:
                    j = rd['j']
                    P.op('dve', lambda g, o=smt[:, 4:8], a=accv[:, :, 64], e=esink[:, 4 * j:4 * j + 4]: g.tensor_tensor(o, a, e, ALU.add),
                         reads=[('bank', ab), kes], writes=[ks_ + 'a'])
                    P.op('dve', lambda g, o=smt[:, 0:4], a=smt[:, 4:8]: g.reciprocal(o, a), reads=[ks_ + 'a'], writes=[ks_])
                    col = 256 + 256 * j
                P.op('dve', lambda g, o=ob_[:, col:col + 256].rearrange('p (h d) -> p h d', d=64), a=accv[:, :, 0:64],
                     r=smt[:, 0:4].unsqueeze(2).to_broadcast([128, 4, 64]): g.tensor_tensor(o, a, r, ALU.mult),
                     reads=[('bank', ab), ks_], writes=[koall[qb]])
                return
            g_, half, br = rd['g'], rd['half'], rd['br']
            hh0 = 6 * g_ + 3 * half
            ui = 2 * g_ + half
            accv = cx.bank(ab)[:, 0:195].rearrange('p (h c) -> p h c', c=65)
            P.op('dve', lambda g, o=smt[:, 0:3], a=accv[:, :, 64]: g.tensor_scalar(o, a, 1e-30, None, ALU.max), reads=[('bank', ab)], writes=[ks_ + 'a'])
            P.op('dve', lambda g, o=smt[:, 4:7], a=smt[:, 0:3]: g.reciprocal(o, a), reads=[ks_ + 'a'], writes=[ks_ + 'b'])
            P.op('dve', lambda g, o=smt[:, 8:11], a=smt[:, 4:7], c=cgv[:, hh0:hh0 + 3, br]: g.tensor_tensor(o, a, c, ALU.mult),
                 reads=[ks_ + 'b', kcg[qb]], writes=[ks_ + 'c'])
            t3 = ism[0] % 2
            ocv = oc[:, hh0 * 64:(hh0 + 3) * 64]
            if kind == 'CMP':
                P.op('dve', lambda g, o=ocv.rearrange('p (h d) -> p h d', d=64), a=accv[:, :, 0:64],
                     w=smt[:, 8:11].unsqueeze(2).to_broadcast([128, 3, 64]): g.tensor_tensor(o, a, w, ALU.mult),
                     reads=[('bank', ab), ks_ + 'c'], writes=[koc[ui]])
                for r in range(3):
                    src = tb if (half == 0 and r == 0) else imp[g_]
                    P.op('dve', lambda g, o=imp[g_], u=cx.bank(5)[:, r * 128:(r + 1) * 128], s=smt[:, 4 + r:5 + r], a=src:
                         g.scalar_tensor_tensor(o, u, s, a, ALU.mult, ALU.add), reads=[('bank', 5), ks_ + 'b', ktb, kimp[g_]], writes=[kimp[g_]])
                if half == 1:
                    P.op('dve', lambda g, o=wk[0], a=imp[g_], f=frc[qb][:, 0:128]: g.tensor_tensor(o, a, f, ALU.max), reads=[kimp[g_], kfrc[qb]], writes=[kwk + '0'])
                    P.op('dve', lambda g, o=wk[1], a=wk[0], f=frc[qb][:, 128:256]: g.tensor_tensor(o, a, f, ALU.min), reads=[kwk + '0', kfrc[qb]], writes=[kwk + '1'])
                    P.op('dve', lambda g, o=m8[:, 0:8], a=wk[1]: g.max(o, a), reads=[kwk + '1'], writes=[kwk + 'm'])
                    P.op('dve', lambda g, o=wk[2], a=m8[:, 0:8], b=wk[1]: g.match_replace(o, a, b, -3.0e38), reads=[kwk + 'm', kwk + '1'], writes=[kwk + '2'])
                    P.op('dve', lambda g, o=m8[:, 8:16], a=wk[2]: g.max(o, a), reads=[kwk + '2'], writes=[kwk + 'n'])
                    P.op('dve', lambda g, o=wk[0], a=wk[1], t=m8[:, 15:16]: g.tensor_scalar(o, a, t, None, ALU.is_ge), reads=[kwk + '1', kwk + 'n'], writes=[kwk + '0'])
                    P.op('dve', lambda g, o=selm[g_], a=wk[0]: g.tensor_scalar(o, a, -1.0, -NEG, ALU.add, ALU.mult), reads=[kwk + '0'], writes=[kselm[g_]])
                    tb_ = tpbank()
                    P.op('pe', lambda g, o=cx.bank_bf(tb_)[:, 0:128], a=selm[g_]: g.transpose(o, a, ident), reads=[kselm[g_], kid], writes=[('bank', tb_)])
                    P.op('act', lambda g, o=selmT[g_], a=cx.bank_bf(tb_)[:, 0:128]: g.copy(o, a), reads=[('bank', tb_)], writes=[kselmT[g_]])
            else:
                P.op('dve', lambda g, o=tmp3[t3].rearrange('p (h d) -> p h d', d=64), a=accv[:, :, 0:64],
                     w=smt[:, 8:11].unsqueeze(2).to_broadcast([128, 3, 64]): g.tensor_tensor(o, a, w, ALU.mult),
                     reads=[('bank', ab), ks_ + 'c'], writes=[ktmp3[t3]])
                if kind == 'CW':
                    P.op('pool', lambda g, o=ocv, a=ocv, b=tmp3[t3]: g.tensor_tensor(o, a, b, ALU.add), reads=[koc[ui], ktmp3[t3]], writes=[koc[ui]])
                else:
                    P.op('pool', lambda g, o=ob_[:, 768 + hh0 * 64:768 + (hh0 + 3) * 64], a=ocv, b=tmp3[t3]: g.tensor_tensor(o, a, b, ALU.add),
                         reads=[koc[ui], ktmp3[t3]], writes=[koall[qb]])

        LOOK = 2
        n = len(steps)
        for k in range(n + LOOK):
            if k < n:
                emit_qk(k)
            if k >= LOOK:
                emit_pv(k - LOOK)
        for f0, nf, b in ((0, 8, 0), (8, 4, 1)):
            for f in range(nf):
                P.op('pe', lambda g, o=cx.bank_bf(b)[:, f * 128:(f + 1) * 128], a=oall[qb][:, (f0 + f) * 128:(f0 + f + 1) * 128]: g.transpose(o, a, ident),
                     reads=[koall[qb], kid], writes=[('bank', b)])
            P.op('act', lambda g, o=oTs[:, f0 * 128:(f0 + nf) * 128], a=cx.bank_bf(b)[:, 0:nf * 128]: g.copy(o, a), reads=[('bank', b)], writes=[koTs])
        P.dma(ot_d.rearrange('f p t -> p f t')[:, :, i * 128:(i + 1) * 128], oTs.rearrange('p (f t) -> p f t', f=12), reads=[koTs])

    prep(0)
    for i in range(NQT):
        if i + 1 < NQT:
            prep(i + 1)
        run_tile(i)
    P.barrier()


BR_FB = ((0, 2), (2, 6), (6, 12))


def phase_out(cx, x_in, x_out, gcol_d, wgate_b, wbr_b, wout_b, ot_d, NT, ident_d, TT=512):
    P, A = cx.P, cx.A
    A.reset()
    nsub = TT // 128
    ident = A.alloc(128, BF16)
    kid = cx.key('ident')
    gcol = A.alloc(KC, F32)
    kg = cx.key('gcol')
    P.dma(ident, ident_d[:, :], writes=[kid])
    P.dma(gcol, gcol_d[:, :], writes=[kg])
    xnT = A.alloc(KC * TT, BF16)
    kxn = cx.key('xnT')
    mT = A.alloc(KC * TT, BF16)
    kmT = [cx.key('mT') for _ in range(KC)]
    oTt = A.alloc(12 * TT, BF16)
    koT = cx.key('oTt')
    xs = [A.alloc(D, F32) for _ in range(2)]
    kxs = [cx.key('xs') for _ in range(2)]
    xb = [A.alloc(D, BF16) for _ in range(2)]
    kxb = [cx.key('xb') for _ in range(2)]
    junk = A.alloc(D, BF16)
    kjunk = cx.key('junk')
    st = [A.alloc(8, F32) for _ in range(2)]
    kst = [cx.key('st') for _ in range(2)]
    NG = 4
    wg = [A.alloc(KC * 128, BF16) for _ in range(NG)]
    kwg = [cx.key('wg') for _ in range(NG)]
    wbr = [A.alloc(1536, BF16) for _ in range(2)]
    kwbr = [cx.key('wbr') for _ in range(2)]
    wo = [A.alloc(D, BF16) for _ in range(3)]
    kwo = [cx.key('wo') for _ in range(3)]
    sg = [A.alloc(TT, F32) for _ in range(2)]
    ksg = [cx.key('sg') for _ in range(2)]
    macc = [A.alloc(TT, F32) for _ in range(2)]
    kmacc = [cx.key('macc') for _ in range(2)]
    tmpm = [A.alloc(TT, F32) for _ in range(2)]
    ktmpm = [cx.key('tmpm') for _ in range(2)]
    xr = [A.alloc(D, F32) for _ in range(2)]
    kxr = [cx.key('xr') for _ in range(2)]
    xo = [A.alloc(D, F32) for _ in range(2)]
    kxo = [cx.key('xo') for _ in range(2)]
    ig = 0
    iwo = 0
    ipair = 0
    for t0 in range(0, NT, TT):
        emit_norm_transpose(cx, x_in, t0, nsub, gcol, kg, xnT, kxn, xs, kxs, xb, kxb, junk, kjunk, st, kst, ident, 4)
        P.dma(oTt.rearrange('p (f t) -> p f t', f=12), ot_d.rearrange('f p t -> p f t')[:, :, t0:t0 + TT], writes=[koT])
        for dc in range(KC):
            wb = dc % 2
            P.dma(wbr[wb], wbr_b[dc * 128:(dc + 1) * 128, :], writes=[kwbr[wb]])
            ma = dc % 2
            for b in range(3):
                s = ig % NG
                ig += 1
                P.dma(wg[s], wgate_b[(b * KC + dc) * 128:(b * KC + dc + 1) * 128, :], writes=[kwg[s]])
                bg = ipair % 2
                bb = 2 + ipair % 2
                ipair += 1
                pg, pb = cx.bank(bg)[:, 0:TT], cx.bank(bb)[:, 0:TT]
                for kc in range(KC):
                    P.op('pe', lambda g, o=pg, w=wg[s][:, kc * 128:(kc + 1) * 128], a=xnT[:, kc * TT:(kc + 1) * TT], k=kc:
                         g.matmul(o, w, a, start=(k == 0), stop=(k == KC - 1)), reads=[kwg[s], kxn], writes=[('bank', bg)])
                f0, f1 = BR_FB[b]
                for fb in range(f0, f1):
                    P.op('pe', lambda g, o=pb, w=wbr[wb][:, fb * 128:(fb + 1) * 128], a=oTt[:, fb * TT:(fb + 1) * TT], f=fb, f0=f0, f1=f1:
                         g.matmul(o, w, a, start=(f == f0), stop=(f == f1 - 1)), reads=[kwbr[wb], koT], writes=[('bank', bb)])
                q = ipair % 2
                P.op('act', lambda g, o=sg[q], a=pg: g.activation(o, a, AF.Sigmoid), reads=[('bank', bg)], writes=[ksg[q]])
                if b == 0:
                    P.op('dve', lambda g, o=macc[ma], a=pb, c=sg[q]: g.tensor_tensor(o, a, c, ALU.mult), reads=[('bank', bb), ksg[q]], writes=[kmacc[ma]])
                else:
                    P.op('dve', lambda g, o=tmpm[q], a=pb, c=sg[q]: g.tensor_tensor(o, a, c, ALU.mult), reads=[('bank', bb), ksg[q]], writes=[ktmpm[q]])
                    if b == 1:
                        P.op('pool', lambda g, o=macc[ma], a=macc[ma], c=tmpm[q]: g.tensor_tensor(o, a, c, ALU.add), reads=[kmacc[ma], ktmpm[q]], writes=[kmacc[ma]])
                    else:
                        P.op('pool', lambda g, o=mT[:, dc * TT:(dc + 1) * TT], a=macc[ma], c=tmpm[q]: g.tensor_tensor(o, a, c, ALU.add),
                             reads=[kmacc[ma], ktmpm[q]], writes=[kmT[dc]])
        for s0 in range(0, nsub, 2):
            for dc in range(KC):
                s = iwo % 3
                iwo += 1
                P.dma(wo[s], wout_b[dc * 128:(dc + 1) * 128, :], writes=[kwo[s]])
                for ss in range(2):
                    for cb in range(4):
                        b = ss * 4 + cb
                        P.op('pe', lambda g, o=cx.bank(b), m=mT[:, dc * TT + (s0 + ss) * 128:dc * TT + (s0 + ss + 1) * 128],
                             w=wo[s][:, cb * 512:(cb + 1) * 512], d=dc: g.matmul(o, m, w, start=(d == 0), stop=(d == KC - 1)),
                             reads=[kwo[s], kmT[dc]], writes=[('bank', b)])
            for ss in range(2):
                tok = t0 + (s0 + ss) * 128
                q = ss
                P.dma(xr[q], x_in[tok:tok + 128, :], writes=[kxr[q]])
                for cb in range(4):
                    b = ss * 4 + cb
                    P.op('dve', lambda g, o=xo[q][:, cb * 512:(cb + 1) * 512], a=cx.bank(b), r=xr[q][:, cb * 512:(cb + 1) * 512]:
                         g.tensor_tensor(o, a, r, ALU.add), reads=[('bank', b), kxr[q]], writes=[kxo[q]])
                P.dma(x_out[tok:tok + 128, :], xo[q], reads=[kxo[q]])
    P.barrier()


def lay_wgate(w_in):
    g = w_in[:, GATE0:GATE0 + 3 * D]
    return np.ascontiguousarray(g.reshape(KC, 128, 3 * KC, 128).transpose(2, 1, 0, 3)).reshape(3 * KC * 128, KC * 128)


def lay_wbr(wa, wb, wc):
    w = np.concatenate([wa, wb, wc], axis=0)
    return np.ascontiguousarray(w.reshape(12, 128, KC, 128).transpose(2, 1, 0, 3)).reshape(KC * 128, 1536)


def lay_cmp_w1(w1):
    return np.ascontiguousarray(w1.reshape(2, 16, 128, 256).transpose(0, 2, 1, 3)).reshape(2 * 128, 16 * 256)


def lay_cmp_w2(w2):
    return np.ascontiguousarray(w2.reshape(2, 2, 128, 64).transpose(0, 2, 1, 3)).reshape(2 * 128, 128)


def lay_cmp_pos(pos):
    return np.ascontiguousarray(pos.reshape(2, 16, 2, 64).transpose(0, 2, 3, 1)).reshape(2 * 128, 16)


NCORES = 8
SEQ = 8192
BATCH = 4
NQT_FULL = SEQ // 256
NT_FULL = NQT_FULL * 128


def _din(nc, name, shape, dt):
    return nc.dram_tensor(name, list(shape), dt, kind="ExternalInput").ap()


def _dout(nc, name, shape, dt):
    return nc.dram_tensor(name, list(shape), dt, kind="ExternalOutput").ap()


def _dint(nc, name, shape, dt):
    return nc.dram_tensor(name, list(shape), dt, kind="Internal").ap()


def build_prog_a(NT):
    nc = bass.Bass("TRN2", target_bir_lowering=False)
    x = _din(nc, "x", [NT, D], F32)
    g1 = _din(nc, "g1", [128, KC], F32)
    wgu = _din(nc, "wgu", [FC * 128, KC * 256], F32)
    wdn = _din(nc, "wdn", [DFF, D], F32)
    gm = _din(nc, "gm", [128, KC], F32)
    win = _din(nc, "win", [NB1 * 128, KC * 512], F32)
    gain = _din(nc, "gain", [128, 3200], F32)
    ident = _din(nc, "ident", [128, 128], BF16)
    x1 = _dout(nc, "x1", [NT, D], F32)
    qn = _dout(nc, "qn", [NT, 2048], BF16)
    kv = _dout(nc, "kv", [NT, 2560], BF16)
    cg = _dout(nc, "cg", [NT, 36], F32)
    wgu_b = _dint(nc, "wgu_b", [FC * 128, KC * 256], BF16)
    wdn_b = _dint(nc, "wdn_b", [DFF, D], BF16)
    win_b = _dint(nc, "win_b", [NB1 * 128, KC * 512], BF16)
    cx = Ctx(nc)
    phase_cast(cx, [(wgu, wgu_b), (wdn, wdn_b), (win, win_b)])
    phase_ffn(cx, x, x1, g1, wgu_b, wdn_b, NT, ident)
    phase_proj(cx, x1, gm, win_b, gain, qn, kv, cg, NT, ident)
    cx.P.emit()
    return nc


def build_prog_b(NQT):
    NT = NQT * 128
    ncp = (2 * NT - 32) // 16 + 1
    NCP = (ncp + 127) // 128 * 128
    nc = bass.Bass("TRN2", target_bir_lowering=False)
    x1 = _din(nc, "x1", [NT, D], F32)
    qn = _din(nc, "qn", [NT, 2048], BF16)
    kvf = _din(nc, "kvf", [2 * NT, 2560], BF16)
    cg = _din(nc, "cg", [NT, 36], F32)
    sinks = _din(nc, "sinks", [128, 8], F32)
    w1 = _din(nc, "w1", [256, 4096], F32)
    w2 = _din(nc, "w2", [256, 128], F32)
    pos = _din(nc, "pos", [256, 16], F32)
    kgain = _din(nc, "kgain", [128, 64], F32)
    gm = _din(nc, "gm", [128, KC], F32)
    wgate = _din(nc, "wgate", [3 * KC * 128, KC * 128], F32)
    wbr = _din(nc, "wbr", [KC * 128, 1536], F32)
    wout = _din(nc, "wout", [D, D], F32)
    g2 = _din(nc, "g2", [128, KC], F32)
    wgu = _din(nc, "wgu", [FC * 128, KC * 256], F32)
    wdn = _din(nc, "wdn", [DFF, D], F32)
    C = {k: _din(nc, "c_" + k, shp, dt) for k, (shp, dt) in CONST_SPECS(NQT).items()}
    y = _dout(nc, "y", [NT, D], F32)
    w1_b = _dint(nc, "w1_b", [256, 4096], BF16)
    w2_b = _dint(nc, "w2_b", [256, 128], BF16)
    wgate_b = _dint(nc, "wgate_b", [3 * KC * 128, KC * 128], BF16)
    wbr_b = _dint(nc, "wbr_b", [KC * 128, 1536], BF16)
    wout_b = _dint(nc, "wout_b", [D, D], BF16)
    wgu_b = _dint(nc, "wgu_b", [FC * 128, KC * 256], BF16)
    wdn_b = _dint(nc, "wdn_b", [DFF, D], BF16)
    cmpk = _dint(nc, "cmpk", [NCP, 128], BF16)
    cmpv = _dint(nc, "cmpv", [NCP, 128], BF16)
    ot = _dint(nc, "ot", [12, 128, NT], BF16)
    x2 = _dint(nc, "x2", [NT, D], F32)
    cx = Ctx(nc)
    phase_cast(cx, [(w1, w1_b), (w2, w2_b), (wgate, wgate_b), (wbr, wbr_b), (wout, wout_b), (wgu, wgu_b), (wdn, wdn_b)])
    phase_cmp(cx, kvf, w1_b, w2_b, pos, kgain, C['ident'], cmpk, cmpv, NQT)
    phase_attn(cx, qn, kvf, cmpk, cmpv, cg, sinks, C, ot, NQT)
    phase_out(cx, x1, x2, gm, wgate_b, wbr_b, wout_b, ot, NT, C['ident'])
    phase_ffn(cx, x2, y, g2, wgu_b, wdn_b, NT, C['ident'])
    cx.P.emit()
    return nc


def _rep(v, n=128):
    return np.ascontiguousarray(np.broadcast_to(np.asarray(v, np.float32)[None, :], (n, v.shape[0])))


def kernel(x, ffn1_norm, ffn1_w_gu, ffn1_w_down, mix_norm, w_in, qk_gain, sinks, cmp_pos, cmp_w1, cmp_w2,
           w_branch_a, w_branch_b, w_branch_c, w_out, ffn2_norm, ffn2_w_gu, ffn2_w_down):
    f = lambda a: np.asarray(a, np.float32)
    x = f(x)
    NQT, NT = NQT_FULL, NT_FULL
    cores = list(range(NCORES))
    xs = [np.ascontiguousarray(x[c // 2].reshape(2 * NQT, 128, D)[c % 2::2].reshape(NT, D)) for c in cores]
    ident = np.eye(128, dtype=ml_dtypes.bfloat16)
    consts = [build_consts(hf, NQT) for hf in range(2)]
    nca = build_prog_a(NT)
    ncb = build_prog_b(NQT)
    depth = f(ffn1_norm).shape[0]
    for l in range(depth):
        wa = {"g1": lay_col(f(ffn1_norm)[l]), "wgu": lay_wgu(f(ffn1_w_gu)[l]), "wdn": np.ascontiguousarray(f(ffn1_w_down)[l]),
              "gm": lay_col(f(mix_norm)[l]), "win": lay_win_p1(f(w_in)[l]), "gain": lay_gain(f(qk_gain)[l]), "ident": ident}
        ra = run_bass_kernel_spmd(nca, [dict(wa, x=xs[c]) for c in cores], core_ids=cores).results
        wb = {"sinks": _rep(f(sinks)[l]), "w1": lay_cmp_w1(f(cmp_w1)[l]), "w2": lay_cmp_w2(f(cmp_w2)[l]), "pos": lay_cmp_pos(f(cmp_pos)[l]),
              "kgain": _rep(f(qk_gain)[l, 2, 1]), "gm": wa["gm"], "wgate": lay_wgate(f(w_in)[l]),
              "wbr": lay_wbr(f(w_branch_a)[l], f(w_branch_b)[l], f(w_branch_c)[l]), "wout": np.ascontiguousarray(f(w_out)[l]),
              "g2": lay_col(f(ffn2_norm)[l]), "wgu": lay_wgu(f(ffn2_w_gu)[l]), "wdn": np.ascontiguousarray(f(ffn2_w_down)[l])}
        ims = []
        for c in cores:
            p = c - c % 2
            kvf = np.concatenate([np.asarray(ra[p]["kv"]), np.asarray(ra[p + 1]["kv"])], axis=0)
            im = dict(wb, x1=np.asarray(ra[c]["x1"]), qn=np.asarray(ra[c]["qn"]), kvf=kvf, cg=np.asarray(ra[c]["cg"]))
            for k, v in consts[c % 2].items():
                im["c_" + k] = v
            ims.append(im)
        rb = run_bass_kernel_spmd(ncb, ims, core_ids=cores).results
        xs = [np.asarray(rb[c]["y"]) for c in cores]
    out = np.zeros((BATCH, SEQ, D), np.float32)
    for c in cores:
        out[c // 2].reshape(2 * NQT, 128, D)[c % 2::2] = xs[c].reshape(NQT, 128, D)
    return out
```

```python
import numpy as np
import ml_dtypes
import concourse.bass as bass
import concourse.mybir as mybir
from concourse.bass_utils import run_bass_kernel_spmd

F32 = mybir.dt.float32
BF16 = mybir.dt.bfloat16
AF = mybir.ActivationFunctionType
ALU = mybir.AluOpType
AX = mybir.AxisListType

D = 2048
DFF = 5632
KC = D // 128
FC = DFF // 128
RMS_EPS = 1e-6


class Prog:
    CE = ('pe', 'act', 'dve', 'pool')
    ENG = ('pe', 'act', 'dve', 'pool', 'sp')

    def __init__(self, nc, n_dma=16):
        self.nc = nc
        self.streams = {e: [] for e in self.ENG}
        self.cnt = {e: 0 for e in self.CE}
        self.esem = {e: nc.alloc_semaphore('sem_' + e) for e in self.CE}
        self.dsem = [nc.alloc_semaphore('sem_d%d' % i) for i in range(n_dma)]
        self.dcount = [0] * n_dma
        self.dnext = 0
        self.seen = {e: {} for e in self.ENG}
        self.lastw = {}
        self.readers = {}
        self.n_wait = 0

    def sem(self, key):
        return self.esem[key[1]] if key[0] == 'e' else self.dsem[key[1]]

    def _deps(self, reads, writes):
        deps = []
        for r in reads:
            t = self.lastw.get(r)
            if t:
                deps.append(t)
        for w in writes:
            t = self.lastw.get(w)
            if t:
                deps.append(t)
            deps.extend(self.readers.get(w, {}).items())
        return deps

    def _waits(self, eng, deps):
        need = {}
        for key, val in deps:
            if key == ('e', 'pe') and eng == 'pe':
                continue
            if self.seen[eng].get(key, 0) >= val:
                continue
            if need.get(key, 0) < val:
                need[key] = val
        for key, val in need.items():
            self.seen[eng][key] = val
        self.n_wait += len(need)
        return [(self.sem(key), val) for key, val in need.items()]

    def _commit(self, tok, reads, writes):
        for w in writes:
            self.lastw[w] = tok
            self.readers[w] = {}
        for r in reads:
            d = self.readers.setdefault(r, {})
            if d.get(tok[0], 0) < tok[1]:
                d[tok[0]] = tok[1]

    def op(self, eng, fn, reads=(), writes=()):
        waits = self._waits(eng, self._deps(reads, writes))
        self.cnt[eng] += 1
        tok = (('e', eng), self.cnt[eng])
        self.streams[eng].append((waits, fn, (self.esem[eng], 1)))
        self._commit(tok, reads, writes)

    def dma(self, out, in_, reads=(), writes=(), q='sp'):
        k = self.dnext
        self.dnext = (k + 1) % len(self.dsem)
        deps = self._deps(reads, writes)
        if self.dcount[k]:
            deps.append((('d', k), 16 * self.dcount[k]))
        waits = self._waits(q, deps)
        self.dcount[k] += 1
        tok = (('d', k), 16 * self.dcount[k])
        self.streams[q].append((waits, (lambda e, o=out, i=in_: e.dma_start(out=o, in_=i)), (self.dsem[k], 16)))
        self._commit(tok, reads, writes)

    def collective(self, kind, src, dst, groups, reads=(), writes=()):
        k = self.dnext
        self.dnext = (k + 1) % len(self.dsem)
        deps = self._deps(reads, writes)
        if self.dcount[k]:
            deps.append((('d', k), 16 * self.dcount[k]))
        waits = self._waits('pool', deps)
        self.dcount[k] += 1
        tok = (('d', k), 16 * self.dcount[k])
        self.streams['pool'].append((waits, (lambda e, a=src, b=dst: e.collective_compute(kind, ALU.bypass, replica_groups=groups, ins=[a], outs=[b])), (self.dsem[k], 16)))
        self._commit(tok, reads, writes)

    def barrier(self):
        allt = [(('e', e), self.cnt[e]) for e in self.CE if self.cnt[e]]
        allt += [(('d', k), 16 * c) for k, c in enumerate(self.dcount) if c]
        for e in self.ENG:
            waits = self._waits(e, allt)
            if waits:
                self.streams[e].append((waits, None, None))
        self.lastw.clear()
        self.readers.clear()

    def new_epoch(self):
        nc = self.nc
        self.epoch = getattr(self, 'epoch', 0) + 1
        self.esem = {e: nc.alloc_semaphore('sem%d_%s' % (self.epoch, e)) for e in self.CE}
        self.dsem = [nc.alloc_semaphore('sem%d_d%d' % (self.epoch, i)) for i in range(len(self.dsem))]
        self.cnt = {e: 0 for e in self.CE}
        self.dcount = [0] * len(self.dsem)
        self.dnext = 0
        self.seen = {e: {} for e in self.ENG}

    def emit(self):
        nc = self.nc

        def run(name, eng):
            for waits, fn, inc in self.streams[name]:
                for sem, val in waits:
                    eng.wait_ge(sem, val)
                if fn is not None:
                    ins = fn(eng)
                    ins.then_inc(inc[0], inc[1])

        with nc.Block() as block:
            @block.tensor
            def _(e):
                run('pe', e)

            @block.scalar
            def _(e):
                run('act', e)

            @block.vector
            def _(e):
                run('dve', e)

            @block.gpsimd
            def _(e):
                run('pool', e)

            @block.sync
            def _(e):
                run('sp', e)


class Arena:
    def __init__(self, nc, kbytes=204):
        self.words = kbytes * 1024 // 4
        self.t = nc.alloc_sbuf_tensor('arena', [128, self.words], F32)
        self.off = 0

    def reset(self):
        self.off = 0

    def alloc(self, cols, dtype=F32):
        nw = cols if dtype == F32 else (cols + 1) // 2
        nw = (nw + 7) // 8 * 8
        assert self.off + nw <= self.words, 'SBUF arena overflow: %d + %d > %d' % (self.off, nw, self.words)
        ap = self.t[:, self.off:self.off + nw]
        self.off += nw
        if dtype != F32:
            ap = ap.bitcast(dtype)[:, 0:cols]
        else:
            ap = ap[:, 0:cols]
        return ap


class Ctx:
    def __init__(self, nc):
        self.nc = nc
        self.P = Prog(nc)
        self.A = Arena(nc)
        self.psum = nc.alloc_psum_tensor('psum', [128, 8 * 512], F32)
        self.uid = 0

    def bank(self, b, nb=1):
        return self.psum[:, b * 512:(b + nb) * 512]

    def bank_bf(self, b, nb=1):
        return self.psum[:, b * 512:(b + nb) * 512].bitcast(BF16)

    def key(self, name):
        self.uid += 1
        return '%s#%d' % (name, self.uid)


def phase_cast(cx, pairs, cb=4096):
    P, A = cx.P, cx.A
    A.reset()
    nslot = 3
    stg = [A.alloc(cb, F32) for _ in range(nslot)]
    obf = [A.alloc(cb, BF16) for _ in range(nslot)]
    ks = [cx.key('cs') for _ in range(nslot)]
    ko = [cx.key('co') for _ in range(nslot)]
    engs = ('act', 'dve', 'pool')
    i = 0
    for src, dst in pairs:
        R, C = src.shape
        assert R % 128 == 0
        for r0 in range(0, R, 128):
            for c0 in range(0, C, cb):
                c1 = min(C, c0 + cb)
                n = c1 - c0
                s = i % nslot
                P.dma(stg[s][:, 0:n], src[r0:r0 + 128, c0:c1], writes=[ks[s]])
                e = engs[i % 3]
                if e == 'act':
                    P.op('act', lambda g, o=obf[s][:, 0:n], a=stg[s][:, 0:n]: g.copy(o, a), reads=[ks[s]], writes=[ko[s]])
                else:
                    P.op(e, lambda g, o=obf[s][:, 0:n], a=stg[s][:, 0:n]: g.tensor_copy(o, a), reads=[ks[s]], writes=[ko[s]])
                P.dma(dst[r0:r0 + 128, c0:c1], obf[s][:, 0:n], reads=[ko[s]])
                i += 1
    P.barrier()


def emit_norm_transpose(cx, x_src, t0, nsub, gcol, kg, xnT, kxn, xs, kxs, xb, kxb, junk, kjunk, st, kst, ident, tbank0):
    P = cx.P
    TT = nsub * 128
    xnT3 = xnT.rearrange('p (k t) -> p k t', k=KC)
    for s in range(nsub):
        sl = s % len(xs)
        P.dma(xs[sl], x_src[t0 + s * 128:t0 + (s + 1) * 128, :], writes=[kxs[sl]])
        P.op('act', lambda g, o=junk, a=xs[sl], acc=st[sl][:, 0:1]: g.activation(o, a, AF.Square, accum_out=acc),
             reads=[kxs[sl]], writes=[kjunk, kst[sl] + 'a'])
        P.op('dve', lambda g, o=st[sl][:, 1:2], a=st[sl][:, 0:1]: g.tensor_scalar(o, a, 1.0 / D, RMS_EPS, ALU.mult, ALU.add),
             reads=[kst[sl] + 'a'], writes=[kst[sl] + 'b'])
        P.op('act', lambda g, o=st[sl][:, 3:4], a=st[sl][:, 1:2]: g.sqrt(o, a),
             reads=[kst[sl] + 'b'], writes=[kst[sl] + 'd'])
        P.op('dve', lambda g, o=st[sl][:, 2:3], a=st[sl][:, 3:4]: g.reciprocal(o, a),
             reads=[kst[sl] + 'd'], writes=[kst[sl] + 'c'])
        P.op('pool', lambda g, o=xb[sl], a=xs[sl], r=st[sl][:, 2:3]: g.tensor_scalar(o, a, r, None, ALU.mult),
             reads=[kxs[sl], kst[sl] + 'c'], writes=[kxb[sl]])
        tb = tbank0 + 2 * (s % 2)
        kb = ('bank', tb), ('bank', tb + 1)
        pbf = cx.bank_bf(tb, 2)
        for kc in range(KC):
            b = kb[0] if kc < 8 else kb[1]
            P.op('pe', lambda g, o=pbf[:, kc * 128:(kc + 1) * 128], a=xb[sl][:, kc * 128:(kc + 1) * 128]: g.transpose(o, a, ident),
                 reads=[kxb[sl]], writes=[b])
        P.op('dve', lambda g, o=xnT3[:, :, s * 128:(s + 1) * 128], a=pbf.rearrange('p (k t) -> p k t', k=KC),
             gc=gcol.unsqueeze(2).to_broadcast([128, KC, 128]): g.tensor_tensor(o, a, gc, ALU.mult),
             reads=[kb[0], kb[1], kg], writes=[kxn])


def load_ident(cx, ident_dram):
    pass


def phase_ffn(cx, x_in, x_out, gcol_d, wgu, wdn, NT, ident_d, TT=1024):
    P, A = cx.P, cx.A
    A.reset()
    assert NT % TT == 0 and TT == 1024
    nsub = TT // 128
    ident = A.alloc(128, BF16)
    kid = cx.key('ident')
    gcol = A.alloc(KC, F32)
    kg = cx.key('gcol')
    P.dma(ident, ident_d[:, :], writes=[kid])
    P.dma(gcol, gcol_d[:, :], writes=[kg])
    xnT = A.alloc(KC * TT, BF16)
    kxn = cx.key('xnT')
    hT = A.alloc(FC * TT, BF16)
    khT = [cx.key('hT') for _ in range(FC)]
    xs = [A.alloc(D, F32) for _ in range(2)]
    kxs = [cx.key('xs') for _ in range(2)]
    xb = [A.alloc(D, BF16) for _ in range(2)]
    kxb = [cx.key('xb') for _ in range(2)]
    junk = A.alloc(D, BF16)
    kjunk = cx.key('junk')
    st = [A.alloc(8, F32) for _ in range(2)]
    kst = [cx.key('st') for _ in range(2)]
    NW = 2
    wg = [A.alloc(KC * 256, BF16) for _ in range(NW)]
    kwg = [cx.key('wg') for _ in range(NW)]
    ND = 3
    FG = 4
    wd = [A.alloc(FG * 512, BF16) for _ in range(ND)]
    kwd = [cx.key('wd') for _ in range(ND)]
    sg = [A.alloc(TT, F32) for _ in range(2)]
    ksg = [cx.key('sg') for _ in range(2)]
    xr = [A.alloc(512, F32) for _ in range(2)]
    kxr = [cx.key('xr') for _ in range(2)]
    xo = [A.alloc(512, F32) for _ in range(2)]
    kxo = [cx.key('xo') for _ in range(2)]
    iw = 0
    idn = 0
    io = 0
    for t0 in range(0, NT, TT):
        emit_norm_transpose(cx, x_in, t0, nsub, gcol, kg, xnT, kxn, xs, kxs, xb, kxb, junk, kjunk, st, kst, ident, 4)
        for fc in range(FC):
            s = iw % NW
            iw += 1
            P.dma(wg[s], wgu[fc * 128:(fc + 1) * 128, :], writes=[kwg[s]])
            gb = 4 * (fc % 2)
            for gu in range(2):
                for kc in range(KC):
                    for h in range(2):
                        b = gb + 2 * gu + h
                        P.op('pe', lambda g, o=cx.bank(b), w=wg[s][:, kc * 256 + gu * 128:kc * 256 + (gu + 1) * 128],
                             a=xnT[:, kc * TT + h * 512:kc * TT + (h + 1) * 512], k=kc: g.matmul(o, w, a, start=(k == 0), stop=(k == KC - 1)),
                             reads=[kwg[s], kxn], writes=[('bank', b)])
            q = fc % 2
            for h in range(2):
                P.op('act', lambda g, o=sg[q][:, h * 512:(h + 1) * 512], a=cx.bank(gb + h): g.activation(o, a, AF.Silu),
                     reads=[('bank', gb + h)], writes=[ksg[q] + str(h)])
                P.op('dve', lambda g, o=hT[:, fc * TT + h * 512:fc * TT + (h + 1) * 512], a=cx.bank(gb + 2 + h), b_=sg[q][:, h * 512:(h + 1) * 512]:
                     g.tensor_tensor(o, a, b_, ALU.mult), reads=[('bank', gb + 2 + h), ksg[q] + str(h)], writes=[khT[fc]])
        for cb in range(4):
            for f0 in range(0, FC, FG):
                s = idn % ND
                idn += 1
                P.dma(wd[s].rearrange('p (f c) -> p f c', f=FG),
                      wdn[f0 * 128:(f0 + FG) * 128, cb * 512:(cb + 1) * 512].rearrange('(f p) c -> p f c', p=128), writes=[kwd[s]])
                for fi in range(FG):
                    fc = f0 + fi
                    for ss in range(nsub):
                        P.op('pe', lambda g, o=cx.bank(ss), h_=hT[:, fc * TT + ss * 128:fc * TT + (ss + 1) * 128],
                             w=wd[s][:, fi * 512:(fi + 1) * 512], f=fc: g.matmul(o, h_, w, start=(f == 0), stop=(f == FC - 1)),
                             reads=[kwd[s], khT[fc]], writes=[('bank', ss)])
            for ss in range(nsub):
                tok = t0 + ss * 128
                q = io % 2
                io += 1
                P.dma(xr[q], x_in[tok:tok + 128, cb * 512:(cb + 1) * 512], writes=[kxr[q]])
                P.op('dve', lambda g, o=xo[q], a=cx.bank(ss), r=xr[q]: g.scalar_tensor_tensor(o, a, 0.5, r, ALU.mult, ALU.add),
                     reads=[('bank', ss), kxr[q]], writes=[kxo[q]])
                P.dma(x_out[tok:tok + 128, cb * 512:(cb + 1) * 512], xo[q], reads=[kxo[q]])
    P.barrier()


HD = 64
P1_BLOCKS = [('n', 512, 0, 0), ('n', 512, 0, 512), ('n', 512, 0, 1024), ('n', 512, 0, 1536),
             ('n', 512, 1, 0), ('n', 512, 1, 512), ('n', 128, 1, 1024),
             ('c', 512, 1, 1152), ('c', 512, 1, 1664), ('c', 384, 1, 2176),
             ('g', 36, 2, 0)]
NB1 = len(P1_BLOCKS)


def phase_proj(cx, x_in, gcol_d, win_b, gain_d, qn_d, kv_d, cg_d, NT, ident_d, TT=512):
    P, A = cx.P, cx.A
    A.reset()
    nsub = TT // 128
    ident = A.alloc(128, BF16)
    kid = cx.key('ident')
    gcol = A.alloc(KC, F32)
    kg = cx.key('gcol')
    gain = A.alloc(3200, F32)
    kgain = cx.key('gain')
    P.dma(ident, ident_d[:, :], writes=[kid])
    P.dma(gcol, gcol_d[:, :], writes=[kg])
    P.dma(gain, gain_d[:, :], writes=[kgain])
    P.op('dve', lambda g, o=gain[:, 0:2048]: g.tensor_scalar(o, o, 0.125, None, ALU.mult), reads=[kgain], writes=[kgain])
    xnT = A.alloc(KC * TT, BF16)
    kxn = cx.key('xnT')
    xs = [A.alloc(D, F32) for _ in range(2)]
    kxs = [cx.key('xs') for _ in range(2)]
    xb = [A.alloc(D, BF16) for _ in range(2)]
    kxb = [cx.key('xb') for _ in range(2)]
    junk = A.alloc(D, BF16)
    kjunk = cx.key('junk')
    st = [A.alloc(8, F32) for _ in range(2)]
    kst = [cx.key('st') for _ in range(2)]
    NW = 2
    wt = [A.alloc(KC * 512, BF16) for _ in range(NW)]
    kwt = [cx.key('wt') for _ in range(NW)]
    NE = 2
    sq = [A.alloc(512, F32) for _ in range(NE)]
    ksq = [cx.key('sq') for _ in range(NE)]
    tmp = [A.alloc(512, F32) for _ in range(NE)]
    ktmp = [cx.key('tmp') for _ in range(NE)]
    ss = [A.alloc(32, F32) for _ in range(NE)]
    kss = [cx.key('ss') for _ in range(NE)]
    ob = [A.alloc(512, BF16) for _ in range(NE)]
    kob = [cx.key('ob') for _ in range(NE)]
    og = [A.alloc(64, F32) for _ in range(NE)]
    kog = [cx.key('og') for _ in range(NE)]
    dests = (qn_d, kv_d, cg_d)
    iw = 0
    ie = 0
    for t0 in range(0, NT, TT):
        emit_norm_transpose(cx, x_in, t0, nsub, gcol, kg, xnT, kxn, xs, kxs, xb, kxb, junk, kjunk, st, kst, ident, 4)
        for bi, (kind, w, di, dc) in enumerate(P1_BLOCKS):
            sl = iw % NW
            iw += 1
            P.dma(wt[sl], win_b[bi * 128:(bi + 1) * 128, :], writes=[kwt[sl]])
            for s in range(nsub):
                b = (bi * nsub + s) % 4
                ps = cx.bank(b)[:, 0:w]
                for kc in range(KC):
                    P.op('pe', lambda g, o=ps, a=xnT[:, kc * TT + s * 128:kc * TT + (s + 1) * 128],
                         wv=wt[sl][:, kc * 512:kc * 512 + w], k=kc: g.matmul(o, a, wv, start=(k == 0), stop=(k == KC - 1)),
                         reads=[kxn, kwt[sl]], writes=[('bank', b)])
                e = ie % NE
                ie += 1
                tok = t0 + s * 128
                dst = dests[di][tok:tok + 128, dc:dc + w]
                if kind == 'n':
                    nh = w // HD
                    gofs = dc if di == 0 else 2048 + dc
                    P.op('act', lambda g, o=sq[e][:, 0:w], a=ps: g.activation(o, a, AF.Square), reads=[('bank', b)], writes=[ksq[e]])
                    P.op('dve', lambda g, o=ss[e][:, 0:nh], a=sq[e][:, 0:w].rearrange('p (h d) -> p h d', d=HD):
                         g.tensor_reduce(o, a, AX.X, ALU.add), reads=[ksq[e]], writes=[kss[e] + 'a'])
                    P.op('dve', lambda g, o=ss[e][:, 8:8 + nh], a=ss[e][:, 0:nh]: g.tensor_scalar(o, a, 1.0 / HD, RMS_EPS, ALU.mult, ALU.add),
                         reads=[kss[e] + 'a'], writes=[kss[e] + 'b'])
                    P.op('act', lambda g, o=ss[e][:, 16:16 + nh], a=ss[e][:, 8:8 + nh]: g.sqrt(o, a), reads=[kss[e] + 'b'], writes=[kss[e] + 'c'])
                    P.op('dve', lambda g, o=ss[e][:, 24:24 + nh], a=ss[e][:, 16:16 + nh]: g.reciprocal(o, a), reads=[kss[e] + 'c'], writes=[kss[e] + 'd'])
                    P.op('dve', lambda g, o=tmp[e][:, 0:w].rearrange('p (h d) -> p h d', d=HD), a=ps.rearrange('p (h d) -> p h d', d=HD),
                         r=ss[e][:, 24:24 + nh].unsqueeze(2).to_broadcast([128, nh, HD]): g.tensor_tensor(o, a, r, ALU.mult),
                         reads=[('bank', b), kss[e] + 'd'], writes=[ktmp[e]])
                    P.op('pool', lambda g, o=ob[e][:, 0:w], a=tmp[e][:, 0:w], gn=gain[:, gofs:gofs + w]: g.tensor_tensor(o, a, gn, ALU.mult),
                         reads=[ktmp[e], kgain], writes=[kob[e]])
                    P.dma(dst, ob[e][:, 0:w], reads=[kob[e]])
                elif kind == 'c':
                    P.op('act', lambda g, o=ob[e][:, 0:w], a=ps: g.copy(o, a), reads=[('bank', b)], writes=[kob[e]])
                    P.dma(dst, ob[e][:, 0:w], reads=[kob[e]])
                else:
                    P.op('act', lambda g, o=og[e][:, 0:w], a=ps: g.activation(o, a, AF.Sigmoid), reads=[('bank', b)], writes=[kog[e]])
                    P.dma(dst, og[e][:, 0:w], reads=[kog[e]])
    P.barrier()


def _r(a, b):
    return np.arange(a, b)


_Aq, _Ak, _Av = _r(0, 768), _r(768, 1536), _r(1536, 2304)
_Bq, _Bk, _Bv = _r(2304, 2816), _r(2816, 2944), _r(2944, 3072)
_Cq = _r(3072, 3840)
_Ckc, _Cvc, _Cks, _Cvs, _Ckw, _Cvw = [_r(3840 + 128 * i, 3968 + 128 * i) for i in range(6)]
_Cg = _r(4608, 4644)
GATE0 = 4644
P1_ORDER = np.concatenate([_Aq, _Bq, _Cq, _Ak, _Bk, _Cks, _Ckw, _Av, _Bv, _Ckc, _Cvc, _Cvs, _Cvw, _Cg])
P1_OFFS = [0, 512, 1024, 1536, 2048, 2560, 3072, 3200, 3712, 4224, 4608]
KV_AK, KV_BK, KV_CKS, KV_CKW, KV_AV, KV_BV, KV_CKC, KV_CVC, KV_CVS, KV_CVW = 0, 768, 896, 1024, 1152, 1920, 2048, 2176, 2304, 2432
QN_A, QN_B, QN_C = 0, 768, 1280


def lay_col(v):
    return np.ascontiguousarray(v.reshape(-1, 128).T)


def lay_wgu(w):
    return np.ascontiguousarray(w.reshape(KC, 128, 2, FC, 128).transpose(3, 1, 0, 2, 4)).reshape(FC * 128, KC * 256)


def lay_win_p1(w_in):
    out = np.zeros((NB1, 128, KC, 512), np.float32)
    for b, (kind, w, di, dc) in enumerate(P1_BLOCKS):
        cols = P1_ORDER[P1_OFFS[b]:P1_OFFS[b] + w]
        out[b, :, :, :w] = w_in[:, cols].reshape(KC, 128, w).transpose(1, 0, 2)
    return out.reshape(NB1 * 128, KC * 512)


def lay_gain(qk_gain):
    row = np.concatenate([np.tile(qk_gain[0, 0], 12), np.tile(qk_gain[1, 0], 8), np.tile(qk_gain[2, 0], 12),
                          np.tile(qk_gain[0, 1], 12), np.tile(qk_gain[1, 1], 2), np.tile(qk_gain[2, 1], 4)]).astype(np.float32)
    return np.ascontiguousarray(np.broadcast_to(row[None, :], (128, 3200)))


NEG = -30000.0
BIGF = 1.0e30
N_QH = 32
MB_B, MB_A0, MB_A1, MB_A2, MB_CW, MB_SL = 0, 3, 6, 12, 30, 36
NMASK = 38


def _bf(x):
    return np.asarray(x, np.float32).astype(ml_dtypes.bfloat16)


def _slopes(n):
    return (2.0 ** (-8.0 * np.arange(1, n + 1, dtype=np.float32) / n)).astype(np.float32)


def head_slopes():
    return np.concatenate([_slopes(12), _slopes(8), _slopes(12)]).astype(np.float64)


def build_consts(hf, NQT):
    S = 2 * NQT * 128
    NT = NQT * 128
    sl = head_slopes()
    s_hi = _bf(sl).astype(np.float64)
    s_lo = _bf(sl - s_hi).astype(np.float64)
    slp = s_hi + s_lo
    i = np.arange(NT)
    tq = (2 * (i // 128) + hf) * 128 + (i % 128)
    qpos = np.zeros((7, N_QH, NT), np.float64)
    qpos[0] = (128 * s_hi)[:, None]
    qpos[1] = (128 * s_lo)[:, None]
    qpos[2] = s_hi[:, None]
    qpos[3] = s_lo[:, None]
    v = -slp[:, None] * tq[None, :]
    v1 = _bf(v).astype(np.float64)
    v2 = _bf(v - v1).astype(np.float64)
    v3 = _bf(v - v1 - v2).astype(np.float64)
    qpos[4], qpos[5], qpos[6] = v1, v2, v3

    def kp(pos):
        t = np.zeros((7, len(pos)), np.float64)
        t[0] = t[1] = pos // 128
        t[2] = t[3] = pos % 128
        t[4:7] = 1.0
        return _bf(t)
    kpos = kp(np.arange(S))
    ncp = (S - 32) // 16 + 1
    NCT = (ncp + 127) // 128
    kposc = kp(16 * np.arange(NCT * 128) + 31)
    ki = np.arange(128)[:, None]
    qi = np.arange(128)[None, :]

    def wmask(delta, lo, hi, dil):
        dist = 128 * delta + qi - ki
        ok = (dist >= lo) & (dist <= hi) & (dist % dil == 0)
        return np.where(ok, 0.0, NEG)
    masks = np.zeros((NMASK, 128, 128), np.float32)
    for base, nrel, lo, hi, dil in ((MB_B, 3, 0, 127, 1), (MB_A0, 3, 0, 128, 1), (MB_A1, 6, 0, 512, 4),
                                    (MB_A2, 18, 0, 2048, 16), (MB_CW, 6, 0, 511, 1), (MB_SL, 2, 0, 1 << 30, 1)):
        for r in range(nrel):
            masks[base + r] = wmask(hf - 1 + r, lo, hi, dil)
    masks = np.ascontiguousarray(masks.transpose(1, 0, 2)).reshape(128, NMASK * 128)
    cmpmask = np.zeros((NQT, 128, 2, 128), np.float32)
    force = np.zeros((NQT, 128, 2, 128), np.float32)
    for it in range(NQT):
        Tq = 2 * it + hf
        t = Tq * 128 + np.arange(128)
        for rc in range(2):
            c = Tq // 16 - rc
            n = c * 128 + np.arange(128)
            ok = (16 * n[:, None] + 31) <= t[None, :]
            cmpmask[it, :, rc, :] = np.where(ok, 0.0, NEG)
        cur = t // 64
        j = np.arange(128)[None, :]
        forced = (j == 0) | (j == cur[:, None]) | (j == cur[:, None] - 1)
        force[it, :, 0, :] = np.where(forced, BIGF, -BIGF)
        force[it, :, 1, :] = np.where(j <= cur[:, None], BIGF, -BIGF)
    n = np.arange(NCT * 128)
    j = np.arange(128)
    ov = ((16 * n[:, None] <= 64 * j[None, :] + 63) & (16 * n[:, None] + 31 >= 64 * j[None, :]) & (n[:, None] < ncp))
    ov = np.ascontiguousarray(ov.reshape(NCT, 128, 128).transpose(1, 0, 2)).reshape(128, NCT * 128)
    tb = np.broadcast_to((1e-30 * (128 - np.arange(128)))[None, :], (128, 128))
    pp = np.arange(128)[:, None, None]
    t16 = np.arange(32)[None, :, None]
    kk = np.arange(128)[None, None, :]
    e32 = ((pp % 64) == 2 * t16 + (kk >= 64)).astype(np.float32).reshape(128, 4096)
    return {
        'e32': _bf(e32),
        'qpos': _bf(qpos), 'kpos': kpos, 'kposc': kposc, 'masks': _bf(masks),
        'cmpmask': _bf(cmpmask.reshape(NQT, 128, 256)), 'force': force.reshape(NQT, 128, 256).astype(np.float32),
        'ov': _bf(ov), 'tb': np.ascontiguousarray(tb).astype(np.float32),
        'ident': np.eye(128, dtype=ml_dtypes.bfloat16),
    }


CONST_SPECS = lambda NQT: {
    'qpos': ([7, N_QH, NQT * 128], BF16), 'kpos': ([7, 2 * NQT * 128], BF16),
    'kposc': ([7, ((2 * NQT * 128 - 32) // 16 + 1 + 127) // 128 * 128], BF16),
    'masks': ([128, NMASK * 128], BF16), 'cmpmask': ([NQT, 128, 256], BF16), 'force': ([NQT, 128, 256], F32),
    'ov': ([128, ((2 * NQT * 128 - 32) // 16 + 1 + 127) // 128 * 128], BF16), 'tb': ([128, 128], F32),
    'ident': ([128, 128], BF16), 'e32': ([128, 4096], BF16),
}


def kv_rows(kvf_d, T, NT, r0=0, r1=128, c0=0, c1=2560):
    base = (T % 2) * NT + (T // 2) * 128
    return kvf_d[base + r0:base + r1, c0:c1]


def phase_cmp(cx, kvf_d, w1_b, w2_b, pos_d, kgain_d, ident_d, cmpk_d, cmpv_d, NQT):
    P, A = cx.P, cx.A
    A.reset()
    NT = NQT * 128
    NTL = 2 * NQT
    S = NTL * 128
    ncp = (S - 32) // 16 + 1
    NCT = (ncp + 127) // 128
    NCP = NCT * 128
    ident = A.alloc(128, BF16)
    kid = cx.key('ident')
    P.dma(ident, ident_d[:, :], writes=[kid])
    kgain = A.alloc(64, F32)
    kkg = cx.key('kgain')
    P.dma(kgain, kgain_d[:, :], writes=[kkg])
    TT2 = A.alloc(S, BF16)
    ktt = cx.key('tt2')
    stage = [A.alloc(128, BF16) for _ in range(4)]
    kstage = [cx.key('stage') for _ in range(4)]
    for s_ in range(4):
        P.op('pool', lambda g, o=stage[s_]: g.memset(o, 0.0), writes=[kstage[s_]])
    w1 = A.alloc(16 * 256, BF16)
    kw1 = cx.key('w1')
    w2 = A.alloc(128, BF16)
    kw2 = cx.key('w2')
    posf = A.alloc(16, F32)
    posb = A.alloc(16, BF16)
    kpos = cx.key('pos')
    bvec = A.alloc(2, F32)
    kbv = cx.key('bvec')
    HT = A.alloc(2 * NCP, BF16)
    kht = cx.key('HT')
    P.op('pool', lambda g, o=HT: g.memset(o, 0.0), writes=[kht])
    sq = A.alloc(64, F32)
    st = A.alloc(8, F32)
    tmp = A.alloc(64, F32)
    ob = [A.alloc(64, BF16) for _ in range(2)]
    kob = [cx.key('ob') for _ in range(2)]
    kfin = cx.key('fin')
    io = 0
    ist = 0
    for kind in range(2):
        col0 = KV_CKC if kind == 0 else KV_CVC
        P.dma(w1, w1_b[kind * 128:(kind + 1) * 128, :], writes=[kw1])
        P.dma(w2, w2_b[kind * 128:(kind + 1) * 128, :], writes=[kw2])
        P.dma(posf, pos_d[kind * 128:(kind + 1) * 128, :], writes=[kpos + 'f'])
        P.op('dve', lambda g, o=posb, a=posf: g.tensor_copy(o, a), reads=[kpos + 'f'], writes=[kpos])
        for hc in range(2):
            for c in range(16):
                P.op('pe', lambda g, o=cx.bank(2)[:, hc:hc + 1], w=w1[:, c * 256 + hc * 128:c * 256 + (hc + 1) * 128], a=posb[:, c:c + 1], k=c, h=hc:
                     g.matmul(o, w, a, start=(k == 0 and h == 0), stop=(k == 15), skip_group_check=True),
                     reads=[kw1, kpos], writes=[('bank', 2)])
        P.op('dve', lambda g, o=bvec, a=cx.bank(2)[:, 0:2]: g.tensor_copy(o, a), reads=[('bank', 2)], writes=[kbv])
        for gi in range(2):
            col = col0 + 64 * gi
            for T in range(NTL):
                sg_ = ist % 4
                ist += 1
                P.dma(stage[sg_][:, 0:64], kv_rows(kvf_d, T, NT, 0, 128, col, col + 64), writes=[kstage[sg_]])
                P.dma(stage[sg_][0:127, 64:128], kv_rows(kvf_d, T, NT, 1, 128, col, col + 64), writes=[kstage[sg_] + 'b'])
                rd = [kstage[sg_], kstage[sg_] + 'b']
                if T + 1 < NTL:
                    P.dma(stage[sg_][127:128, 64:128], kv_rows(kvf_d, T + 1, NT, 0, 1, col, col + 64), writes=[kstage[sg_] + 'c'])
                    rd.append(kstage[sg_] + 'c')
                bnk = T % 2
                P.op('pe', lambda g, o=cx.bank_bf(bnk)[:, 0:128], a=stage[sg_]: g.transpose(o, a, ident), reads=rd + [kid], writes=[('bank', bnk)])
                eng = 'act' if T % 2 == 0 else 'dve'
                if eng == 'act':
                    P.op('act', lambda g, o=TT2[:, T * 128:(T + 1) * 128], a=cx.bank_bf(bnk)[:, 0:128]: g.copy(o, a),
                         reads=[('bank', bnk)], writes=[ktt, kstage[sg_], kstage[sg_] + 'b', kstage[sg_] + 'c'])
                else:
                    P.op('dve', lambda g, o=TT2[:, T * 128:(T + 1) * 128], a=cx.bank_bf(bnk)[:, 0:128]: g.tensor_copy(o, a),
                         reads=[('bank', bnk)], writes=[ktt, kstage[sg_], kstage[sg_] + 'b', kstage[sg_] + 'c'])
            TT3 = TT2.rearrange('p (n s) -> p n s', s=16)
            for hc in range(2):
                for n0 in range(0, ncp, 512):
                    n1 = min(ncp, n0 + 512)
                    bnk = 3 + hc
                    for c in range(16):
                        P.op('pe', lambda g, o=cx.bank(bnk)[:, 0:n1 - n0], w=w1[:, c * 256 + hc * 128:c * 256 + (hc + 1) * 128],
                             a=TT3[:, n0 + (2 * c) // 16:n1 + (2 * c) // 16, (2 * c) % 16], k=c: g.matmul(o, w, a, start=(k == 0), stop=(k == 15)),
                             reads=[kw1, ktt], writes=[('bank', bnk)])
                    P.op('act', lambda g, o=HT[:, hc * NCP + n0:hc * NCP + n1], a=cx.bank(bnk)[:, 0:n1 - n0], b=bvec[:, hc:hc + 1]:
                         g.activation(o, a, AF.Silu, bias=b), reads=[('bank', bnk), kbv], writes=[kht])
            for j in range(NCT):
                bnk = 5 + j % 2
                for hc in range(2):
                    P.op('pe', lambda g, o=cx.bank(bnk)[:, 0:64], a=HT[:, hc * NCP + j * 128:hc * NCP + (j + 1) * 128], w=w2[:, hc * 64:(hc + 1) * 64], h=hc:
                         g.matmul(o, a, w, start=(h == 0), stop=(h == 1)), reads=[kht, kw2], writes=[('bank', bnk)])
                o_ = io % 2
                io += 1
                ps = cx.bank(bnk)[:, 0:64]
                if kind == 0:
                    P.op('act', lambda g, o=sq, a=ps, acc=st[:, 0:1]: g.activation(o, a, AF.Square, accum_out=acc), reads=[('bank', bnk)], writes=[kfin + 'a'])
                    P.op('dve', lambda g, o=st[:, 1:2], a=st[:, 0:1]: g.tensor_scalar(o, a, 1.0 / HD, RMS_EPS, ALU.mult, ALU.add), reads=[kfin + 'a'], writes=[kfin + 'b'])
                    P.op('act', lambda g, o=st[:, 2:3], a=st[:, 1:2]: g.sqrt(o, a), reads=[kfin + 'b'], writes=[kfin + 'c'])
                    P.op('dve', lambda g, o=st[:, 3:4], a=st[:, 2:3]: g.reciprocal(o, a), reads=[kfin + 'c'], writes=[kfin + 'd'])
                    P.op('dve', lambda g, o=tmp, a=ps, r=st[:, 3:4]: g.tensor_scalar(o, a, r, None, ALU.mult), reads=[('bank', bnk), kfin + 'd'], writes=[kfin + 'e'])
                    P.op('dve', lambda g, o=ob[o_], a=tmp, gn=kgain: g.tensor_tensor(o, a, gn, ALU.mult), reads=[kfin + 'e', kkg], writes=[kob[o_]])
                    P.dma(cmpk_d[j * 128:(j + 1) * 128, gi * 64:(gi + 1) * 64], ob[o_], reads=[kob[o_]])
                else:
                    P.op('act', lambda g, o=ob[o_], a=ps: g.copy(o, a), reads=[('bank', bnk)], writes=[kob[o_]])
                    P.dma(cmpv_d[j * 128:(j + 1) * 128, gi * 64:(gi + 1) * 64], ob[o_], reads=[kob[o_]])
    P.barrier()


def phase_attn(cx, qn_d, kvf_d, cmpk_d, cmpv_d, cg_d, sinks_d, C, ot_d, NQT):
    P, A = cx.P, cx.A
    A.reset()
    NT = NQT * 128
    NTL = 2 * NQT
    S = NTL * 128
    ncp = (S - 32) // 16 + 1
    NCT = (ncp + 127) // 128
    VW = 66
    GROUPS = [('A0', 4, KV_AK, KV_AV, 3, 5, 0, 1, MB_A0), ('A1', 4, KV_AK + 256, KV_AV + 256, 6, 8, 4, 1, MB_A1),
              ('A2', 4, KV_AK + 512, KV_AV + 512, 18, 20, 8, 1, MB_A2), ('B', 2, KV_BK, KV_BV, 3, 5, 12, 4, MB_B),
              ('CW', 2, KV_CKW, KV_CVW, 6, 8, 20, 6, MB_CW), ('SL', 2, KV_CKS, KV_CVS, NTL, NTL, 20, 6, MB_SL)]
    GI = {g[0]: g for g in GROUPS}

    def load_const(name, cols, dtype, src):
        t = A.alloc(cols, dtype)
        k = cx.key(name)
        P.dma(t, src, writes=[k])
        return t, k
    ident, kid = load_const('ident', 128, BF16, C['ident'][:, :])
    masks, kmk = load_const('masks', NMASK * 128, BF16, C['masks'][:, :])
    ov, kov = load_const('ov', NCT * 128, BF16, C['ov'][:, :])
    tb, ktb = load_const('tb', 128, F32, C['tb'][:, :])
    e32, ke32 = load_const('e32', 4096, BF16, C['e32'][:, :])
    esink, kes = load_const('esink', 8, F32, sinks_d[:, :])
    P.op('act', lambda g, o=esink: g.activation(o, o, AF.Exp), reads=[kes], writes=[kes])
    ident3 = lambda R: ident.unsqueeze(1).to_broadcast([128, R, 128])
    kt, vc, kK, kV = {}, {}, {}, {}
    for (name, ns, kcol, vcol, nrel, ring, qh0, R, mb) in GROUPS + [('CMP', 2, 0, 0, NCT, NCT, 20, 6, 0)]:
        kt[name] = A.alloc(ring * ns * 128, BF16)
        vc[name] = A.alloc(ring * ns * VW, BF16)
        kK[name] = [cx.key('K' + name) for _ in range(ring)]
        kV[name] = [cx.key('V' + name) for _ in range(ring)]
        P.op('pool', lambda g, o=vc[name]: g.memset(o, 1.0), writes=kV[name])

    def Kap(name, pos, s):
        ns = 2 if name == 'CMP' else GI[name][1]
        return kt[name][0:71, (pos * ns + s) * 128:(pos * ns + s + 1) * 128]

    def Vap(name, pos, s):
        ns = 2 if name == 'CMP' else GI[name][1]
        return vc[name][:, (pos * ns + s) * VW:(pos * ns + s) * VW + 65]
    QaT = [A.alloc(N_QH * 128, BF16) for _ in range(2)]
    kQ = [cx.key('QaT') for _ in range(2)]
    qst = [A.alloc(2048, BF16) for _ in range(2)]
    kqst = [cx.key('qst') for _ in range(2)]
    kvst = [A.alloc(2560, BF16) for _ in range(2)]
    kkvst = [cx.key('kvst') for _ in range(2)]
    NPT = 4
    PT = [A.alloc(512, BF16) for _ in range(NPT)]
    kPT = [cx.key('PT') for _ in range(NPT)]
    cgt = [A.alloc(36, F32) for _ in range(2)]
    kcg = [cx.key('cg') for _ in range(2)]
    frc = [A.alloc(256, F32) for _ in range(2)]
    kfrc = [cx.key('frc') for _ in range(2)]
    cmk = [A.alloc(256, BF16) for _ in range(2)]
    kcmk = [cx.key('cmk') for _ in range(2)]
    oall = [A.alloc(1536, BF16) for _ in range(2)]
    koall = [cx.key('oall') for _ in range(2)]
    oTs = A.alloc(1536, BF16)
    koTs = cx.key('oTs')
    oc = A.alloc(768, F32)
    koc = [cx.key('oc') for _ in range(4)]
    tmp3 = [A.alloc(192, F32) for _ in range(2)]
    ktmp3 = [cx.key('tmp3') for _ in range(2)]
    sm = [A.alloc(16, F32) for _ in range(4)]
    ksm = [cx.key('sm') for _ in range(4)]
    imp = [A.alloc(128, F32) for _ in range(2)]
    kimp = [cx.key('imp') for _ in range(2)]
    wk = [A.alloc(128, F32) for _ in range(3)]
    kwk = cx.key('wk')
    m8 = A.alloc(16, F32)
    selm = [A.alloc(128, BF16) for _ in range(2)]
    kselm = [cx.key('selm') for _ in range(2)]
    selmT = [A.alloc(128, BF16) for _ in range(2)]
    kselmT = [cx.key('selmT') for _ in range(2)]
    ism = [0]
    itp = [0]
    tpbank = lambda: (itp.__setitem__(0, itp[0] + 1), (itp[0] % 2))[1]

    cst = [A.alloc(128, BF16) for _ in range(2)]
    kcst = [cx.key('cst') for _ in range(2)]
    for c in range(NCT):
        s_ = c % 2
        P.dma(cst[s_], cmpk_d[c * 128:(c + 1) * 128, :], writes=[kcst[s_]])
        b = tpbank()
        for g_ in range(2):
            P.op('pe', lambda g, o=cx.bank_bf(b)[0:64, g_ * 128:(g_ + 1) * 128], a=cst[s_][:, g_ * 64:(g_ + 1) * 64]: g.transpose(o, a, ident),
                 reads=[kcst[s_], kid], writes=[('bank', b)])
        P.op('dve', lambda g, o=kt['CMP'][0:64, c * 256:(c + 1) * 256], a=cx.bank_bf(b)[0:64, 0:256]: g.tensor_copy(o, a),
             reads=[('bank', b)], writes=[kK['CMP'][c]])
        P.dma(kt['CMP'][64:71, c * 256:(c + 1) * 256].rearrange('p (s t) -> p s t', s=2),
              C['kposc'][:, c * 128:(c + 1) * 128].unsqueeze(1).to_broadcast([7, 2, 128]), writes=[kK['CMP'][c] + 'p'])
        P.dma(vc['CMP'][:, c * 2 * VW:(c + 1) * 2 * VW].rearrange('p (s w) -> p s w', w=VW)[:, :, 0:64],
              cmpv_d[c * 128:(c + 1) * 128, :].rearrange('p (s d) -> p s d', d=64), reads=[kV['CMP'][c]], writes=[kV['CMP'][c] + 'd'])

    def prep(i):
        for T in (2 * i, 2 * i + 1):
            s_ = T % 2
            P.dma(kvst[s_], kv_rows(kvf_d, T, NT), writes=[kkvst[s_]])
            for (name, ns, kcol, vcol, nrel, ring, qh0, R, mb) in GROUPS:
                pos = T % ring
                b = tpbank()
                for s2 in range(ns):
                    P.op('pe', lambda g, o=cx.bank_bf(b)[0:64, s2 * 128:(s2 + 1) * 128], a=kvst[s_][:, kcol + s2 * 64:kcol + (s2 + 1) * 64]:
                         g.transpose(o, a, ident), reads=[kkvst[s_], kid], writes=[('bank', b)])
                dst = kt[name][0:64, pos * ns * 128:(pos + 1) * ns * 128]
                if b == 0:
                    P.op('act', lambda g, o=dst, a=cx.bank_bf(b)[0:64, 0:ns * 128]: g.copy(o, a), reads=[('bank', b)], writes=[kK[name][pos]])
                else:
                    P.op('dve', lambda g, o=dst, a=cx.bank_bf(b)[0:64, 0:ns * 128]: g.tensor_copy(o, a), reads=[('bank', b)], writes=[kK[name][pos]])
                P.dma(kt[name][64:71, pos * ns * 128:(pos + 1) * ns * 128].rearrange('p (s t) -> p s t', s=ns),
                      C['kpos'][:, T * 128:(T + 1) * 128].unsqueeze(1).to_broadcast([7, ns, 128]),
                      reads=[kK[name][pos]], writes=[kK[name][pos] + 'p'])
                P.op('pool', lambda g, o=vc[name][:, pos * ns * VW:(pos + 1) * ns * VW].rearrange('p (s w) -> p s w', w=VW)[:, :, 0:64],
                     a=kvst[s_][:, vcol:vcol + ns * 64].rearrange('p (s d) -> p s d', d=64): g.tensor_copy(o, a),
                     reads=[kkvst[s_], kV[name][pos]], writes=[kV[name][pos] + 'd'])
        qb = i % 2
        P.dma(qst[qb], qn_d[i * 128:(i + 1) * 128, :], writes=[kqst[qb]])
        for h0 in range(0, N_QH, 8):
            b = tpbank()
            for h in range(8):
                P.op('pe', lambda g, o=cx.bank_bf(b)[0:64, h * 128:(h + 1) * 128], a=qst[qb][:, (h0 + h) * 64:(h0 + h + 1) * 64]:
                     g.transpose(o, a, ident), reads=[kqst[qb], kid], writes=[('bank', b)])
            if b == 0:
                P.op('act', lambda g, o=QaT[qb][0:64, h0 * 128:(h0 + 8) * 128], a=cx.bank_bf(b)[0:64, :]: g.copy(o, a), reads=[('bank', b)], writes=[kQ[qb]])
            else:
                P.op('dve', lambda g, o=QaT[qb][0:64, h0 * 128:(h0 + 8) * 128], a=cx.bank_bf(b)[0:64, :]: g.tensor_copy(o, a), reads=[('bank', b)], writes=[kQ[qb]])
        P.dma(QaT[qb][64:71, :].rearrange('p (h t) -> p h t', h=N_QH), C['qpos'][:, :, i * 128:(i + 1) * 128], reads=[kQ[qb]], writes=[kQ[qb] + 'p'])
        P.dma(cgt[qb], cg_d[i * 128:(i + 1) * 128, :], writes=[kcg[qb]])
        P.dma(frc[qb], C['force'][i], writes=[kfrc[qb]])
        P.dma(cmk[qb], C['cmpmask'][i], writes=[kcmk[qb]])

    def kdeps(name, pos):
        return [kK[name][pos], kK[name][pos] + 'p']

    def vdeps(name, pos):
        return [kV[name][pos], kV[name][pos] + 'd']

    def run_tile(i):
        qb = i % 2
        Tq_c = i // 8
        cgv = cgt[qb].rearrange('p (h b) -> p h b', b=3)
        steps = []
        rounds = []

        def add_round(kind, units, **kw):
            rd = dict(kind=kind, nsteps=0, **kw)
            rd['acc'] = 6 + len(rounds) % 2
            rounds.append(rd)
            for u in units:
                for tl in u['tiles']:
                    steps.append((rd, u, tl))
                    rd['nsteps'] += 1
        for g_ in range(2):
            for half in range(2):
                tiles = []
                for c in range(0, Tq_c + 1):
                    m = []
                    if c == Tq_c:
                        m = [('w', cmk[qb][:, 0:128], [kcmk[qb]])]
                    elif c == Tq_c - 1:
                        m = [('w', cmk[qb][:, 128:256], [kcmk[qb]])]
                    tiles.append(dict(K=Kap('CMP', c, g_), V=Vap('CMP', c, g_), kd=kdeps('CMP', c), vd=vdeps('CMP', c), masks=m, ov=c))
                add_round('CMP', [dict(q0=20 + 6 * g_ + 3 * half, R=3, slot=0, tiles=tiles)], g=g_, half=half, br=0)
        units = []
        for gn in ('A0', 'A1', 'A2'):
            (name, ns, kcol, vcol, nrel, ring, qh0, R, mb) = GI[gn]
            for h in range(4):
                tiles = []
                for r in range(nrel):
                    T = 2 * i + 1 - r
                    if T < 0:
                        continue
                    pos = T % ring
                    tiles.append(dict(K=Kap(name, pos, h), V=Vap(name, pos, h), kd=kdeps(name, pos), vd=vdeps(name, pos),
                                      masks=[('w', masks[:, (mb + r) * 128:(mb + r + 1) * 128], [kmk])]))
                units.append(dict(q0=qh0 + h, R=1, slot=h, tiles=tiles))
        add_round('A', units)
        for j in range(2):
            (name, ns, kcol, vcol, nrel, ring, qh0, R, mb) = GI['B']
            tiles = []
            for r in range(nrel):
                T = 2 * i + 1 - r
                if T < 0:
                    continue
                pos = T % ring
                tiles.append(dict(K=Kap(name, pos, j), V=Vap(name, pos, j), kd=kdeps(name, pos), vd=vdeps(name, pos),
                                  masks=[('w', masks[:, (mb + r) * 128:(mb + r + 1) * 128], [kmk])]))
            add_round('B', [dict(q0=qh0 + 4 * j, R=4, slot=0, tiles=tiles)], j=j)
        for gname, br in (('CW', 2), ('SL', 1)):
            (name, ns, kcol, vcol, nrel, ring, qh0, R, mb) = GI[gname]
            for g_ in range(2):
                for half in range(2):
                    tiles = []
                    rels = range(nrel) if gname == 'CW' else range(2 * i + 1, -1, -1)
                    for r in rels:
                        T = 2 * i + 1 - r
                        if T < 0:
                            continue
                        pos = T % ring
                        m = []
                        if gname == 'CW':
                            m = [('w', masks[:, (mb + r) * 128:(mb + r + 1) * 128], [kmk])]
                        else:
                            m = [('s', (T, selmT[g_]), [kselmT[g_]])]
                            if r < 2:
                                m.append(('w', masks[:, (mb + r) * 128:(mb + r + 1) * 128], [kmk]))
                        tiles.append(dict(K=Kap(name, pos, g_), V=Vap(name, pos, g_), kd=kdeps(name, pos), vd=vdeps(name, pos), masks=m))
                    add_round(gname, [dict(q0=qh0 + 6 * g_ + 3 * half, R=3, slot=0, tiles=tiles)], g=g_, half=half, br=br)

        def emit_qk(k):
            rd, u, tl = steps[k]
            R = u['R']
            sb = 2 + k % 3
            Sv = cx.bank(sb)[:, 0:R * 128]
            S3 = Sv.rearrange('p (r q) -> p r q', r=R)
            nm = len(tl['masks'])
            P.op('pe', lambda g, o=Sv, kk=tl['K'], q=QaT[qb][0:71, u['q0'] * 128:(u['q0'] + R) * 128], last=(nm == 0):
                 g.matmul(o, kk, q, start=True, stop=last), reads=tl['kd'] + [kQ[qb], kQ[qb] + 'p'], writes=[('bank', sb)])
            for mi, (mk, ap, deps) in enumerate(tl['masks']):
                last = (mi == nm - 1)
                if mk == 'w':
                    P.op('pe', lambda g, o=S3, a=ap.unsqueeze(1).to_broadcast([128, R, 128]), last=last:
                         g.matmul(o, ident, a, start=False, stop=last), reads=deps + [kid], writes=[('bank', sb)])
                else:
                    T_, sT = ap
                    q4, t16 = T_ // 32, T_ % 32
                    P.op('pe', lambda g, o=S3, w=e32[q4 * 64:(q4 + 1) * 64, t16 * 128:(t16 + 1) * 128],
                         a=sT[q4 * 64:(q4 + 1) * 64, :].unsqueeze(1).to_broadcast([64, R, 128]), last=last:
                         g.matmul(o, w, a, start=False, stop=last), reads=deps + [ke32], writes=[('bank', sb)])
            pt = k % NPT
            P.op('act', lambda g, o=PT[pt][:, 0:R * 128], a=Sv: g.activation(o, a, AF.Exp), reads=[('bank', sb)], writes=[kPT[pt]])

        def emit_pv(k):
            rd, u, tl = steps[k]
            R = u['R']
            pt = k % NPT
            ab = rd['acc']
            for r in range(R):
                first = not rd.get('acc_started', False)
                rd['acc_started'] = True
                sl_ = u['slot'] + r
                P.op('pe', lambda g, o=cx.bank(ab)[:, sl_ * 65:(sl_ + 1) * 65], p=PT[pt][:, r * 128:(r + 1) * 128], v=tl['V'], first=first:
                     g.matmul(o, p, v, start=first, stop=True, skip_group_check=True), reads=[kPT[pt]] + tl['vd'], writes=[('bank', ab)])
            if rd['kind'] == 'CMP':
                for r in range(R):
                    first = not rd.get('u_started', False)
                    rd['u_started'] = True
                    P.op('pe', lambda g, o=cx.bank(5)[:, r * 128:(r + 1) * 128], p=PT[pt][:, r * 128:(r + 1) * 128],
                         w=ov[:, tl['ov'] * 128:(tl['ov'] + 1) * 128], first=first:
                         g.matmul(o, p, w, start=first, stop=True, skip_group_check=True), reads=[kPT[pt], kov], writes=[('bank', 5)])
            rd['nsteps'] -= 1
            if rd['nsteps'] == 0:
                finalize(rd)

        def finalize(rd):
            ab = rd['acc']
            kind = rd['kind']
            ob_ = oall[qb]
            s_ = ism[0] % 4
            ism[0] += 1
            smt, ks_ = sm[s_], ksm[s_]
            if kind in ('A', 'B'):
                accv = cx.bank(ab)[:, 0:260].rearrange('p (h c) -> p h c', c=65)
                if kind == 'A':
                    P.op('dve', lambda g, o=smt[:, 0:4], a=accv[:, :, 64]: g.reciprocal(o, a), reads=[('bank', ab)], writes=[ks_])
                    col = 0
                else:
                    j = rd['j']
                    P.op('dve', lambda g, o=smt[:, 4:8], a=accv[:, :, 64], e=esink[:, 4 * j:4 * j + 4]: g.tensor_tensor(o, a, e, ALU.add),
                         reads=[('bank', ab), kes], writes=[ks_ + 'a'])
                    P.op('dve', lambda g, o=smt[:, 0:4], a=smt[:, 4:8]: g.reciprocal(o, a), reads=[ks_ + 'a'], writes=[ks_])
                    col = 256 + 256 * j
                P.op('dve', lambda g, o=ob_[:, col:col + 256].rearrange('p (h d) -> p h d', d=64), a=accv[:, :, 0:64],
                     r=smt[:, 0:4].unsqueeze(2).to_broadcast([128, 4, 64]): g.tensor_tensor(o, a, r, ALU.mult),
                     reads=[('bank', ab), ks_], writes=[koall[qb]])
                return
            g_, half, br = rd['g'], rd['half'], rd['br']
            hh0 = 6 * g_ + 3 * half
            ui = 2 * g_ + half
            accv = cx.bank(ab)[:, 0:195].rearrange('p (h c) -> p h c', c=65)
            P.op('dve', lambda g, o=smt[:, 0:3], a=accv[:, :, 64]: g.tensor_scalar(o, a, 1e-30, None, ALU.max), reads=[('bank', ab)], writes=[ks_ + 'a'])
            P.op('dve', lambda g, o=smt[:, 4:7], a=smt[:, 0:3]: g.reciprocal(o, a), reads=[ks_ + 'a'], writes=[ks_ + 'b'])
            P.op('dve', lambda g, o=smt[:, 8:11], a=smt[:, 4:7], c=cgv[:, hh0:hh0 + 3, br]: g.tensor_tensor(o, a, c, ALU.mult),
                 reads=[ks_ + 'b', kcg[qb]], writes=[ks_ + 'c'])
            t3 = ism[0] % 2
            ocv = oc[:, hh0 * 64:(hh0 + 3) * 64]
            if kind == 'CMP':
                P.op('dve', lambda g, o=ocv.rearrange('p (h d) -> p h d', d=64), a=accv[:, :, 0:64],
                     w=smt[:, 8:11].unsqueeze(2).to_broadcast([128, 3, 64]): g.tensor_tensor(o, a, w, ALU.mult),
                     reads=[('bank', ab), ks_ + 'c'], writes=[koc[ui]])
                for r in range(3):
                    src = tb if (half == 0 and r == 0) else imp[g_]
                    P.op('dve', lambda g, o=imp[g_], u=cx.bank(5)[:, r * 128:(r + 1) * 128], s=smt[:, 4 + r:5 + r], a=src:
                         g.scalar_tensor_tensor(o, u, s, a, ALU.mult, ALU.add), reads=[('bank', 5), ks_ + 'b', ktb, kimp[g_]], writes=[kimp[g_]])
                if half == 1:
                    P.op('dve', lambda g, o=wk[0], a=imp[g_], f=frc[qb][:, 0:128]: g.tensor_tensor(o, a, f, ALU.max), reads=[kimp[g_], kfrc[qb]], writes=[kwk + '0'])
                    P.op('dve', lambda g, o=wk[1], a=wk[0], f=frc[qb][:, 128:256]: g.tensor_tensor(o, a, f, ALU.min), reads=[kwk + '0', kfrc[qb]], writes=[kwk + '1'])
                    P.op('dve', lambda g, o=m8[:, 0:8], a=wk[1]: g.max(o, a), reads=[kwk + '1'], writes=[kwk + 'm'])
                    P.op('dve', lambda g, o=wk[2], a=m8[:, 0:8], b=wk[1]: g.match_replace(o, a, b, -3.0e38), reads=[kwk + 'm', kwk + '1'], writes=[kwk + '2'])
                    P.op('dve', lambda g, o=m8[:, 8:16], a=wk[2]: g.max(o, a), reads=[kwk + '2'], writes=[kwk + 'n'])
                    P.op('dve', lambda g, o=wk[0], a=wk[1], t=m8[:, 15:16]: g.tensor_scalar(o, a, t, None, ALU.is_ge), reads=[kwk + '1', kwk + 'n'], writes=[kwk + '0'])
                    P.op('dve', lambda g, o=selm[g_], a=wk[0]: g.tensor_scalar(o, a, -1.0, -NEG, ALU.add, ALU.mult), reads=[kwk + '0'], writes=[kselm[g_]])
                    tb_ = tpbank()
                    P.op('pe', lambda g, o=cx.bank_bf(tb_)[:, 0:128], a=selm[g_]: g.transpose(o, a, ident), reads=[kselm[g_], kid], writes=[('bank', tb_)])
                    P.op('act', lambda g, o=selmT[g_], a=cx.bank_bf(tb_)[:, 0:128]: g.copy(o, a), reads=[('bank', tb_)], writes=[kselmT[g_]])
            else:
                P.op('dve', lambda g, o=tmp3[t3].rearrange('p (h d) -> p h d', d=64), a=accv[:, :, 0:64],
                     w=smt[:, 8:11].unsqueeze(2).to_broadcast([128, 3, 64]): g.tensor_tensor(o, a, w, ALU.mult),
                     reads=[('bank', ab), ks_ + 'c'], writes=[ktmp3[t3]])
                if kind == 'CW':
                    P.op('pool', lambda g, o=ocv, a=ocv, b=tmp3[t3]: g.tensor_tensor(o, a, b, ALU.add), reads=[koc[ui], ktmp3[t3]], writes=[koc[ui]])
                else:
                    P.op('pool', lambda g, o=ob_[:, 768 + hh0 * 64:768 + (hh0 + 3) * 64], a=ocv, b=tmp3[t3]: g.tensor_tensor(o, a, b, ALU.add),
                         reads=[koc[ui], ktmp3[t3]], writes=[koall[qb]])

        LOOK = 2
        n = len(steps)
        for k in range(n + LOOK):
            if k < n:
                emit_qk(k)
            if k >= LOOK:
                emit_pv(k - LOOK)
        for f0, nf, b in ((0, 8, 0), (8, 4, 1)):
            for f in range(nf):
                P.op('pe', lambda g, o=cx.bank_bf(b)[:, f * 128:(f + 1) * 128], a=oall[qb][:, (f0 + f) * 128:(f0 + f + 1) * 128]: g.transpose(o, a, ident),
                     reads=[koall[qb], kid], writes=[('bank', b)])
            P.op('act', lambda g, o=oTs[:, f0 * 128:(f0 + nf) * 128], a=cx.bank_bf(b)[:, 0:nf * 128]: g.copy(o, a), reads=[('bank', b)], writes=[koTs])
        P.dma(ot_d.rearrange('f p t -> p f t')[:, :, i * 128:(i + 1) * 128], oTs.rearrange('p (f t) -> p f t', f=12), reads=[koTs])

    prep(0)
    for i in range(NQT):
        if i + 1 < NQT:
            prep(i + 1)
        run_tile(i)
    P.barrier()


BR_FB = ((0, 2), (2, 6), (6, 12))


def phase_out(cx, x_in, x_out, gcol_d, wgate_b, wbr_b, wout_b, ot_d, NT, ident_d, TT=512):
    P, A = cx.P, cx.A
    A.reset()
    nsub = TT // 128
    ident = A.alloc(128, BF16)
    kid = cx.key('ident')
    gcol = A.alloc(KC, F32)
    kg = cx.key('gcol')
    P.dma(ident, ident_d[:, :], writes=[kid])
    P.dma(gcol, gcol_d[:, :], writes=[kg])
    xnT = A.alloc(KC * TT, BF16)
    kxn = cx.key('xnT')
    mT = A.alloc(KC * TT, BF16)
    kmT = [cx.key('mT') for _ in range(KC)]
    oTt = A.alloc(12 * TT, BF16)
    koT = cx.key('oTt')
    xs = [A.alloc(D, F32) for _ in range(2)]
    kxs = [cx.key('xs') for _ in range(2)]
    xb = [A.alloc(D, BF16) for _ in range(2)]
    kxb = [cx.key('xb') for _ in range(2)]
    junk = A.alloc(D, BF16)
    kjunk = cx.key('junk')
    st = [A.alloc(8, F32) for _ in range(2)]
    kst = [cx.key('st') for _ in range(2)]
    NG = 4
    wg = [A.alloc(KC * 128, BF16) for _ in range(NG)]
    kwg = [cx.key('wg') for _ in range(NG)]
    wbr = [A.alloc(1536, BF16) for _ in range(2)]
    kwbr = [cx.key('wbr') for _ in range(2)]
    wo = [A.alloc(D, BF16) for _ in range(3)]
    kwo = [cx.key('wo') for _ in range(3)]
    sg = [A.alloc(TT, F32) for _ in range(2)]
    ksg = [cx.key('sg') for _ in range(2)]
    macc = [A.alloc(TT, F32) for _ in range(2)]
    kmacc = [cx.key('macc') for _ in range(2)]
    tmpm = [A.alloc(TT, F32) for _ in range(2)]
    ktmpm = [cx.key('tmpm') for _ in range(2)]
    xr = [A.alloc(D, F32) for _ in range(2)]
    kxr = [cx.key('xr') for _ in range(2)]
    xo = [A.alloc(D, F32) for _ in range(2)]
    kxo = [cx.key('xo') for _ in range(2)]
    ig = 0
    iwo = 0
    ipair = 0
    for t0 in range(0, NT, TT):
        emit_norm_transpose(cx, x_in, t0, nsub, gcol, kg, xnT, kxn, xs, kxs, xb, kxb, junk, kjunk, st, kst, ident, 4)
        P.dma(oTt.rearrange('p (f t) -> p f t', f=12), ot_d.rearrange('f p t -> p f t')[:, :, t0:t0 + TT], writes=[koT])
        for dc in range(KC):
            wb = dc % 2
            P.dma(wbr[wb], wbr_b[dc * 128:(dc + 1) * 128, :], writes=[kwbr[wb]])
            ma = dc % 2
            for b in range(3):
                s = ig % NG
                ig += 1
                P.dma(wg[s], wgate_b[(b * KC + dc) * 128:(b * KC + dc + 1) * 128, :], writes=[kwg[s]])
                bg = ipair % 2
                bb = 2 + ipair % 2
                ipair += 1
                pg, pb = cx.bank(bg)[:, 0:TT], cx.bank(bb)[:, 0:TT]
                for kc in range(KC):
                    P.op('pe', lambda g, o=pg, w=wg[s][:, kc * 128:(kc + 1) * 128], a=xnT[:, kc * TT:(kc + 1) * TT], k=kc:
                         g.matmul(o, w, a, start=(k == 0), stop=(k == KC - 1)), reads=[kwg[s], kxn], writes=[('bank', bg)])
                f0, f1 = BR_FB[b]
                for fb in range(f0, f1):
                    P.op('pe', lambda g, o=pb, w=wbr[wb][:, fb * 128:(fb + 1) * 128], a=oTt[:, fb * TT:(fb + 1) * TT], f=fb, f0=f0, f1=f1:
                         g.matmul(o, w, a, start=(f == f0), stop=(f == f1 - 1)), reads=[kwbr[wb], koT], writes=[('bank', bb)])
                q = ipair % 2
                P.op('act', lambda g, o=sg[q], a=pg: g.activation(o, a, AF.Sigmoid), reads=[('bank', bg)], writes=[ksg[q]])
                if b == 0:
                    P.op('dve', lambda g, o=macc[ma], a=pb, c=sg[q]: g.tensor_tensor(o, a, c, ALU.mult), reads=[('bank', bb), ksg[q]], writes=[kmacc[ma]])
                else:
                    P.op('dve', lambda g, o=tmpm[q], a=pb, c=sg[q]: g.tensor_tensor(o, a, c, ALU.mult), reads=[('bank', bb), ksg[q]], writes=[ktmpm[q]])
                    if b == 1:
                        P.op('pool', lambda g, o=macc[ma], a=macc[ma], c=tmpm[q]: g.tensor_tensor(o, a, c, ALU.add), reads=[kmacc[ma], ktmpm[q]], writes=[kmacc[ma]])
                    else:
                        P.op('pool', lambda g, o=mT[:, dc * TT:(dc + 1) * TT], a=macc[ma], c=tmpm[q]: g.tensor_tensor(o, a, c, ALU.add),
                             reads=[kmacc[ma], ktmpm[q]], writes=[kmT[dc]])
        for s0 in range(0, nsub, 2):
            for dc in range(KC):
                s = iwo % 3
                iwo += 1
                P.dma(wo[s], wout_b[dc * 128:(dc + 1) * 128, :], writes=[kwo[s]])
                for ss in range(2):
                    for cb in range(4):
                        b = ss * 4 + cb
                        P.op('pe', lambda g, o=cx.bank(b), m=mT[:, dc * TT + (s0 + ss) * 128:dc * TT + (s0 + ss + 1) * 128],
                             w=wo[s][:, cb * 512:(cb + 1) * 512], d=dc: g.matmul(o, m, w, start=(d == 0), stop=(d == KC - 1)),
                             reads=[kwo[s], kmT[dc]], writes=[('bank', b)])
            for ss in range(2):
                tok = t0 + (s0 + ss) * 128
                q = ss
                P.dma(xr[q], x_in[tok:tok + 128, :], writes=[kxr[q]])
                for cb in range(4):
                    b = ss * 4 + cb
                    P.op('dve', lambda g, o=xo[q][:, cb * 512:(cb + 1) * 512], a=cx.bank(b), r=xr[q][:, cb * 512:(cb + 1) * 512]:
                         g.tensor_tensor(o, a, r, ALU.add), reads=[('bank', b), kxr[q]], writes=[kxo[q]])
                P.dma(x_out[tok:tok + 128, :], xo[q], reads=[kxo[q]])
    P.barrier()


def lay_wgate(w_in):
    g = w_in[:, GATE0:GATE0 + 3 * D]
    return np.ascontiguousarray(g.reshape(KC, 128, 3 * KC, 128).transpose(2, 1, 0, 3)).reshape(3 * KC * 128, KC * 128)


def lay_wbr(wa, wb, wc):
    w = np.concatenate([wa, wb, wc], axis=0)
    return np.ascontiguousarray(w.reshape(12, 128, KC, 128).transpose(2, 1, 0, 3)).reshape(KC * 128, 1536)


def lay_cmp_w1(w1):
    return np.ascontiguousarray(w1.reshape(2, 16, 128, 256).transpose(0, 2, 1, 3)).reshape(2 * 128, 16 * 256)


def lay_cmp_w2(w2):
    return np.ascontiguousarray(w2.reshape(2, 2, 128, 64).transpose(0, 2, 1, 3)).reshape(2 * 128, 128)


def lay_cmp_pos(pos):
    return np.ascontiguousarray(pos.reshape(2, 16, 2, 64).transpose(0, 2, 3, 1)).reshape(2 * 128, 16)


NCORES = 8
SEQ = 8192
BATCH = 4
NQT_FULL = SEQ // 256
NT_FULL = NQT_FULL * 128


def _din(nc, name, shape, dt):
    return nc.dram_tensor(name, list(shape), dt, kind="ExternalInput").ap()


def _dout(nc, name, shape, dt):
    return nc.dram_tensor(name, list(shape), dt, kind="ExternalOutput").ap()


def _dint(nc, name, shape, dt):
    return nc.dram_tensor(name, list(shape), dt, kind="Internal").ap()


def build_prog_a(NT):
    nc = bass.Bass("TRN2", target_bir_lowering=False)
    x = _din(nc, "x", [NT, D], F32)
    g1 = _din(nc, "g1", [128, KC], F32)
    wgu = _din(nc, "wgu", [FC * 128, KC * 256], F32)
    wdn = _din(nc, "wdn", [DFF, D], F32)
    gm = _din(nc, "gm", [128, KC], F32)
    win = _din(nc, "win", [NB1 * 128, KC * 512], F32)
    gain = _din(nc, "gain", [128, 3200], F32)
    ident = _din(nc, "ident", [128, 128], BF16)
    x1 = _dout(nc, "x1", [NT, D], F32)
    qn = _dout(nc, "qn", [NT, 2048], BF16)
    kv = _dout(nc, "kv", [NT, 2560], BF16)
    cg = _dout(nc, "cg", [NT, 36], F32)
    wgu_b = _dint(nc, "wgu_b", [FC * 128, KC * 256], BF16)
    wdn_b = _dint(nc, "wdn_b", [DFF, D], BF16)
    win_b = _dint(nc, "win_b", [NB1 * 128, KC * 512], BF16)
    cx = Ctx(nc)
    phase_cast(cx, [(wgu, wgu_b), (wdn, wdn_b), (win, win_b)])
    phase_ffn(cx, x, x1, g1, wgu_b, wdn_b, NT, ident)
    phase_proj(cx, x1, gm, win_b, gain, qn, kv, cg, NT, ident)
    cx.P.emit()
    return nc


def build_prog_b(NQT):
    NT = NQT * 128
    ncp = (2 * NT - 32) // 16 + 1
    NCP = (ncp + 127) // 128 * 128
    nc = bass.Bass("TRN2", target_bir_lowering=False)
    x1 = _din(nc, "x1", [NT, D], F32)
    qn = _din(nc, "qn", [NT, 2048], BF16)
    kvf = _din(nc, "kvf", [2 * NT, 2560], BF16)
    cg = _din(nc, "cg", [NT, 36], F32)
    sinks = _din(nc, "sinks", [128, 8], F32)
    w1 = _din(nc, "w1", [256, 4096], F32)
    w2 = _din(nc, "w2", [256, 128], F32)
    pos = _din(nc, "pos", [256, 16], F32)
    kgain = _din(nc, "kgain", [128, 64], F32)
    gm = _din(nc, "gm", [128, KC], F32)
    wgate = _din(nc, "wgate", [3 * KC * 128, KC * 128], F32)
    wbr = _din(nc, "wbr", [KC * 128, 1536], F32)
    wout = _din(nc, "wout", [D, D], F32)
    g2 = _din(nc, "g2", [128, KC], F32)
    wgu = _din(nc, "wgu", [FC * 128, KC * 256], F32)
    wdn = _din(nc, "wdn", [DFF, D], F32)
    C = {k: _din(nc, "c_" + k, shp, dt) for k, (shp, dt) in CONST_SPECS(NQT).items()}
    y = _dout(nc, "y", [NT, D], F32)
    w1_b = _dint(nc, "w1_b", [256, 4096], BF16)
    w2_b = _dint(nc, "w2_b", [256, 128], BF16)
    wgate_b = _dint(nc, "wgate_b", [3 * KC * 128, KC * 128], BF16)
    wbr_b = _dint(nc, "wbr_b", [KC * 128, 1536], BF16)
    wout_b = _dint(nc, "wout_b", [D, D], BF16)
    wgu_b = _dint(nc, "wgu_b", [FC * 128, KC * 256], BF16)
    wdn_b = _dint(nc, "wdn_b", [DFF, D], BF16)
    cmpk = _dint(nc, "cmpk", [NCP, 128], BF16)
    cmpv = _dint(nc, "cmpv", [NCP, 128], BF16)
    ot = _dint(nc, "ot", [12, 128, NT], BF16)
    x2 = _dint(nc, "x2", [NT, D], F32)
    cx = Ctx(nc)
    phase_cast(cx, [(w1, w1_b), (w2, w2_b), (wgate, wgate_b), (wbr, wbr_b), (wout, wout_b), (wgu, wgu_b), (wdn, wdn_b)])
    phase_cmp(cx, kvf, w1_b, w2_b, pos, kgain, C['ident'], cmpk, cmpv, NQT)
    phase_attn(cx, qn, kvf, cmpk, cmpv, cg, sinks, C, ot, NQT)
    phase_out(cx, x1, x2, gm, wgate_b, wbr_b, wout_b, ot, NT, C['ident'])
    phase_ffn(cx, x2, y, g2, wgu_b, wdn_b, NT, C['ident'])
    cx.P.emit()
    return nc


def _rep(v, n=128):
    return np.ascontiguousarray(np.broadcast_to(np.asarray(v, np.float32)[None, :], (n, v.shape[0])))


WSPECS = [("g1", [128, KC]), ("wgu1", [FC * 128, KC * 256]), ("wdn1", [DFF, D]), ("gm", [128, KC]), ("win", [NB1 * 128, KC * 512]),
          ("gain", [128, 3200]), ("sinks", [128, 8]), ("w1", [256, 4096]), ("w2", [256, 128]), ("pos", [256, 16]), ("kgain", [128, 64]),
          ("wgate", [3 * KC * 128, KC * 128]), ("wbr", [KC * 128, 1536]), ("wout", [D, D]), ("g2", [128, KC]),
          ("wgu2", [FC * 128, KC * 256]), ("wdn2", [DFF, D])]
WCAST = ("wgu1", "wdn1", "win", "w1", "w2", "wgate", "wbr", "wout", "wgu2", "wdn2")


def build_fused(NQT, depth):
    NT = NQT * 128
    NT2 = 2 * NT
    ncp = (NT2 - 32) // 16 + 1
    NCP = (ncp + 127) // 128 * 128
    nc = bass.Bass("TRN2", target_bir_lowering=False)
    x = _din(nc, "x", [NT2, D], F32)
    W = [{k: _din(nc, "l%d_%s" % (l, k), shp, F32) for k, shp in WSPECS} for l in range(depth)]
    Cs = [{k: _din(nc, "c%d_%s" % (hf, k), shp, dt) for k, (shp, dt) in CONST_SPECS(NQT).items()} for hf in range(2)]
    y = _dout(nc, "y", [NT2, D], F32)
    Wb = {k: _dint(nc, "b_" + k, dict(WSPECS)[k], BF16) for k in WCAST}
    xa = _dint(nc, "xa", [NT2, D], F32)
    xb_ = _dint(nc, "xb", [NT2, D], F32)
    xc = _dint(nc, "xc", [NT2, D], F32)
    qn = _dint(nc, "qn", [NT2, 2048], BF16)
    kvf = _dint(nc, "kvf", [NT2, 2560], BF16)
    cg = _dint(nc, "cg", [NT2, 36], F32)
    cmpk = _dint(nc, "cmpk", [NCP, 128], BF16)
    cmpv = _dint(nc, "cmpv", [NCP, 128], BF16)
    ot = _dint(nc, "ot", [12, 128, NT2], BF16)
    cx = Ctx(nc)
    ident = Cs[0]['ident']
    cur = x
    for l in range(depth):
        w = W[l]
        phase_cast(cx, [(w[k], Wb[k]) for k in WCAST])
        phase_ffn(cx, cur, xa, w["g1"], Wb["wgu1"], Wb["wdn1"], NT2, ident)
        phase_proj(cx, xa, w["gm"], Wb["win"], w["gain"], qn, kvf, cg, NT2, ident)
        phase_cmp(cx, kvf, Wb["w1"], Wb["w2"], w["pos"], w["kgain"], ident, cmpk, cmpv, NQT)
        for hf in range(2):
            phase_attn(cx, qn[hf * NT:(hf + 1) * NT, :], kvf, cmpk, cmpv, cg[hf * NT:(hf + 1) * NT, :], w["sinks"], Cs[hf],
                       ot[:, :, hf * NT:(hf + 1) * NT], NQT)
        phase_out(cx, xa, xb_, w["gm"], Wb["wgate"], Wb["wbr"], Wb["wout"], ot, NT2, ident)
        dst = y if l == depth - 1 else xc
        phase_ffn(cx, xb_, dst, w["g2"], Wb["wgu2"], Wb["wdn2"], NT2, ident)
        cur = xc
        if l + 1 < depth:
            cx.P.new_epoch()
    cx.P.emit()
    return nc


def layer_weights(l, ffn1_norm, ffn1_w_gu, ffn1_w_down, mix_norm, w_in, qk_gain, sinks, cmp_pos, cmp_w1, cmp_w2,
                  w_branch_a, w_branch_b, w_branch_c, w_out, ffn2_norm, ffn2_w_gu, ffn2_w_down):
    return {"g1": lay_col(ffn1_norm[l]), "wgu1": lay_wgu(ffn1_w_gu[l]), "wdn1": np.ascontiguousarray(ffn1_w_down[l]),
            "gm": lay_col(mix_norm[l]), "win": lay_win_p1(w_in[l]), "gain": lay_gain(qk_gain[l]), "sinks": _rep(sinks[l]),
            "w1": lay_cmp_w1(cmp_w1[l]), "w2": lay_cmp_w2(cmp_w2[l]), "pos": lay_cmp_pos(cmp_pos[l]), "kgain": _rep(qk_gain[l, 2, 1]),
            "wgate": lay_wgate(w_in[l]), "wbr": lay_wbr(w_branch_a[l], w_branch_b[l], w_branch_c[l]),
            "wout": np.ascontiguousarray(w_out[l]), "g2": lay_col(ffn2_norm[l]), "wgu2": lay_wgu(ffn2_w_gu[l]),
            "wdn2": np.ascontiguousarray(ffn2_w_down[l])}


def kernel(x, ffn1_norm, ffn1_w_gu, ffn1_w_down, mix_norm, w_in, qk_gain, sinks, cmp_pos, cmp_w1, cmp_w2,
           w_branch_a, w_branch_b, w_branch_c, w_out, ffn2_norm, ffn2_w_gu, ffn2_w_down):
    f = lambda a: np.asarray(a, np.float32)
    x = f(x)
    NQT, NT = NQT_FULL, NT_FULL
    cores = list(range(NCORES))
    xs = [np.ascontiguousarray(x[c // 2].reshape(2 * NQT, 128, D)[c % 2::2].reshape(NT, D)) for c in cores]
    ident = np.eye(128, dtype=ml_dtypes.bfloat16)
    consts = [build_consts(hf, NQT) for hf in range(2)]
    nca = build_prog_a(NT)
    ncb = build_prog_b(NQT)
    depth = f(ffn1_norm).shape[0]
    for l in range(depth):
        wa = {"g1": lay_col(f(ffn1_norm)[l]), "wgu": lay_wgu(f(ffn1_w_gu)[l]), "wdn": np.ascontiguousarray(f(ffn1_w_down)[l]),
              "gm": lay_col(f(mix_norm)[l]), "win": lay_win_p1(f(w_in)[l]), "gain": lay_gain(f(qk_gain)[l]), "ident": ident}
        ra = run_bass_kernel_spmd(nca, [dict(wa, x=xs[c]) for c in cores], core_ids=cores).results
        wb = {"sinks": _rep(f(sinks)[l]), "w1": lay_cmp_w1(f(cmp_w1)[l]), "w2": lay_cmp_w2(f(cmp_w2)[l]), "pos": lay_cmp_pos(f(cmp_pos)[l]),
              "kgain": _rep(f(qk_gain)[l, 2, 1]), "gm": wa["gm"], "wgate": lay_wgate(f(w_in)[l]),
              "wbr": lay_wbr(f(w_branch_a)[l], f(w_branch_b)[l], f(w_branch_c)[l]), "wout": np.ascontiguousarray(f(w_out)[l]),
              "g2": lay_col(f(ffn2_norm)[l]), "wgu": lay_wgu(f(ffn2_w_gu)[l]), "wdn": np.ascontiguousarray(f(ffn2_w_down)[l])}
        ims = []
        for c in cores:
            p = c - c % 2
            kvf = np.concatenate([np.asarray(ra[p]["kv"]), np.asarray(ra[p + 1]["kv"])], axis=0)
            im = dict(wb, x1=np.asarray(ra[c]["x1"]), qn=np.asarray(ra[c]["qn"]), kvf=kvf, cg=np.asarray(ra[c]["cg"]))
            for k, v in consts[c % 2].items():
                im["c_" + k] = v
            ims.append(im)
        rb = run_bass_kernel_spmd(ncb, ims, core_ids=cores).results
        xs = [np.asarray(rb[c]["y"]) for c in cores]
    out = np.zeros((BATCH, SEQ, D), np.float32)
    for c in cores:
        out[c // 2].reshape(2 * NQT, 128, D)[c % 2::2] = xs[c].reshape(NQT, 128, D)
    return out
```

```python
import numpy as np
import ml_dtypes
import concourse.bass as bass
import concourse.mybir as mybir
from concourse.bass_utils import run_bass_kernel_spmd

F32 = mybir.dt.float32
BF16 = mybir.dt.bfloat16
AF = mybir.ActivationFunctionType
ALU = mybir.AluOpType
AX = mybir.AxisListType

D = 2048
DFF = 5632
KC = D // 128
FC = DFF // 128
RMS_EPS = 1e-6


class Prog:
    CE = ('pe', 'act', 'dve', 'pool')
    ENG = ('pe', 'act', 'dve', 'pool', 'sp')

    def __init__(self, nc, n_dma=16):
        self.nc = nc
        self.streams = {e: [] for e in self.ENG}
        self.cnt = {e: 0 for e in self.CE}
        self.esem = {e: nc.alloc_semaphore('sem_' + e) for e in self.CE}
        self.dsem = [nc.alloc_semaphore('sem_d%d' % i) for i in range(n_dma)]
        self.dcount = [0] * n_dma
        self.dnext = 0
        self.seen = {e: {} for e in self.ENG}
        self.lastw = {}
        self.readers = {}
        self.n_wait = 0

    def sem(self, key):
        return self.esem[key[1]] if key[0] == 'e' else self.dsem[key[1]]

    def _deps(self, reads, writes):
        deps = []
        for r in reads:
            t = self.lastw.get(r)
            if t:
                deps.append(t)
        for w in writes:
            t = self.lastw.get(w)
            if t:
                deps.append(t)
            deps.extend(self.readers.get(w, {}).items())
        return deps

    def _waits(self, eng, deps):
        need = {}
        for key, val in deps:
            if key == ('e', 'pe') and eng == 'pe':
                continue
            if self.seen[eng].get(key, 0) >= val:
                continue
            if need.get(key, 0) < val:
                need[key] = val
        for key, val in need.items():
            self.seen[eng][key] = val
        self.n_wait += len(need)
        return [(self.sem(key), val) for key, val in need.items()]

    def _commit(self, tok, reads, writes):
        for w in writes:
            self.lastw[w] = tok
            self.readers[w] = {}
        for r in reads:
            d = self.readers.setdefault(r, {})
            if d.get(tok[0], 0) < tok[1]:
                d[tok[0]] = tok[1]

    def op(self, eng, fn, reads=(), writes=()):
        waits = self._waits(eng, self._deps(reads, writes))
        self.cnt[eng] += 1
        tok = (('e', eng), self.cnt[eng])
        self.streams[eng].append((waits, fn, (self.esem[eng], 1)))
        self._commit(tok, reads, writes)

    def dma(self, out, in_, reads=(), writes=(), q='sp'):
        k = self.dnext
        self.dnext = (k + 1) % len(self.dsem)
        deps = self._deps(reads, writes)
        if self.dcount[k]:
            deps.append((('d', k), 16 * self.dcount[k]))
        waits = self._waits(q, deps)
        self.dcount[k] += 1
        tok = (('d', k), 16 * self.dcount[k])
        self.streams[q].append((waits, (lambda e, o=out, i=in_: e.dma_start(out=o, in_=i)), (self.dsem[k], 16)))
        self._commit(tok, reads, writes)

    def collective(self, kind, src, dst, groups, reads=(), writes=()):
        k = self.dnext
        self.dnext = (k + 1) % len(self.dsem)
        deps = self._deps(reads, writes)
        if self.dcount[k]:
            deps.append((('d', k), 16 * self.dcount[k]))
        waits = self._waits('pool', deps)
        self.dcount[k] += 1
        tok = (('d', k), 16 * self.dcount[k])
        self.streams['pool'].append((waits, (lambda e, a=src, b=dst: e.collective_compute(kind, ALU.bypass, replica_groups=groups, ins=[a], outs=[b])), (self.dsem[k], 16)))
        self._commit(tok, reads, writes)

    def barrier(self):
        allt = [(('e', e), self.cnt[e]) for e in self.CE if self.cnt[e]]
        allt += [(('d', k), 16 * c) for k, c in enumerate(self.dcount) if c]
        for e in self.ENG:
            waits = self._waits(e, allt)
            if waits:
                self.streams[e].append((waits, None, None))
        self.lastw.clear()
        self.readers.clear()

    def new_epoch(self):
        nc = self.nc
        self.epoch = getattr(self, 'epoch', 0) + 1
        self.esem = {e: nc.alloc_semaphore('sem%d_%s' % (self.epoch, e)) for e in self.CE}
        self.dsem = [nc.alloc_semaphore('sem%d_d%d' % (self.epoch, i)) for i in range(len(self.dsem))]
        self.cnt = {e: 0 for e in self.CE}
        self.dcount = [0] * len(self.dsem)
        self.dnext = 0
        self.seen = {e: {} for e in self.ENG}

    def emit(self):
        nc = self.nc

        def run(name, eng):
            for waits, fn, inc in self.streams[name]:
                for sem, val in waits:
                    eng.wait_ge(sem, val)
                if fn is not None:
                    ins = fn(eng)
                    ins.then_inc(inc[0], inc[1])

        with nc.Block() as block:
            @block.tensor
            def _(e):
                run('pe', e)

            @block.scalar
            def _(e):
                run('act', e)

            @block.vector
            def _(e):
                run('dve', e)

            @block.gpsimd
            def _(e):
                run('pool', e)

            @block.sync
            def _(e):
                run('sp', e)


class Arena:
    def __init__(self, nc, kbytes=204):
        self.words = kbytes * 1024 // 4
        self.t = nc.alloc_sbuf_tensor('arena', [128, self.words], F32)
        self.off = 0

    def reset(self):
        self.off = 0

    def alloc(self, cols, dtype=F32):
        nw = cols if dtype == F32 else (cols + 1) // 2
        nw = (nw + 7) // 8 * 8
        assert self.off + nw <= self.words, 'SBUF arena overflow: %d + %d > %d' % (self.off, nw, self.words)
        ap = self.t[:, self.off:self.off + nw]
        self.off += nw
        if dtype != F32:
            ap = ap.bitcast(dtype)[:, 0:cols]
        else:
            ap = ap[:, 0:cols]
        return ap


class Ctx:
    def __init__(self, nc):
        self.nc = nc
        self.P = Prog(nc)
        self.A = Arena(nc)
        self.psum = nc.alloc_psum_tensor('psum', [128, 8 * 512], F32)
        self.uid = 0

    def bank(self, b, nb=1):
        return self.psum[:, b * 512:(b + nb) * 512]

    def bank_bf(self, b, nb=1):
        return self.psum[:, b * 512:(b + nb) * 512].bitcast(BF16)

    def key(self, name):
        self.uid += 1
        return '%s#%d' % (name, self.uid)


def phase_cast(cx, pairs, cb=4096):
    P, A = cx.P, cx.A
    A.reset()
    nslot = 3
    stg = [A.alloc(cb, F32) for _ in range(nslot)]
    obf = [A.alloc(cb, BF16) for _ in range(nslot)]
    ks = [cx.key('cs') for _ in range(nslot)]
    ko = [cx.key('co') for _ in range(nslot)]
    engs = ('act', 'dve', 'pool')
    chunks = []
    for src, dst in pairs:
        R, C = src.shape
        assert R % 128 == 0
        for r0 in range(0, R, 128):
            for c0 in range(0, C, cb):
                chunks.append((src, dst, r0, c0, min(C, c0 + cb)))

    def load(i):
        src, dst, r0, c0, c1 = chunks[i]
        s = i % nslot
        P.dma(stg[s][:, 0:c1 - c0], src[r0:r0 + 128, c0:c1], writes=[ks[s]])
    load(0)
    for i, (src, dst, r0, c0, c1) in enumerate(chunks):
        if i + 1 < len(chunks):
            load(i + 1)
        n = c1 - c0
        s = i % nslot
        e = engs[i % 3]
        if e == 'act':
            P.op('act', lambda g, o=obf[s][:, 0:n], a=stg[s][:, 0:n]: g.copy(o, a), reads=[ks[s]], writes=[ko[s]])
        else:
            P.op(e, lambda g, o=obf[s][:, 0:n], a=stg[s][:, 0:n]: g.tensor_copy(o, a), reads=[ks[s]], writes=[ko[s]])
        P.dma(dst[r0:r0 + 128, c0:c1], obf[s][:, 0:n], reads=[ko[s]])
    P.barrier()


def emit_norm_transpose(cx, x_src, t0, nsub, gcol, kg, xnT, kxn, xs, kxs, xb, kxb, junk, kjunk, st, kst, ident, tbank0):
    P = cx.P
    TT = nsub * 128
    xnT3 = xnT.rearrange('p (k t) -> p k t', k=KC)
    for s in range(nsub):
        sl = s % len(xs)
        P.dma(xs[sl], x_src[t0 + s * 128:t0 + (s + 1) * 128, :], writes=[kxs[sl]])
        P.op('act', lambda g, o=junk, a=xs[sl], acc=st[sl][:, 0:1]: g.activation(o, a, AF.Square, accum_out=acc),
             reads=[kxs[sl]], writes=[kjunk, kst[sl] + 'a'])
        P.op('dve', lambda g, o=st[sl][:, 1:2], a=st[sl][:, 0:1]: g.tensor_scalar(o, a, 1.0 / D, RMS_EPS, ALU.mult, ALU.add),
             reads=[kst[sl] + 'a'], writes=[kst[sl] + 'b'])
        P.op('act', lambda g, o=st[sl][:, 3:4], a=st[sl][:, 1:2]: g.sqrt(o, a),
             reads=[kst[sl] + 'b'], writes=[kst[sl] + 'd'])
        P.op('dve', lambda g, o=st[sl][:, 2:3], a=st[sl][:, 3:4]: g.reciprocal(o, a),
             reads=[kst[sl] + 'd'], writes=[kst[sl] + 'c'])
        P.op('pool', lambda g, o=xb[sl], a=xs[sl], r=st[sl][:, 2:3]: g.tensor_scalar(o, a, r, None, ALU.mult),
             reads=[kxs[sl], kst[sl] + 'c'], writes=[kxb[sl]])
        tb = tbank0 + 2 * (s % 2)
        kb = ('bank', tb), ('bank', tb + 1)
        pbf = cx.bank_bf(tb, 2)
        for kc in range(KC):
            b = kb[0] if kc < 8 else kb[1]
            P.op('pe', lambda g, o=pbf[:, kc * 128:(kc + 1) * 128], a=xb[sl][:, kc * 128:(kc + 1) * 128]: g.transpose(o, a, ident),
                 reads=[kxb[sl]], writes=[b])
        P.op('dve', lambda g, o=xnT3[:, :, s * 128:(s + 1) * 128], a=pbf.rearrange('p (k t) -> p k t', k=KC),
             gc=gcol.unsqueeze(2).to_broadcast([128, KC, 128]): g.tensor_tensor(o, a, gc, ALU.mult),
             reads=[kb[0], kb[1], kg], writes=[kxn])


def load_ident(cx, ident_dram):
    pass


def phase_ffn(cx, x_in, x_out, gcol_d, wgu, wdn, NT, ident_d, TT=1024):
    P, A = cx.P, cx.A
    A.reset()
    assert NT % TT == 0 and TT == 1024
    nsub = TT // 128
    ident = A.alloc(128, BF16)
    kid = cx.key('ident')
    gcol = A.alloc(KC, F32)
    kg = cx.key('gcol')
    P.dma(ident, ident_d[:, :], writes=[kid])
    P.dma(gcol, gcol_d[:, :], writes=[kg])
    xnT = A.alloc(KC * TT, BF16)
    kxn = cx.key('xnT')
    hT = A.alloc(FC * TT, BF16)
    khT = [cx.key('hT') for _ in range(FC)]
    xs = [A.alloc(D, F32) for _ in range(2)]
    kxs = [cx.key('xs') for _ in range(2)]
    xb = [A.alloc(D, BF16) for _ in range(2)]
    kxb = [cx.key('xb') for _ in range(2)]
    junk = A.alloc(D, BF16)
    kjunk = cx.key('junk')
    st = [A.alloc(8, F32) for _ in range(2)]
    kst = [cx.key('st') for _ in range(2)]
    NW = 2
    wg = [A.alloc(KC * 256, BF16) for _ in range(NW)]
    kwg = [cx.key('wg') for _ in range(NW)]
    ND = 3
    FG = 4
    wd = [A.alloc(FG * 512, BF16) for _ in range(ND)]
    kwd = [cx.key('wd') for _ in range(ND)]
    sg = [A.alloc(TT, F32) for _ in range(2)]
    ksg = [cx.key('sg') for _ in range(2)]
    xr = [A.alloc(512, F32) for _ in range(2)]
    kxr = [cx.key('xr') for _ in range(2)]
    xo = [A.alloc(512, F32) for _ in range(2)]
    kxo = [cx.key('xo') for _ in range(2)]
    iw = 0
    idn = 0
    io = 0
    for t0 in range(0, NT, TT):
        emit_norm_transpose(cx, x_in, t0, nsub, gcol, kg, xnT, kxn, xs, kxs, xb, kxb, junk, kjunk, st, kst, ident, 4)
        for fc in range(FC):
            s = iw % NW
            iw += 1
            P.dma(wg[s], wgu[fc * 128:(fc + 1) * 128, :], writes=[kwg[s]])
            gb = 4 * (fc % 2)
            for gu in range(2):
                for kc in range(KC):
                    for h in range(2):
                        b = gb + 2 * gu + h
                        P.op('pe', lambda g, o=cx.bank(b), w=wg[s][:, kc * 256 + gu * 128:kc * 256 + (gu + 1) * 128],
                             a=xnT[:, kc * TT + h * 512:kc * TT + (h + 1) * 512], k=kc: g.matmul(o, w, a, start=(k == 0), stop=(k == KC - 1)),
                             reads=[kwg[s], kxn], writes=[('bank', b)])
            q = fc % 2
            for h in range(2):
                P.op('act', lambda g, o=sg[q][:, h * 512:(h + 1) * 512], a=cx.bank(gb + h): g.activation(o, a, AF.Silu),
                     reads=[('bank', gb + h)], writes=[ksg[q] + str(h)])
                P.op('dve', lambda g, o=hT[:, fc * TT + h * 512:fc * TT + (h + 1) * 512], a=cx.bank(gb + 2 + h), b_=sg[q][:, h * 512:(h + 1) * 512]:
                     g.tensor_tensor(o, a, b_, ALU.mult), reads=[('bank', gb + 2 + h), ksg[q] + str(h)], writes=[khT[fc]])
        for cb in range(4):
            for f0 in range(0, FC, FG):
                s = idn % ND
                idn += 1
                P.dma(wd[s].rearrange('p (f c) -> p f c', f=FG),
                      wdn[f0 * 128:(f0 + FG) * 128, cb * 512:(cb + 1) * 512].rearrange('(f p) c -> p f c', p=128), writes=[kwd[s]])
                for fi in range(FG):
                    fc = f0 + fi
                    for ss in range(nsub):
                        P.op('pe', lambda g, o=cx.bank(ss), h_=hT[:, fc * TT + ss * 128:fc * TT + (ss + 1) * 128],
                             w=wd[s][:, fi * 512:(fi + 1) * 512], f=fc: g.matmul(o, h_, w, start=(f == 0), stop=(f == FC - 1)),
                             reads=[kwd[s], khT[fc]], writes=[('bank', ss)])
            for ss in range(nsub):
                tok = t0 + ss * 128
                q = io % 2
                io += 1
                P.dma(xr[q], x_in[tok:tok + 128, cb * 512:(cb + 1) * 512], writes=[kxr[q]])
                P.op('dve', lambda g, o=xo[q], a=cx.bank(ss), r=xr[q]: g.scalar_tensor_tensor(o, a, 0.5, r, ALU.mult, ALU.add),
                     reads=[('bank', ss), kxr[q]], writes=[kxo[q]])
                P.dma(x_out[tok:tok + 128, cb * 512:(cb + 1) * 512], xo[q], reads=[kxo[q]])
    P.barrier()


HD = 64
P1_BLOCKS = [('n', 512, 0, 0), ('n', 512, 0, 512), ('n', 512, 0, 1024), ('n', 512, 0, 1536),
             ('n', 512, 1, 0), ('n', 512, 1, 512), ('n', 128, 1, 1024),
             ('c', 512, 1, 1152), ('c', 512, 1, 1664), ('c', 384, 1, 2176),
             ('g', 36, 2, 0)]
NB1 = len(P1_BLOCKS)


def phase_proj(cx, x_in, gcol_d, win_b, gain_d, qn_d, kv_d, cg_d, NT, ident_d, TT=1024):
    P, A = cx.P, cx.A
    A.reset()
    nsub = TT // 128
    ident = A.alloc(128, BF16)
    kid = cx.key('ident')
    gcol = A.alloc(KC, F32)
    kg = cx.key('gcol')
    gain = A.alloc(3200, F32)
    kgain = cx.key('gain')
    P.dma(ident, ident_d[:, :], writes=[kid])
    P.dma(gcol, gcol_d[:, :], writes=[kg])
    P.dma(gain, gain_d[:, :], writes=[kgain])
    P.op('dve', lambda g, o=gain[:, 0:2048]: g.tensor_scalar(o, o, 0.125, None, ALU.mult), reads=[kgain], writes=[kgain])
    xnT = A.alloc(KC * TT, BF16)
    kxn = cx.key('xnT')
    xs = [A.alloc(D, F32) for _ in range(2)]
    kxs = [cx.key('xs') for _ in range(2)]
    xb = [A.alloc(D, BF16) for _ in range(2)]
    kxb = [cx.key('xb') for _ in range(2)]
    junk = A.alloc(D, BF16)
    kjunk = cx.key('junk')
    st = [A.alloc(8, F32) for _ in range(2)]
    kst = [cx.key('st') for _ in range(2)]
    NW = 3
    wt = [A.alloc(KC * 512, BF16) for _ in range(NW)]
    kwt = [cx.key('wt') for _ in range(NW)]
    NE = 2
    sq = [A.alloc(512, F32) for _ in range(NE)]
    ksq = [cx.key('sq') for _ in range(NE)]
    tmp = [A.alloc(512, F32) for _ in range(NE)]
    ktmp = [cx.key('tmp') for _ in range(NE)]
    ss = [A.alloc(32, F32) for _ in range(NE)]
    kss = [cx.key('ss') for _ in range(NE)]
    ob = [A.alloc(512, BF16) for _ in range(NE)]
    kob = [cx.key('ob') for _ in range(NE)]
    og = [A.alloc(64, F32) for _ in range(NE)]
    kog = [cx.key('og') for _ in range(NE)]
    dests = (qn_d, kv_d, cg_d)
    iw = 0
    ie = 0
    for t0 in range(0, NT, TT):
        emit_norm_transpose(cx, x_in, t0, nsub, gcol, kg, xnT, kxn, xs, kxs, xb, kxb, junk, kjunk, st, kst, ident, 4)
        for bi, (kind, w, di, dc) in enumerate(P1_BLOCKS):
            sl = iw % NW
            if iw == 0:
                P.dma(wt[sl], win_b[bi * 128:(bi + 1) * 128, :], writes=[kwt[sl]])
            iw += 1
            if not (t0 + TT >= NT and bi == NB1 - 1):
                nb_ = (bi + 1) % NB1
                P.dma(wt[iw % NW], win_b[nb_ * 128:(nb_ + 1) * 128, :], writes=[kwt[iw % NW]])
            for s in range(nsub):
                b = (bi * nsub + s) % 4
                ps = cx.bank(b)[:, 0:w]
                for kc in range(KC):
                    P.op('pe', lambda g, o=ps, a=xnT[:, kc * TT + s * 128:kc * TT + (s + 1) * 128],
                         wv=wt[sl][:, kc * 512:kc * 512 + w], k=kc: g.matmul(o, a, wv, start=(k == 0), stop=(k == KC - 1)),
                         reads=[kxn, kwt[sl]], writes=[('bank', b)])
                e = ie % NE
                ie += 1
                tok = t0 + s * 128
                dst = dests[di][tok:tok + 128, dc:dc + w]
                if kind == 'n':
                    nh = w // HD
                    gofs = dc if di == 0 else 2048 + dc
                    P.op('act', lambda g, o=sq[e][:, 0:w], a=ps: g.activation(o, a, AF.Square), reads=[('bank', b)], writes=[ksq[e]])
                    P.op('dve', lambda g, o=ss[e][:, 0:nh], a=sq[e][:, 0:w].rearrange('p (h d) -> p h d', d=HD):
                         g.tensor_reduce(o, a, AX.X, ALU.add), reads=[ksq[e]], writes=[kss[e] + 'a'])
                    P.op('dve', lambda g, o=ss[e][:, 8:8 + nh], a=ss[e][:, 0:nh]: g.tensor_scalar(o, a, 1.0 / HD, RMS_EPS, ALU.mult, ALU.add),
                         reads=[kss[e] + 'a'], writes=[kss[e] + 'b'])
                    P.op('act', lambda g, o=ss[e][:, 16:16 + nh], a=ss[e][:, 8:8 + nh]: g.sqrt(o, a), reads=[kss[e] + 'b'], writes=[kss[e] + 'c'])
                    P.op('dve', lambda g, o=ss[e][:, 24:24 + nh], a=ss[e][:, 16:16 + nh]: g.reciprocal(o, a), reads=[kss[e] + 'c'], writes=[kss[e] + 'd'])
                    P.op('dve', lambda g, o=tmp[e][:, 0:w].rearrange('p (h d) -> p h d', d=HD), a=ps.rearrange('p (h d) -> p h d', d=HD),
                         r=ss[e][:, 24:24 + nh].unsqueeze(2).to_broadcast([128, nh, HD]): g.tensor_tensor(o, a, r, ALU.mult),
                         reads=[('bank', b), kss[e] + 'd'], writes=[ktmp[e]])
                    P.op('pool', lambda g, o=ob[e][:, 0:w], a=tmp[e][:, 0:w], gn=gain[:, gofs:gofs + w]: g.tensor_tensor(o, a, gn, ALU.mult),
                         reads=[ktmp[e], kgain], writes=[kob[e]])
                    P.dma(dst, ob[e][:, 0:w], reads=[kob[e]])
                elif kind == 'c':
                    P.op('act', lambda g, o=ob[e][:, 0:w], a=ps: g.copy(o, a), reads=[('bank', b)], writes=[kob[e]])
                    P.dma(dst, ob[e][:, 0:w], reads=[kob[e]])
                else:
                    P.op('act', lambda g, o=og[e][:, 0:w], a=ps: g.activation(o, a, AF.Sigmoid), reads=[('bank', b)], writes=[kog[e]])
                    P.dma(dst, og[e][:, 0:w], reads=[kog[e]])
    P.barrier()


def _r(a, b):
    return np.arange(a, b)


_Aq, _Ak, _Av = _r(0, 768), _r(768, 1536), _r(1536, 2304)
_Bq, _Bk, _Bv = _r(2304, 2816), _r(2816, 2944), _r(2944, 3072)
_Cq = _r(3072, 3840)
_Ckc, _Cvc, _Cks, _Cvs, _Ckw, _Cvw = [_r(3840 + 128 * i, 3968 + 128 * i) for i in range(6)]
_Cg = _r(4608, 4644)
GATE0 = 4644
P1_ORDER = np.concatenate([_Aq, _Bq, _Cq, _Ak, _Bk, _Cks, _Ckw, _Av, _Bv, _Ckc, _Cvc, _Cvs, _Cvw, _Cg])
P1_OFFS = [0, 512, 1024, 1536, 2048, 2560, 3072, 3200, 3712, 4224, 4608]
KV_AK, KV_BK, KV_CKS, KV_CKW, KV_AV, KV_BV, KV_CKC, KV_CVC, KV_CVS, KV_CVW = 0, 768, 896, 1024, 1152, 1920, 2048, 2176, 2304, 2432
QN_A, QN_B, QN_C = 0, 768, 1280


def lay_col(v):
    return np.ascontiguousarray(v.reshape(-1, 128).T)


def lay_wgu(w):
    return np.ascontiguousarray(w.reshape(KC, 128, 2, FC, 128).transpose(3, 1, 0, 2, 4)).reshape(FC * 128, KC * 256)


def lay_win_p1(w_in):
    out = np.zeros((NB1, 128, KC, 512), np.float32)
    for b, (kind, w, di, dc) in enumerate(P1_BLOCKS):
        cols = P1_ORDER[P1_OFFS[b]:P1_OFFS[b] + w]
        out[b, :, :, :w] = w_in[:, cols].reshape(KC, 128, w).transpose(1, 0, 2)
    return out.reshape(NB1 * 128, KC * 512)


def lay_gain(qk_gain):
    row = np.concatenate([np.tile(qk_gain[0, 0], 12), np.tile(qk_gain[1, 0], 8), np.tile(qk_gain[2, 0], 12),
                          np.tile(qk_gain[0, 1], 12), np.tile(qk_gain[1, 1], 2), np.tile(qk_gain[2, 1], 4)]).astype(np.float32)
    return np.ascontiguousarray(np.broadcast_to(row[None, :], (128, 3200)))


NEG = -30000.0
BIGF = 1.0e30
N_QH = 32
MB_B, MB_A0, MB_A1, MB_A2, MB_CW, MB_SL = 0, 3, 6, 12, 30, 36
NMASK = 38


def _bf(x):
    return np.asarray(x, np.float32).astype(ml_dtypes.bfloat16)


def _slopes(n):
    return (2.0 ** (-8.0 * np.arange(1, n + 1, dtype=np.float32) / n)).astype(np.float32)


def head_slopes():
    return np.concatenate([_slopes(12), _slopes(8), _slopes(12)]).astype(np.float64)


def build_consts(hf, NQT):
    S = 2 * NQT * 128
    NT = NQT * 128
    sl = head_slopes()
    s_hi = _bf(sl).astype(np.float64)
    s_lo = _bf(sl - s_hi).astype(np.float64)
    slp = s_hi + s_lo
    i = np.arange(NT)
    tq = (2 * (i // 128) + hf) * 128 + (i % 128)
    qpos = np.zeros((7, N_QH, NT), np.float64)
    qpos[0] = (128 * s_hi)[:, None]
    qpos[1] = (128 * s_lo)[:, None]
    qpos[2] = s_hi[:, None]
    qpos[3] = s_lo[:, None]
    v = -slp[:, None] * tq[None, :]
    v1 = _bf(v).astype(np.float64)
    v2 = _bf(v - v1).astype(np.float64)
    v3 = _bf(v - v1 - v2).astype(np.float64)
    qpos[4], qpos[5], qpos[6] = v1, v2, v3

    def kp(pos):
        t = np.zeros((7, len(pos)), np.float64)
        t[0] = t[1] = pos // 128
        t[2] = t[3] = pos % 128
        t[4:7] = 1.0
        return _bf(t)
    kpos = kp(np.arange(S))
    ncp = (S - 32) // 16 + 1
    NCT = (ncp + 127) // 128
    kposc = kp(16 * np.arange(NCT * 128) + 31)
    ki = np.arange(128)[:, None]
    qi = np.arange(128)[None, :]

    def wmask(delta, lo, hi, dil):
        dist = 128 * delta + qi - ki
        ok = (dist >= lo) & (dist <= hi) & (dist % dil == 0)
        return np.where(ok, 0.0, NEG)
    masks = np.zeros((NMASK, 128, 128), np.float32)
    for base, nrel, lo, hi, dil in ((MB_B, 3, 0, 127, 1), (MB_A0, 3, 0, 128, 1), (MB_A1, 6, 0, 512, 4),
                                    (MB_A2, 18, 0, 2048, 16), (MB_CW, 6, 0, 511, 1), (MB_SL, 2, 0, 1 << 30, 1)):
        for r in range(nrel):
            masks[base + r] = wmask(hf - 1 + r, lo, hi, dil)
    masks = np.ascontiguousarray(masks.transpose(1, 0, 2)).reshape(128, NMASK * 128)
    cmpmask = np.zeros((NQT, 128, 2, 128), np.float32)
    force = np.zeros((NQT, 128, 2, 128), np.float32)
    for it in range(NQT):
        Tq = 2 * it + hf
        t = Tq * 128 + np.arange(128)
        for rc in range(2):
            c = Tq // 16 - rc
            n = c * 128 + np.arange(128)
            ok = (16 * n[:, None] + 31) <= t[None, :]
            cmpmask[it, :, rc, :] = np.where(ok, 0.0, NEG)
        cur = t // 64
        j = np.arange(128)[None, :]
        forced = (j == 0) | (j == cur[:, None]) | (j == cur[:, None] - 1)
        force[it, :, 0, :] = np.where(forced, BIGF, -BIGF)
        force[it, :, 1, :] = np.where(j <= cur[:, None], BIGF, -BIGF)
    n = np.arange(NCT * 128)
    j = np.arange(128)
    ov = ((16 * n[:, None] <= 64 * j[None, :] + 63) & (16 * n[:, None] + 31 >= 64 * j[None, :]) & (n[:, None] < ncp))
    ov = np.ascontiguousarray(ov.reshape(NCT, 128, 128).transpose(1, 0, 2)).reshape(128, NCT * 128)
    tb = np.broadcast_to((1e-30 * (128 - np.arange(128)))[None, :], (128, 128))
    pp = np.arange(128)[:, None, None]
    t16 = np.arange(32)[None, :, None]
    kk = np.arange(128)[None, None, :]
    e32 = ((pp % 64) == 2 * t16 + (kk >= 64)).astype(np.float32).reshape(128, 4096)
    return {
        'e32': _bf(e32),
        'qpos': _bf(qpos), 'kpos': kpos, 'kposc': kposc, 'masks': _bf(masks),
        'cmpmask': _bf(cmpmask.reshape(NQT, 128, 256)), 'force': force.reshape(NQT, 128, 256).astype(np.float32),
        'ov': _bf(ov), 'tb': np.ascontiguousarray(tb).astype(np.float32),
        'ident': np.eye(128, dtype=ml_dtypes.bfloat16),
    }


CONST_SPECS = lambda NQT: {
    'qpos': ([7, N_QH, NQT * 128], BF16), 'kpos': ([7, 2 * NQT * 128], BF16),
    'kposc': ([7, ((2 * NQT * 128 - 32) // 16 + 1 + 127) // 128 * 128], BF16),
    'masks': ([128, NMASK * 128], BF16), 'cmpmask': ([NQT, 128, 256], BF16), 'force': ([NQT, 128, 256], F32),
    'ov': ([128, ((2 * NQT * 128 - 32) // 16 + 1 + 127) // 128 * 128], BF16), 'tb': ([128, 128], F32),
    'ident': ([128, 128], BF16), 'e32': ([128, 4096], BF16),
}


def kv_rows(kvf_d, T, NT, r0=0, r1=128, c0=0, c1=2560):
    base = (T % 2) * NT + (T // 2) * 128
    return kvf_d[base + r0:base + r1, c0:c1]


def phase_cmp(cx, kvf_d, w1_b, w2_b, pos_d, kgain_d, ident_d, cmpk_d, cmpv_d, NQT):
    P, A = cx.P, cx.A
    A.reset()
    NT = NQT * 128
    NTL = 2 * NQT
    S = NTL * 128
    ncp = (S - 32) // 16 + 1
    NCT = (ncp + 127) // 128
    NCP = NCT * 128
    ident = A.alloc(128, BF16)
    kid = cx.key('ident')
    P.dma(ident, ident_d[:, :], writes=[kid])
    kgain = A.alloc(64, F32)
    kkg = cx.key('kgain')
    P.dma(kgain, kgain_d[:, :], writes=[kkg])
    TT2 = A.alloc(S, BF16)
    ktt = cx.key('tt2')
    stage = [A.alloc(128, BF16) for _ in range(4)]
    kstage = [cx.key('stage') for _ in range(4)]
    for s_ in range(4):
        P.op('pool', lambda g, o=stage[s_]: g.memset(o, 0.0), writes=[kstage[s_]])
    w1 = A.alloc(16 * 256, BF16)
    kw1 = cx.key('w1')
    w2 = A.alloc(128, BF16)
    kw2 = cx.key('w2')
    posf = A.alloc(16, F32)
    posb = A.alloc(16, BF16)
    kpos = cx.key('pos')
    bvec = A.alloc(2, F32)
    kbv = cx.key('bvec')
    HT = A.alloc(2 * NCP, BF16)
    kht = cx.key('HT')
    P.op('pool', lambda g, o=HT: g.memset(o, 0.0), writes=[kht])
    sq = A.alloc(64, F32)
    st = A.alloc(8, F32)
    tmp = A.alloc(64, F32)
    ob = [A.alloc(64, BF16) for _ in range(2)]
    kob = [cx.key('ob') for _ in range(2)]
    kfin = cx.key('fin')
    io = 0
    ist = 0
    for kind in range(2):
        col0 = KV_CKC if kind == 0 else KV_CVC
        P.dma(w1, w1_b[kind * 128:(kind + 1) * 128, :], writes=[kw1])
        P.dma(w2, w2_b[kind * 128:(kind + 1) * 128, :], writes=[kw2])
        P.dma(posf, pos_d[kind * 128:(kind + 1) * 128, :], writes=[kpos + 'f'])
        P.op('dve', lambda g, o=posb, a=posf: g.tensor_copy(o, a), reads=[kpos + 'f'], writes=[kpos])
        for hc in range(2):
            for c in range(16):
                P.op('pe', lambda g, o=cx.bank(2)[:, hc:hc + 1], w=w1[:, c * 256 + hc * 128:c * 256 + (hc + 1) * 128], a=posb[:, c:c + 1], k=c, h=hc:
                     g.matmul(o, w, a, start=(k == 0 and h == 0), stop=(k == 15), skip_group_check=True),
                     reads=[kw1, kpos], writes=[('bank', 2)])
        P.op('dve', lambda g, o=bvec, a=cx.bank(2)[:, 0:2]: g.tensor_copy(o, a), reads=[('bank', 2)], writes=[kbv])
        for gi in range(2):
            col = col0 + 64 * gi
            for T in range(NTL):
                sg_ = ist % 4
                ist += 1
                P.dma(stage[sg_][:, 0:64], kv_rows(kvf_d, T, NT, 0, 128, col, col + 64), writes=[kstage[sg_]])
                P.dma(stage[sg_][0:127, 64:128], kv_rows(kvf_d, T, NT, 1, 128, col, col + 64), writes=[kstage[sg_] + 'b'])
                rd = [kstage[sg_], kstage[sg_] + 'b']
                if T + 1 < NTL:
                    P.dma(stage[sg_][127:128, 64:128], kv_rows(kvf_d, T + 1, NT, 0, 1, col, col + 64), writes=[kstage[sg_] + 'c'])
                    rd.append(kstage[sg_] + 'c')
                bnk = T % 2
                P.op('pe', lambda g, o=cx.bank_bf(bnk)[:, 0:128], a=stage[sg_]: g.transpose(o, a, ident), reads=rd + [kid], writes=[('bank', bnk)])
                eng = 'act' if T % 2 == 0 else 'dve'
                if eng == 'act':
                    P.op('act', lambda g, o=TT2[:, T * 128:(T + 1) * 128], a=cx.bank_bf(bnk)[:, 0:128]: g.copy(o, a),
                         reads=[('bank', bnk)], writes=[ktt, kstage[sg_], kstage[sg_] + 'b', kstage[sg_] + 'c'])
                else:
                    P.op('dve', lambda g, o=TT2[:, T * 128:(T + 1) * 128], a=cx.bank_bf(bnk)[:, 0:128]: g.tensor_copy(o, a),
                         reads=[('bank', bnk)], writes=[ktt, kstage[sg_], kstage[sg_] + 'b', kstage[sg_] + 'c'])
            TT3 = TT2.rearrange('p (n s) -> p n s', s=16)
            for hc in range(2):
                for n0 in range(0, ncp, 512):
                    n1 = min(ncp, n0 + 512)
                    bnk = 3 + hc
                    for c in range(16):
                        P.op('pe', lambda g, o=cx.bank(bnk)[:, 0:n1 - n0], w=w1[:, c * 256 + hc * 128:c * 256 + (hc + 1) * 128],
                             a=TT3[:, n0 + (2 * c) // 16:n1 + (2 * c) // 16, (2 * c) % 16], k=c: g.matmul(o, w, a, start=(k == 0), stop=(k == 15)),
                             reads=[kw1, ktt], writes=[('bank', bnk)])
                    P.op('act', lambda g, o=HT[:, hc * NCP + n0:hc * NCP + n1], a=cx.bank(bnk)[:, 0:n1 - n0], b=bvec[:, hc:hc + 1]:
                         g.activation(o, a, AF.Silu, bias=b), reads=[('bank', bnk), kbv], writes=[kht])
            for j in range(NCT):
                bnk = 5 + j % 2
                for hc in range(2):
                    P.op('pe', lambda g, o=cx.bank(bnk)[:, 0:64], a=HT[:, hc * NCP + j * 128:hc * NCP + (j + 1) * 128], w=w2[:, hc * 64:(hc + 1) * 64], h=hc:
                         g.matmul(o, a, w, start=(h == 0), stop=(h == 1)), reads=[kht, kw2], writes=[('bank', bnk)])
                o_ = io % 2
                io += 1
                ps = cx.bank(bnk)[:, 0:64]
                if kind == 0:
                    P.op('act', lambda g, o=sq, a=ps, acc=st[:, 0:1]: g.activation(o, a, AF.Square, accum_out=acc), reads=[('bank', bnk)], writes=[kfin + 'a'])
                    P.op('dve', lambda g, o=st[:, 1:2], a=st[:, 0:1]: g.tensor_scalar(o, a, 1.0 / HD, RMS_EPS, ALU.mult, ALU.add), reads=[kfin + 'a'], writes=[kfin + 'b'])
                    P.op('act', lambda g, o=st[:, 2:3], a=st[:, 1:2]: g.sqrt(o, a), reads=[kfin + 'b'], writes=[kfin + 'c'])
                    P.op('dve', lambda g, o=st[:, 3:4], a=st[:, 2:3]: g.reciprocal(o, a), reads=[kfin + 'c'], writes=[kfin + 'd'])
                    P.op('dve', lambda g, o=tmp, a=ps, r=st[:, 3:4]: g.tensor_scalar(o, a, r, None, ALU.mult), reads=[('bank', bnk), kfin + 'd'], writes=[kfin + 'e'])
                    P.op('dve', lambda g, o=ob[o_], a=tmp, gn=kgain: g.tensor_tensor(o, a, gn, ALU.mult), reads=[kfin + 'e', kkg], writes=[kob[o_]])
                    P.dma(cmpk_d[j * 128:(j + 1) * 128, gi * 64:(gi + 1) * 64], ob[o_], reads=[kob[o_]])
                else:
                    P.op('act', lambda g, o=ob[o_], a=ps: g.copy(o, a), reads=[('bank', bnk)], writes=[kob[o_]])
                    P.dma(cmpv_d[j * 128:(j + 1) * 128, gi * 64:(gi + 1) * 64], ob[o_], reads=[kob[o_]])
    P.barrier()


def phase_attn(cx, qn_d, kvf_d, cmpk_d, cmpv_d, cg_d, sinks_d, C, ot_d, NQT):
    P, A = cx.P, cx.A
    A.reset()
    NT = NQT * 128
    NTL = 2 * NQT
    S = NTL * 128
    ncp = (S - 32) // 16 + 1
    NCT = (ncp + 127) // 128
    VW = 66
    GROUPS = [('A0', 4, KV_AK, KV_AV, 3, 5, 0, 1, MB_A0), ('A1', 4, KV_AK + 256, KV_AV + 256, 6, 8, 4, 1, MB_A1),
              ('A2', 4, KV_AK + 512, KV_AV + 512, 18, 20, 8, 1, MB_A2), ('B', 2, KV_BK, KV_BV, 3, 5, 12, 4, MB_B),
              ('CW', 2, KV_CKW, KV_CVW, 6, 8, 20, 6, MB_CW), ('SL', 2, KV_CKS, KV_CVS, NTL, NTL, 20, 6, MB_SL)]
    GI = {g[0]: g for g in GROUPS}

    def load_const(name, cols, dtype, src):
        t = A.alloc(cols, dtype)
        k = cx.key(name)
        P.dma(t, src, writes=[k])
        return t, k
    ident, kid = load_const('ident', 128, BF16, C['ident'][:, :])
    masks, kmk = load_const('masks', NMASK * 128, BF16, C['masks'][:, :])
    ov, kov = load_const('ov', NCT * 128, BF16, C['ov'][:, :])
    tb, ktb = load_const('tb', 128, F32, C['tb'][:, :])
    e32, ke32 = load_const('e32', 4096, BF16, C['e32'][:, :])
    esink, kes = load_const('esink', 8, F32, sinks_d[:, :])
    P.op('act', lambda g, o=esink: g.activation(o, o, AF.Exp), reads=[kes], writes=[kes])
    ident3 = lambda R: ident.unsqueeze(1).to_broadcast([128, R, 128])
    kt, vc, kK, kV = {}, {}, {}, {}
    for (name, ns, kcol, vcol, nrel, ring, qh0, R, mb) in GROUPS + [('CMP', 2, 0, 0, NCT, NCT, 20, 6, 0)]:
        kt[name] = A.alloc(ring * ns * 128, BF16)
        vc[name] = A.alloc(ring * ns * VW, BF16)
        kK[name] = [cx.key('K' + name) for _ in range(ring)]
        kV[name] = [cx.key('V' + name) for _ in range(ring)]
        P.op('pool', lambda g, o=vc[name]: g.memset(o, 1.0), writes=kV[name])

    def Kap(name, pos, s):
        ns = 2 if name == 'CMP' else GI[name][1]
        return kt[name][0:71, (pos * ns + s) * 128:(pos * ns + s + 1) * 128]

    def Vap(name, pos, s):
        ns = 2 if name == 'CMP' else GI[name][1]
        return vc[name][:, (pos * ns + s) * VW:(pos * ns + s) * VW + 65]
    QaT = [A.alloc(N_QH * 128, BF16) for _ in range(2)]
    kQ = [cx.key('QaT') for _ in range(2)]
    qst = [A.alloc(2048, BF16) for _ in range(2)]
    kqst = [cx.key('qst') for _ in range(2)]
    kvst = [A.alloc(2560, BF16) for _ in range(2)]
    kkvst = [cx.key('kvst') for _ in range(2)]
    NPT = 4
    PT = [A.alloc(512, BF16) for _ in range(NPT)]
    kPT = [cx.key('PT') for _ in range(NPT)]
    cgt = [A.alloc(36, F32) for _ in range(2)]
    kcg = [cx.key('cg') for _ in range(2)]
    frc = [A.alloc(256, F32) for _ in range(2)]
    kfrc = [cx.key('frc') for _ in range(2)]
    cmk = [A.alloc(256, BF16) for _ in range(2)]
    kcmk = [cx.key('cmk') for _ in range(2)]
    oall = [A.alloc(1536, BF16) for _ in range(2)]
    koall = [cx.key('oall') for _ in range(2)]
    oTs = A.alloc(1536, BF16)
    koTs = cx.key('oTs')
    oc = A.alloc(768, F32)
    koc = [cx.key('oc') for _ in range(4)]
    tmp3 = [A.alloc(192, F32) for _ in range(2)]
    ktmp3 = [cx.key('tmp3') for _ in range(2)]
    sm = [A.alloc(16, F32) for _ in range(4)]
    ksm = [cx.key('sm') for _ in range(4)]
    imp = [A.alloc(128, F32) for _ in range(2)]
    kimp = [cx.key('imp') for _ in range(2)]
    wk = [A.alloc(128, F32) for _ in range(3)]
    kwk = cx.key('wk')
    m8 = A.alloc(16, F32)
    selm = [A.alloc(128, BF16) for _ in range(2)]
    kselm = [cx.key('selm') for _ in range(2)]
    selmT = [A.alloc(128, BF16) for _ in range(2)]
    kselmT = [cx.key('selmT') for _ in range(2)]
    ism = [0]
    itp = [0]
    tpbank = lambda: (itp.__setitem__(0, itp[0] + 1), (itp[0] % 2))[1]

    cst = [A.alloc(128, BF16) for _ in range(2)]
    kcst = [cx.key('cst') for _ in range(2)]
    for c in range(NCT):
        s_ = c % 2
        P.dma(cst[s_], cmpk_d[c * 128:(c + 1) * 128, :], writes=[kcst[s_]])
        b = tpbank()
        for g_ in range(2):
            P.op('pe', lambda g, o=cx.bank_bf(b)[0:64, g_ * 128:(g_ + 1) * 128], a=cst[s_][:, g_ * 64:(g_ + 1) * 64]: g.transpose(o, a, ident),
                 reads=[kcst[s_], kid], writes=[('bank', b)])
        P.op('dve', lambda g, o=kt['CMP'][0:64, c * 256:(c + 1) * 256], a=cx.bank_bf(b)[0:64, 0:256]: g.tensor_copy(o, a),
             reads=[('bank', b)], writes=[kK['CMP'][c]])
        P.dma(kt['CMP'][64:71, c * 256:(c + 1) * 256].rearrange('p (s t) -> p s t', s=2),
              C['kposc'][:, c * 128:(c + 1) * 128].unsqueeze(1).to_broadcast([7, 2, 128]), writes=[kK['CMP'][c] + 'p'])
        P.dma(vc['CMP'][:, c * 2 * VW:(c + 1) * 2 * VW].rearrange('p (s w) -> p s w', w=VW)[:, :, 0:64],
              cmpv_d[c * 128:(c + 1) * 128, :].rearrange('p (s d) -> p s d', d=64), reads=[kV['CMP'][c]], writes=[kV['CMP'][c] + 'd'])

    def prep(i):
        for T in (2 * i, 2 * i + 1):
            s_ = T % 2
            P.dma(kvst[s_], kv_rows(kvf_d, T, NT), writes=[kkvst[s_]])
            for (name, ns, kcol, vcol, nrel, ring, qh0, R, mb) in GROUPS:
                pos = T % ring
                b = tpbank()
                for s2 in range(ns):
                    P.op('pe', lambda g, o=cx.bank_bf(b)[0:64, s2 * 128:(s2 + 1) * 128], a=kvst[s_][:, kcol + s2 * 64:kcol + (s2 + 1) * 64]:
                         g.transpose(o, a, ident), reads=[kkvst[s_], kid], writes=[('bank', b)])
                dst = kt[name][0:64, pos * ns * 128:(pos + 1) * ns * 128]
                if b == 0:
                    P.op('act', lambda g, o=dst, a=cx.bank_bf(b)[0:64, 0:ns * 128]: g.copy(o, a), reads=[('bank', b)], writes=[kK[name][pos]])
                else:
                    P.op('dve', lambda g, o=dst, a=cx.bank_bf(b)[0:64, 0:ns * 128]: g.tensor_copy(o, a), reads=[('bank', b)], writes=[kK[name][pos]])
                P.dma(kt[name][64:71, pos * ns * 128:(pos + 1) * ns * 128].rearrange('p (s t) -> p s t', s=ns),
                      C['kpos'][:, T * 128:(T + 1) * 128].unsqueeze(1).to_broadcast([7, ns, 128]),
                      reads=[kK[name][pos]], writes=[kK[name][pos] + 'p'])
                P.op('pool', lambda g, o=vc[name][:, pos * ns * VW:(pos + 1) * ns * VW].rearrange('p (s w) -> p s w', w=VW)[:, :, 0:64],
                     a=kvst[s_][:, vcol:vcol + ns * 64].rearrange('p (s d) -> p s d', d=64): g.tensor_copy(o, a),
                     reads=[kkvst[s_], kV[name][pos]], writes=[kV[name][pos] + 'd'])
        qb = i % 2
        P.dma(qst[qb], qn_d[i * 128:(i + 1) * 128, :], writes=[kqst[qb]])
        for h0 in range(0, N_QH, 8):
            b = tpbank()
            for h in range(8):
                P.op('pe', lambda g, o=cx.bank_bf(b)[0:64, h * 128:(h + 1) * 128], a=qst[qb][:, (h0 + h) * 64:(h0 + h + 1) * 64]:
                     g.transpose(o, a, ident), reads=[kqst[qb], kid], writes=[('bank', b)])
            if b == 0:
                P.op('act', lambda g, o=QaT[qb][0:64, h0 * 128:(h0 + 8) * 128], a=cx.bank_bf(b)[0:64, :]: g.copy(o, a), reads=[('bank', b)], writes=[kQ[qb]])
            else:
                P.op('dve', lambda g, o=QaT[qb][0:64, h0 * 128:(h0 + 8) * 128], a=cx.bank_bf(b)[0:64, :]: g.tensor_copy(o, a), reads=[('bank', b)], writes=[kQ[qb]])
        P.dma(QaT[qb][64:71, :].rearrange('p (h t) -> p h t', h=N_QH), C['qpos'][:, :, i * 128:(i + 1) * 128], reads=[kQ[qb]], writes=[kQ[qb] + 'p'])
        P.dma(cgt[qb], cg_d[i * 128:(i + 1) * 128, :], writes=[kcg[qb]])
        P.dma(frc[qb], C['force'][i], writes=[kfrc[qb]])
        P.dma(cmk[qb], C['cmpmask'][i], writes=[kcmk[qb]])

    def kdeps(name, pos):
        return [kK[name][pos], kK[name][pos] + 'p']

    def vdeps(name, pos):
        return [kV[name][pos], kV[name][pos] + 'd']

    def run_tile(i):
        qb = i % 2
        Tq_c = i // 8
        cgv = cgt[qb].rearrange('p (h b) -> p h b', b=3)
        steps = []
        rounds = []

        def add_round(kind, units, **kw):
            rd = dict(kind=kind, nsteps=0, **kw)
            rd['acc'] = 6 + len(rounds) % 2
            rounds.append(rd)
            for u in units:
                for tl in u['tiles']:
                    steps.append((rd, u, tl))
                    rd['nsteps'] += 1
        for g_ in range(2):
            for half in range(2):
                tiles = []
                for c in range(0, Tq_c + 1):
                    m = []
                    if c == Tq_c:
                        m = [('w', cmk[qb][:, 0:128], [kcmk[qb]])]
                    elif c == Tq_c - 1:
                        m = [('w', cmk[qb][:, 128:256], [kcmk[qb]])]
                    tiles.append(dict(K=Kap('CMP', c, g_), V=Vap('CMP', c, g_), kd=kdeps('CMP', c), vd=vdeps('CMP', c), masks=m, ov=c))
                add_round('CMP', [dict(q0=20 + 6 * g_ + 3 * half, R=3, slot=0, tiles=tiles)], g=g_, half=half, br=0)
        units = []
        for gn in ('A0', 'A1', 'A2'):
            (name, ns, kcol, vcol, nrel, ring, qh0, R, mb) = GI[gn]
            for h in range(4):
                tiles = []
                for r in range(nrel):
                    T = 2 * i + 1 - r
                    if T < 0:
                        continue
                    pos = T % ring
                    tiles.append(dict(K=Kap(name, pos, h), V=Vap(name, pos, h), kd=kdeps(name, pos), vd=vdeps(name, pos),
                                      masks=[('w', masks[:, (mb + r) * 128:(mb + r + 1) * 128], [kmk])]))
                units.append(dict(q0=qh0 + h, R=1, slot=h, tiles=tiles))
        add_round('A', units)
        for j in range(2):
            (name, ns, kcol, vcol, nrel, ring, qh0, R, mb) = GI['B']
            tiles = []
            for r in range(nrel):
                T = 2 * i + 1 - r
                if T < 0:
                    continue
                pos = T % ring
                tiles.append(dict(K=Kap(name, pos, j), V=Vap(name, pos, j), kd=kdeps(name, pos), vd=vdeps(name, pos),
                                  masks=[('w', masks[:, (mb + r) * 128:(mb + r + 1) * 128], [kmk])]))
            add_round('B', [dict(q0=qh0 + 4 * j, R=4, slot=0, tiles=tiles)], j=j)
        for gname, br in (('CW', 2), ('SL', 1)):
            (name, ns, kcol, vcol, nrel, ring, qh0, R, mb) = GI[gname]
            for g_ in range(2):
                for half in range(2):
                    tiles = []
                    rels = range(nrel) if gname == 'CW' else range(2 * i + 1, -1, -1)
                    for r in rels:
                        T = 2 * i + 1 - r
                        if T < 0:
                            continue
                        pos = T % ring
                        m = []
                        if gname == 'CW':
                            m = [('w', masks[:, (mb + r) * 128:(mb + r + 1) * 128], [kmk])]
                        else:
                            m = [('s', (T, selmT[g_]), [kselmT[g_]])]
                            if r < 2:
                                m.append(('w', masks[:, (mb + r) * 128:(mb + r + 1) * 128], [kmk]))
                        tiles.append(dict(K=Kap(name, pos, g_), V=Vap(name, pos, g_), kd=kdeps(name, pos), vd=vdeps(name, pos), masks=m))
                    add_round(gname, [dict(q0=qh0 + 6 * g_ + 3 * half, R=3, slot=0, tiles=tiles)], g=g_, half=half, br=br)

        def emit_qk(k):
            rd, u, tl = steps[k]
            R = u['R']
            sb = 2 + k % 3
            Sv = cx.bank(sb)[:, 0:R * 128]
            S3 = Sv.rearrange('p (r q) -> p r q', r=R)
            nm = len(tl['masks'])
            P.op('pe', lambda g, o=Sv, kk=tl['K'], q=QaT[qb][0:71, u['q0'] * 128:(u['q0'] + R) * 128], last=(nm == 0):
                 g.matmul(o, kk, q, start=True, stop=last), reads=tl['kd'] + [kQ[qb], kQ[qb] + 'p'], writes=[('bank', sb)])
            for mi, (mk, ap, deps) in enumerate(tl['masks']):
                last = (mi == nm - 1)
                if mk == 'w':
                    P.op('pe', lambda g, o=S3, a=ap.unsqueeze(1).to_broadcast([128, R, 128]), last=last:
                         g.matmul(o, ident, a, start=False, stop=last), reads=deps + [kid], writes=[('bank', sb)])
                else:
                    T_, sT = ap
                    q4, t16 = T_ // 32, T_ % 32
                    P.op('pe', lambda g, o=S3, w=e32[q4 * 64:(q4 + 1) * 64, t16 * 128:(t16 + 1) * 128],
                         a=sT[q4 * 64:(q4 + 1) * 64, :].unsqueeze(1).to_broadcast([64, R, 128]), last=last:
                         g.matmul(o, w, a, start=False, stop=last), reads=deps + [ke32], writes=[('bank', sb)])
            pt = k % NPT
            P.op('act', lambda g, o=PT[pt][:, 0:R * 128], a=Sv: g.activation(o, a, AF.Exp), reads=[('bank', sb)], writes=[kPT[pt]])

        def emit_pv(k):
            rd, u, tl = steps[k]
            R = u['R']
            pt = k % NPT
            ab = rd['acc']
            for r in range(R):
                first = not rd.get('acc_started', False)
                rd['acc_started'] = True
                sl_ = u['slot'] + r
                P.op('pe', lambda g, o=cx.bank(ab)[:, sl_ * 65:(sl_ + 1) * 65], p=PT[pt][:, r * 128:(r + 1) * 128], v=tl['V'], first=first:
                     g.matmul(o, p, v, start=first, stop=True, skip_group_check=True), reads=[kPT[pt]] + tl['vd'], writes=[('bank', ab)])
            if rd['kind'] == 'CMP':
                for r in range(R):
                    first = not rd.get('u_started', False)
                    rd['u_started'] = True
                    P.op('pe', lambda g, o=cx.bank(5)[:, r * 128:(r + 1) * 128], p=PT[pt][:, r * 128:(r + 1) * 128],
                         w=ov[:, tl['ov'] * 128:(tl['ov'] + 1) * 128], first=first:
                         g.matmul(o, p, w, start=first, stop=True, skip_group_check=True), reads=[kPT[pt], kov], writes=[('bank', 5)])
            rd['nsteps'] -= 1
            if rd['nsteps'] == 0:
                finalize(rd)

        def finalize(rd):
            ab = rd['acc']
            kind = rd['kind']
            ob_ = oall[qb]
            s_ = ism[0] % 4
            ism[0] += 1
            smt, ks_ = sm[s_], ksm[s_]
            if kind in ('A', 'B'):
                accv = cx.bank(ab)[:, 0:260].rearrange('p (h c) -> p h c', c=65)
                if kind == 'A':
                    P.op('dve', lambda g, o=smt[:, 0:4], a=accv[:, :, 64]: g.reciprocal(o, a), reads=[('bank', ab)], writes=[ks_])
                    col = 0
                else:
                    j = rd['j']
                    P.op('dve', lambda g, o=smt[:, 4:8], a=accv[:, :, 64], e=esink[:, 4 * j:4 * j + 4]: g.tensor_tensor(o, a, e, ALU.add),
                         reads=[('bank', ab), kes], writes=[ks_ + 'a'])
                    P.op('dve', lambda g, o=smt[:, 0:4], a=smt[:, 4:8]: g.reciprocal(o, a), reads=[ks_ + 'a'], writes=[ks_])
                    col = 256 + 256 * j
                P.op('dve', lambda g, o=ob_[:, col:col + 256].rearrange('p (h d) -> p h d', d=64), a=accv[:, :, 0:64],
                     r=smt[:, 0:4].unsqueeze(2).to_broadcast([128, 4, 64]): g.tensor_tensor(o, a, r, ALU.mult),
                     reads=[('bank', ab), ks_], writes=[koall[qb]])
                return
            g_, half, br = rd['g'], rd['half'], rd['br']
            hh0 = 6 * g_ + 3 * half
            ui = 2 * g_ + half
            accv = cx.bank(ab)[:, 0:195].rearrange('p (h c) -> p h c', c=65)
            P.op('dve', lambda g, o=smt[:, 0:3], a=accv[:, :, 64]: g.tensor_scalar(o, a, 1e-30, None, ALU.max), reads=[('bank', ab)], writes=[ks_ + 'a'])
            P.op('dve', lambda g, o=smt[:, 4:7], a=smt[:, 0:3]: g.reciprocal(o, a), reads=[ks_ + 'a'], writes=[ks_ + 'b'])
            P.op('dve', lambda g, o=smt[:, 8:11], a=smt[:, 4:7], c=cgv[:, hh0:hh0 + 3, br]: g.tensor_tensor(o, a, c, ALU.mult),
                 reads=[ks_ + 'b', kcg[qb]], writes=[ks_ + 'c'])
            t3 = ism[0] % 2
            ocv = oc[:, hh0 * 64:(hh0 + 3) * 64]
            if kind == 'CMP':
                P.op('dve', lambda g, o=ocv.rearrange('p (h d) -> p h d', d=64), a=accv[:, :, 0:64],
                     w=smt[:, 8:11].unsqueeze(2).to_broadcast([128, 3, 64]): g.tensor_tensor(o, a, w, ALU.mult),
                     reads=[('bank', ab), ks_ + 'c'], writes=[koc[ui]])
                for r in range(3):
                    src = tb if (half == 0 and r == 0) else imp[g_]
                    P.op('dve', lambda g, o=imp[g_], u=cx.bank(5)[:, r * 128:(r + 1) * 128], s=smt[:, 4 + r:5 + r], a=src:
                         g.scalar_tensor_tensor(o, u, s, a, ALU.mult, ALU.add), reads=[('bank', 5), ks_ + 'b', ktb, kimp[g_]], writes=[kimp[g_]])
                if half == 1:
                    P.op('dve', lambda g, o=wk[0], a=imp[g_], f=frc[qb][:, 0:128]: g.tensor_tensor(o, a, f, ALU.max), reads=[kimp[g_], kfrc[qb]], writes=[kwk + '0'])
                    P.op('dve', lambda g, o=wk[1], a=wk[0], f=frc[qb][:, 128:256]: g.tensor_tensor(o, a, f, ALU.min), reads=[kwk + '0', kfrc[qb]], writes=[kwk + '1'])
                    P.op('dve', lambda g, o=m8[:, 0:8], a=wk[1]: g.max(o, a), reads=[kwk + '1'], writes=[kwk + 'm'])
                    P.op('dve', lambda g, o=wk[2], a=m8[:, 0:8], b=wk[1]: g.match_replace(o, a, b, -3.0e38), reads=[kwk + 'm', kwk + '1'], writes=[kwk + '2'])
                    P.op('dve', lambda g, o=m8[:, 8:16], a=wk[2]: g.max(o, a), reads=[kwk + '2'], writes=[kwk + 'n'])
                    P.op('dve', lambda g, o=wk[0], a=wk[1], t=m8[:, 15:16]: g.tensor_scalar(o, a, t, None, ALU.is_ge), reads=[kwk + '1', kwk + 'n'], writes=[kwk + '0'])
                    P.op('dve', lambda g, o=selm[g_], a=wk[0]: g.tensor_scalar(o, a, -1.0, -NEG, ALU.add, ALU.mult), reads=[kwk + '0'], writes=[kselm[g_]])
                    tb_ = tpbank()
                    P.op('pe', lambda g, o=cx.bank_bf(tb_)[:, 0:128], a=selm[g_]: g.transpose(o, a, ident), reads=[kselm[g_], kid], writes=[('bank', tb_)])
                    P.op('act', lambda g, o=selmT[g_], a=cx.bank_bf(tb_)[:, 0:128]: g.copy(o, a), reads=[('bank', tb_)], writes=[kselmT[g_]])
            else:
                P.op('dve', lambda g, o=tmp3[t3].rearrange('p (h d) -> p h d', d=64), a=accv[:, :, 0:64],
                     w=smt[:, 8:11].unsqueeze(2).to_broadcast([128, 3, 64]): g.tensor_tensor(o, a, w, ALU.mult),
                     reads=[('bank', ab), ks_ + 'c'], writes=[ktmp3[t3]])
                if kind == 'CW':
                    P.op('pool', lambda g, o=ocv, a=ocv, b=tmp3[t3]: g.tensor_tensor(o, a, b, ALU.add), reads=[koc[ui], ktmp3[t3]], writes=[koc[ui]])
                else:
                    P.op('pool', lambda g, o=ob_[:, 768 + hh0 * 64:768 + (hh0 + 3) * 64], a=ocv, b=tmp3[t3]: g.tensor_tensor(o, a, b, ALU.add),
                         reads=[koc[ui], ktmp3[t3]], writes=[koall[qb]])

        LOOK = 2
        n = len(steps)
        for k in range(n + LOOK):
            if k < n:
                emit_qk(k)
            if k >= LOOK:
                emit_pv(k - LOOK)
        for f0, nf, b in ((0, 8, 0), (8, 4, 1)):
            for f in range(nf):
                P.op('pe', lambda g, o=cx.bank_bf(b)[:, f * 128:(f + 1) * 128], a=oall[qb][:, (f0 + f) * 128:(f0 + f + 1) * 128]: g.transpose(o, a, ident),
                     reads=[koall[qb], kid], writes=[('bank', b)])
            P.op('act', lambda g, o=oTs[:, f0 * 128:(f0 + nf) * 128], a=cx.bank_bf(b)[:, 0:nf * 128]: g.copy(o, a), reads=[('bank', b)], writes=[koTs])
        P.dma(ot_d.rearrange('f p t -> p f t')[:, :, i * 128:(i + 1) * 128], oTs.rearrange('p (f t) -> p f t', f=12), reads=[koTs])

    prep(0)
    for i in range(NQT):
        if i + 1 < NQT:
            prep(i + 1)
        run_tile(i)
    P.barrier()


BR_FB = ((0, 2), (2, 6), (6, 12))


def phase_out(cx, x_in, x_out, gcol_d, wgate_b, wbr_b, wout_b, ot_d, NT, ident_d, TT=512):
    P, A = cx.P, cx.A
    A.reset()
    nsub = TT // 128
    ident = A.alloc(128, BF16)
    kid = cx.key('ident')
    gcol = A.alloc(KC, F32)
    kg = cx.key('gcol')
    P.dma(ident, ident_d[:, :], writes=[kid])
    P.dma(gcol, gcol_d[:, :], writes=[kg])
    xnT = A.alloc(KC * TT, BF16)
    kxn = cx.key('xnT')
    mT = A.alloc(KC * TT, BF16)
    kmT = [cx.key('mT') for _ in range(KC)]
    oTt = A.alloc(12 * TT, BF16)
    koT = cx.key('oTt')
    xs = [A.alloc(D, F32) for _ in range(2)]
    kxs = [cx.key('xs') for _ in range(2)]
    xb = [A.alloc(D, BF16) for _ in range(2)]
    kxb = [cx.key('xb') for _ in range(2)]
    junk = A.alloc(D, BF16)
    kjunk = cx.key('junk')
    st = [A.alloc(8, F32) for _ in range(2)]
    kst = [cx.key('st') for _ in range(2)]
    NG = 4
    wg = [A.alloc(KC * 128, BF16) for _ in range(NG)]
    kwg = [cx.key('wg') for _ in range(NG)]
    wbr = [A.alloc(1536, BF16) for _ in range(2)]
    kwbr = [cx.key('wbr') for _ in range(2)]
    wo = [A.alloc(D, BF16) for _ in range(3)]
    kwo = [cx.key('wo') for _ in range(3)]
    sg = [A.alloc(TT, F32) for _ in range(2)]
    ksg = [cx.key('sg') for _ in range(2)]
    macc = [A.alloc(TT, F32) for _ in range(2)]
    kmacc = [cx.key('macc') for _ in range(2)]
    tmpm = [A.alloc(TT, F32) for _ in range(2)]
    ktmpm = [cx.key('tmpm') for _ in range(2)]
    xr = [A.alloc(D, F32) for _ in range(2)]
    kxr = [cx.key('xr') for _ in range(2)]
    xo = [A.alloc(D, F32) for _ in range(2)]
    kxo = [cx.key('xo') for _ in range(2)]
    ig = 0
    iwo = 0
    ipair = 0
    for t0 in range(0, NT, TT):
        emit_norm_transpose(cx, x_in, t0, nsub, gcol, kg, xnT, kxn, xs, kxs, xb, kxb, junk, kjunk, st, kst, ident, 4)
        P.dma(oTt.rearrange('p (f t) -> p f t', f=12), ot_d.rearrange('f p t -> p f t')[:, :, t0:t0 + TT], writes=[koT])
        for dc in range(KC):
            wb = dc % 2
            P.dma(wbr[wb], wbr_b[dc * 128:(dc + 1) * 128, :], writes=[kwbr[wb]])
            ma = dc % 2
            for b in range(3):
                s = ig % NG
                ig += 1
                P.dma(wg[s], wgate_b[(b * KC + dc) * 128:(b * KC + dc + 1) * 128, :], writes=[kwg[s]])
                bg = ipair % 2
                bb = 2 + ipair % 2
                ipair += 1
                pg, pb = cx.bank(bg)[:, 0:TT], cx.bank(bb)[:, 0:TT]
                for kc in range(KC):
                    P.op('pe', lambda g, o=pg, w=wg[s][:, kc * 128:(kc + 1) * 128], a=xnT[:, kc * TT:(kc + 1) * TT], k=kc:
                         g.matmul(o, w, a, start=(k == 0), stop=(k == KC - 1)), reads=[kwg[s], kxn], writes=[('bank', bg)])
                f0, f1 = BR_FB[b]
                for fb in range(f0, f1):
                    P.op('pe', lambda g, o=pb, w=wbr[wb][:, fb * 128:(fb + 1) * 128], a=oTt[:, fb * TT:(fb + 1) * TT], f=fb, f0=f0, f1=f1:
                         g.matmul(o, w, a, start=(f == f0), stop=(f == f1 - 1)), reads=[kwbr[wb], koT], writes=[('bank', bb)])
                q = ipair % 2
                P.op('act', lambda g, o=sg[q], a=pg: g.activation(o, a, AF.Sigmoid), reads=[('bank', bg)], writes=[ksg[q]])
                if b == 0:
                    P.op('dve', lambda g, o=macc[ma], a=pb, c=sg[q]: g.tensor_tensor(o, a, c, ALU.mult), reads=[('bank', bb), ksg[q]], writes=[kmacc[ma]])
                else:
                    P.op('dve', lambda g, o=tmpm[q], a=pb, c=sg[q]: g.tensor_tensor(o, a, c, ALU.mult), reads=[('bank', bb), ksg[q]], writes=[ktmpm[q]])
                    if b == 1:
                        P.op('pool', lambda g, o=macc[ma], a=macc[ma], c=tmpm[q]: g.tensor_tensor(o, a, c, ALU.add), reads=[kmacc[ma], ktmpm[q]], writes=[kmacc[ma]])
                    else:
                        P.op('pool', lambda g, o=mT[:, dc * TT:(dc + 1) * TT], a=macc[ma], c=tmpm[q]: g.tensor_tensor(o, a, c, ALU.add),
                             reads=[kmacc[ma], ktmpm[q]], writes=[kmT[dc]])
        for s0 in range(0, nsub, 2):
            for dc in range(KC):
                s = iwo % 3
                iwo += 1
                P.dma(wo[s], wout_b[dc * 128:(dc + 1) * 128, :], writes=[kwo[s]])
                for ss in range(2):
                    for cb in range(4):
                        b = ss * 4 + cb
                        P.op('pe', lambda g, o=cx.bank(b), m=mT[:, dc * TT + (s0 + ss) * 128:dc * TT + (s0 + ss + 1) * 128],
                             w=wo[s][:, cb * 512:(cb + 1) * 512], d=dc: g.matmul(o, m, w, start=(d == 0), stop=(d == KC - 1)),
                             reads=[kwo[s], kmT[dc]], writes=[('bank', b)])
            for ss in range(2):
                tok = t0 + (s0 + ss) * 128
                q = ss
                P.dma(xr[q], x_in[tok:tok + 128, :], writes=[kxr[q]])
                for cb in range(4):
                    b = ss * 4 + cb
                    P.op('dve', lambda g, o=xo[q][:, cb * 512:(cb + 1) * 512], a=cx.bank(b), r=xr[q][:, cb * 512:(cb + 1) * 512]:
                         g.tensor_tensor(o, a, r, ALU.add), reads=[('bank', b), kxr[q]], writes=[kxo[q]])
                P.dma(x_out[tok:tok + 128, :], xo[q], reads=[kxo[q]])
    P.barrier()


def lay_wgate(w_in):
    g = w_in[:, GATE0:GATE0 + 3 * D]
    return np.ascontiguousarray(g.reshape(KC, 128, 3 * KC, 128).transpose(2, 1, 0, 3)).reshape(3 * KC * 128, KC * 128)


def lay_wbr(wa, wb, wc):
    w = np.concatenate([wa, wb, wc], axis=0)
    return np.ascontiguousarray(w.reshape(12, 128, KC, 128).transpose(2, 1, 0, 3)).reshape(KC * 128, 1536)


def lay_cmp_w1(w1):
    return np.ascontiguousarray(w1.reshape(2, 16, 128, 256).transpose(0, 2, 1, 3)).reshape(2 * 128, 16 * 256)


def lay_cmp_w2(w2):
    return np.ascontiguousarray(w2.reshape(2, 2, 128, 64).transpose(0, 2, 1, 3)).reshape(2 * 128, 128)


def lay_cmp_pos(pos):
    return np.ascontiguousarray(pos.reshape(2, 16, 2, 64).transpose(0, 2, 3, 1)).reshape(2 * 128, 16)


NCORES = 8
SEQ = 8192
BATCH = 4
NQT_FULL = SEQ // 256
NT_FULL = NQT_FULL * 128


def _din(nc, name, shape, dt):
    return nc.dram_tensor(name, list(shape), dt, kind="ExternalInput").ap()


def _dout(nc, name, shape, dt):
    return nc.dram_tensor(name, list(shape), dt, kind="ExternalOutput").ap()


def _dint(nc, name, shape, dt):
    return nc.dram_tensor(name, list(shape), dt, kind="Internal").ap()


def build_prog_a(NT):
    nc = bass.Bass("TRN2", target_bir_lowering=False)
    x = _din(nc, "x", [NT, D], F32)
    g1 = _din(nc, "g1", [128, KC], F32)
    wgu = _din(nc, "wgu", [FC * 128, KC * 256], F32)
    wdn = _din(nc, "wdn", [DFF, D], F32)
    gm = _din(nc, "gm", [128, KC], F32)
    win = _din(nc, "win", [NB1 * 128, KC * 512], F32)
    gain = _din(nc, "gain", [128, 3200], F32)
    ident = _din(nc, "ident", [128, 128], BF16)
    x1 = _dout(nc, "x1", [NT, D], F32)
    qn = _dout(nc, "qn", [NT, 2048], BF16)
    kv = _dout(nc, "kv", [NT, 2560], BF16)
    cg = _dout(nc, "cg", [NT, 36], F32)
    wgu_b = _dint(nc, "wgu_b", [FC * 128, KC * 256], BF16)
    wdn_b = _dint(nc, "wdn_b", [DFF, D], BF16)
    win_b = _dint(nc, "win_b", [NB1 * 128, KC * 512], BF16)
    cx = Ctx(nc)
    phase_cast(cx, [(wgu, wgu_b), (wdn, wdn_b), (win, win_b)])
    phase_ffn(cx, x, x1, g1, wgu_b, wdn_b, NT, ident)
    phase_proj(cx, x1, gm, win_b, gain, qn, kv, cg, NT, ident)
    cx.P.emit()
    return nc


def build_prog_b(NQT):
    NT = NQT * 128
    ncp = (2 * NT - 32) // 16 + 1
    NCP = (ncp + 127) // 128 * 128
    nc = bass.Bass("TRN2", target_bir_lowering=False)
    x1 = _din(nc, "x1", [NT, D], F32)
    qn = _din(nc, "qn", [NT, 2048], BF16)
    kvf = _din(nc, "kvf", [2 * NT, 2560], BF16)
    cg = _din(nc, "cg", [NT, 36], F32)
    sinks = _din(nc, "sinks", [128, 8], F32)
    w1 = _din(nc, "w1", [256, 4096], F32)
    w2 = _din(nc, "w2", [256, 128], F32)
    pos = _din(nc, "pos", [256, 16], F32)
    kgain = _din(nc, "kgain", [128, 64], F32)
    gm = _din(nc, "gm", [128, KC], F32)
    wgate = _din(nc, "wgate", [3 * KC * 128, KC * 128], F32)
    wbr = _din(nc, "wbr", [KC * 128, 1536], F32)
    wout = _din(nc, "wout", [D, D], F32)
    g2 = _din(nc, "g2", [128, KC], F32)
    wgu = _din(nc, "wgu", [FC * 128, KC * 256], F32)
    wdn = _din(nc, "wdn", [DFF, D], F32)
    C = {k: _din(nc, "c_" + k, shp, dt) for k, (shp, dt) in CONST_SPECS(NQT).items()}
    y = _dout(nc, "y", [NT, D], F32)
    w1_b = _dint(nc, "w1_b", [256, 4096], BF16)
    w2_b = _dint(nc, "w2_b", [256, 128], BF16)
    wgate_b = _dint(nc, "wgate_b", [3 * KC * 128, KC * 128], BF16)
    wbr_b = _dint(nc, "wbr_b", [KC * 128, 1536], BF16)
    wout_b = _dint(nc, "wout_b", [D, D], BF16)
    wgu_b = _dint(nc, "wgu_b", [FC * 128, KC * 256], BF16)
    wdn_b = _dint(nc, "wdn_b", [DFF, D], BF16)
    cmpk = _dint(nc, "cmpk", [NCP, 128], BF16)
    cmpv = _dint(nc, "cmpv", [NCP, 128], BF16)
    ot = _dint(nc, "ot", [12, 128, NT], BF16)
    x2 = _dint(nc, "x2", [NT, D], F32)
    cx = Ctx(nc)
    phase_cast(cx, [(w1, w1_b), (w2, w2_b), (wgate, wgate_b), (wbr, wbr_b), (wout, wout_b), (wgu, wgu_b), (wdn, wdn_b)])
    phase_cmp(cx, kvf, w1_b, w2_b, pos, kgain, C['ident'], cmpk, cmpv, NQT)
    phase_attn(cx, qn, kvf, cmpk, cmpv, cg, sinks, C, ot, NQT)
    phase_out(cx, x1, x2, gm, wgate_b, wbr_b, wout_b, ot, NT, C['ident'])
    phase_ffn(cx, x2, y, g2, wgu_b, wdn_b, NT, C['ident'])
    cx.P.emit()
    return nc


def _rep(v, n=128):
    return np.ascontiguousarray(np.broadcast_to(np.asarray(v, np.float32)[None, :], (n, v.shape[0])))


WSPECS = [("g1", [128, KC]), ("wgu1", [FC * 128, KC * 256]), ("wdn1", [DFF, D]), ("gm", [128, KC]), ("win", [NB1 * 128, KC * 512]),
          ("gain", [128, 3200]), ("sinks", [128, 8]), ("w1", [256, 4096]), ("w2", [256, 128]), ("pos", [256, 16]), ("kgain", [128, 64]),
          ("wgate", [3 * KC * 128, KC * 128]), ("wbr", [KC * 128, 1536]), ("wout", [D, D]), ("g2", [128, KC]),
          ("wgu2", [FC * 128, KC * 256]), ("wdn2", [DFF, D])]
WCAST = ("wgu1", "wdn1", "win", "w1", "w2", "wgate", "wbr", "wout", "wgu2", "wdn2")


def build_fused(NQT, depth):
    NT = NQT * 128
    NT2 = 2 * NT
    ncp = (NT2 - 32) // 16 + 1
    NCP = (ncp + 127) // 128 * 128
    nc = bass.Bass("TRN2", target_bir_lowering=False)
    x = _din(nc, "x", [NT2, D], F32)
    W = [{k: _din(nc, "l%d_%s" % (l, k), shp, F32) for k, shp in WSPECS} for l in range(depth)]
    Cs = [{k: _din(nc, "c%d_%s" % (hf, k), shp, dt) for k, (shp, dt) in CONST_SPECS(NQT).items()} for hf in range(2)]
    y = _dout(nc, "y", [NT2, D], F32)
    Wb = {k: _dint(nc, "b_" + k, dict(WSPECS)[k], BF16) for k in WCAST}
    xa = _dint(nc, "xa", [NT2, D], F32)
    xb_ = _dint(nc, "xb", [NT2, D], F32)
    xc = _dint(nc, "xc", [NT2, D], F32)
    qn = _dint(nc, "qn", [NT2, 2048], BF16)
    kvf = _dint(nc, "kvf", [NT2, 2560], BF16)
    cg = _dint(nc, "cg", [NT2, 36], F32)
    cmpk = _dint(nc, "cmpk", [NCP, 128], BF16)
    cmpv = _dint(nc, "cmpv", [NCP, 128], BF16)
    ot = _dint(nc, "ot", [12, 128, NT2], BF16)
    cx = Ctx(nc)
    ident = Cs[0]['ident']
    cur = x
    for l in range(depth):
        w = W[l]
        phase_cast(cx, [(w[k], Wb[k]) for k in WCAST])
        phase_ffn(cx, cur, xa, w["g1"], Wb["wgu1"], Wb["wdn1"], NT2, ident)
        phase_proj(cx, xa, w["gm"], Wb["win"], w["gain"], qn, kvf, cg, NT2, ident)
        phase_cmp(cx, kvf, Wb["w1"], Wb["w2"], w["pos"], w["kgain"], ident, cmpk, cmpv, NQT)
        for hf in range(2):
            phase_attn(cx, qn[hf * NT:(hf + 1) * NT, :], kvf, cmpk, cmpv, cg[hf * NT:(hf + 1) * NT, :], w["sinks"], Cs[hf],
                       ot[:, :, hf * NT:(hf + 1) * NT], NQT)
        phase_out(cx, xa, xb_, w["gm"], Wb["wgate"], Wb["wbr"], Wb["wout"], ot, NT2, ident)
        dst = y if l == depth - 1 else xc
        phase_ffn(cx, xb_, dst, w["g2"], Wb["wgu2"], Wb["wdn2"], NT2, ident)
        cur = xc
        if l + 1 < depth:
            cx.P.new_epoch()
    cx.P.emit()
    return nc


def layer_weights(l, ffn1_norm, ffn1_w_gu, ffn1_w_down, mix_norm, w_in, qk_gain, sinks, cmp_pos, cmp_w1, cmp_w2,
                  w_branch_a, w_branch_b, w_branch_c, w_out, ffn2_norm, ffn2_w_gu, ffn2_w_down):
    return {"g1": lay_col(ffn1_norm[l]), "wgu1": lay_wgu(ffn1_w_gu[l]), "wdn1": np.ascontiguousarray(ffn1_w_down[l]),
            "gm": lay_col(mix_norm[l]), "win": lay_win_p1(w_in[l]), "gain": lay_gain(qk_gain[l]), "sinks": _rep(sinks[l]),
            "w1": lay_cmp_w1(cmp_w1[l]), "w2": lay_cmp_w2(cmp_w2[l]), "pos": lay_cmp_pos(cmp_pos[l]), "kgain": _rep(qk_gain[l, 2, 1]),
            "wgate": lay_wgate(w_in[l]), "wbr": lay_wbr(w_branch_a[l], w_branch_b[l], w_branch_c[l]),
            "wout": np.ascontiguousarray(w_out[l]), "g2": lay_col(ffn2_norm[l]), "wgu2": lay_wgu(ffn2_w_gu[l]),
            "wdn2": np.ascontiguousarray(ffn2_w_down[l])}


def kernel(x, ffn1_norm, ffn1_w_gu, ffn1_w_down, mix_norm, w_in, qk_gain, sinks, cmp_pos, cmp_w1, cmp_w2,
           w_branch_a, w_branch_b, w_branch_c, w_out, ffn2_norm, ffn2_w_gu, ffn2_w_down):
    f = lambda a: np.asarray(a, np.float32)
    x = f(x)
    NQT, NT = NQT_FULL, NT_FULL
    cores = list(range(NCORES))
    xs = [np.ascontiguousarray(x[c // 2].reshape(2 * NQT, 128, D)[c % 2::2].reshape(NT, D)) for c in cores]
    ident = np.eye(128, dtype=ml_dtypes.bfloat16)
    consts = [build_consts(hf, NQT) for hf in range(2)]
    nca = build_prog_a(NT)
    ncb = build_prog_b(NQT)
    depth = f(ffn1_norm).shape[0]
    for l in range(depth):
        wa = {"g1": lay_col(f(ffn1_norm)[l]), "wgu": lay_wgu(f(ffn1_w_gu)[l]), "wdn": np.ascontiguousarray(f(ffn1_w_down)[l]),
              "gm": lay_col(f(mix_norm)[l]), "win": lay_win_p1(f(w_in)[l]), "gain": lay_gain(f(qk_gain)[l]), "ident": ident}
        ra = run_bass_kernel_spmd(nca, [dict(wa, x=xs[c]) for c in cores], core_ids=cores).results
        wb = {"sinks": _rep(f(sinks)[l]), "w1": lay_cmp_w1(f(cmp_w1)[l]), "w2": lay_cmp_w2(f(cmp_w2)[l]), "pos": lay_cmp_pos(f(cmp_pos)[l]),
              "kgain": _rep(f(qk_gain)[l, 2, 1]), "gm": wa["gm"], "wgate": lay_wgate(f(w_in)[l]),
              "wbr": lay_wbr(f(w_branch_a)[l], f(w_branch_b)[l], f(w_branch_c)[l]), "wout": np.ascontiguousarray(f(w_out)[l]),
              "g2": lay_col(f(ffn2_norm)[l]), "wgu": lay_wgu(f(ffn2_w_gu)[l]), "wdn": np.ascontiguousarray(f(ffn2_w_down)[l])}
        ims = []
        for c in cores:
            p = c - c % 2
            kvf = np.concatenate([np.asarray(ra[p]["kv"]), np.asarray(ra[p + 1]["kv"])], axis=0)
            im = dict(wb, x1=np.asarray(ra[c]["x1"]), qn=np.asarray(ra[c]["qn"]), kvf=kvf, cg=np.asarray(ra[c]["cg"]))
            for k, v in consts[c % 2].items():
                im["c_" + k] = v
            ims.append(im)
        rb = run_bass_kernel_spmd(ncb, ims, core_ids=cores).results
        xs = [np.asarray(rb[c]["y"]) for c in cores]
    out = np.zeros((BATCH, SEQ, D), np.float32)
    for c in cores:
        out[c // 2].reshape(2 * NQT, 128, D)[c % 2::2] = xs[c].reshape(NQT, 128, D)
    return out
```
